# Optimizing a Trainium2 kernel written in Bass

```python
import jax
import jax.numpy as jnp
from jax import lax
import numpy as np


D_MODEL = 4096
BATCH = 2
SEQ = 8192
DEPTH = 2

CTX_LEN = 256
GRID_W = 64
HEAD_DIM = 128
ROPE_PAIRS = HEAD_DIM // 4
ROPE_THETA = 10000.0
A_HEADS = 8
A_KV_HEADS = 2
A_WINDOW = 128
A_BLOCK = 128
B_HEADS = 8
NA_ROWS = 8
NA_COLS = 16
C_HEADS = 8
C_CHUNK = 64
C_CONV = 5
BRANCH_W = 8 * HEAD_DIM
N_BRANCH = 3
D_FF = 7168
N_MOD = 9
EPS = 1e-6
NEG = -1e30
IN_SPLITS = (A_HEADS * HEAD_DIM, A_KV_HEADS * HEAD_DIM, A_KV_HEADS * HEAD_DIM,
             B_HEADS * HEAD_DIM, B_HEADS * HEAD_DIM, B_HEADS * HEAD_DIM,
             C_HEADS * HEAD_DIM, C_HEADS * HEAD_DIM, C_HEADS * HEAD_DIM, C_HEADS * HEAD_DIM,
             4 * C_HEADS)
IN_WIDTH = sum(IN_SPLITS)

kernel_name = 'hybrid_dit_gqa_natten_mlstm'


def rms_norm(x, g):
    xf = x.astype(jnp.float32)
    y = xf * lax.rsqrt(jnp.mean(xf * xf, axis=-1, keepdims=True) + EPS)
    return (y * g.astype(jnp.float32)).astype(x.dtype)


def modulate(x, g, shift, scale):
    return rms_norm(x, g) * (1 + scale) + shift


def swiglu(u, w1, w3, w2):
    return (jax.nn.silu(u @ w1) * (u @ w3)) @ w2


def split_heads(a, h):
    return a.reshape(a.shape[0], a.shape[1], h, HEAD_DIM)


def rope_tables(n):
    t = jnp.arange(n)
    inv = ROPE_THETA ** (-jnp.arange(ROPE_PAIRS, dtype=jnp.float32) / ROPE_PAIRS)
    row = (t // GRID_W).astype(jnp.float32)[:, None] * inv
    col = (t % GRID_W).astype(jnp.float32)[:, None] * inv
    ang = jnp.stack([row, col], axis=1)
    return jnp.cos(ang), jnp.sin(ang)


def rope_2d(x, cos, sin):
    b, n, h, _ = x.shape
    xr = x.reshape(b, n, h, 2, 2, ROPE_PAIRS)
    x1, x2 = xr[..., 0, :], xr[..., 1, :]
    c = cos[None, :, None].astype(x.dtype)
    s = sin[None, :, None].astype(x.dtype)
    out = jnp.stack([x1 * c - x2 * s, x1 * s + x2 * c], axis=-2)
    return out.reshape(b, n, h, HEAD_DIM)


def context_attention(q, k, v, sink):
    s = jnp.einsum('bqhgd,bkhd->bhgqk', q, k).astype(jnp.float32) * HEAD_DIM ** -0.5
    if sink is not None:
        snk = jnp.broadcast_to(sink.astype(jnp.float32)[None, :, :, None, None], s.shape[:-1] + (1,))
        p = jax.nn.softmax(jnp.concatenate([snk, s], axis=-1), axis=-1)[..., 1:]
    else:
        p = jax.nn.softmax(s, axis=-1)
    o = jnp.einsum('bhgqk,bkhd->bqhgd', p.astype(v.dtype), v)
    return o.reshape(o.shape[0], o.shape[1], -1)


def windowed_gqa(pc, pl, g_q, g_k, sink, cos, sin, last):
    qc, kc, vc = pc
    ql, kl, vl = pl
    bsz, n, _ = ql.shape
    lc = qc.shape[1]
    grp = A_HEADS // A_KV_HEADS
    sink = sink.reshape(A_KV_HEADS, grp)
    kc = rms_norm(split_heads(kc, A_KV_HEADS), g_k)
    vc = split_heads(vc, A_KV_HEADS)
    ql = rope_2d(rms_norm(split_heads(ql, A_HEADS), g_q), cos, sin)
    kl = rope_2d(rms_norm(split_heads(kl, A_KV_HEADS), g_k), cos, sin)
    vl = split_heads(vl, A_KV_HEADS)
    nb = n // A_BLOCK
    qb = ql.reshape(bsz, nb, A_BLOCK, A_KV_HEADS, grp, HEAD_DIM)
    pad = ((0, 0), (A_BLOCK, A_BLOCK), (0, 0), (0, 0))
    kp, vp = jnp.pad(kl, pad), jnp.pad(vl, pad)
    idx = jnp.arange(nb)[:, None] * A_BLOCK + jnp.arange(3 * A_BLOCK)[None, :]
    kw, vw = kp[:, idx], vp[:, idx]
    qpos = jnp.arange(nb)[:, None, None] * A_BLOCK + jnp.arange(A_BLOCK)[None, :, None]
    kpos = idx[:, None, :] - A_BLOCK
    band = (jnp.abs(qpos - kpos) <= A_WINDOW) & (kpos >= 0) & (kpos < n)
    scale = HEAD_DIM ** -0.5
    s_win = jnp.einsum('bnqhgd,bnkhd->bnhgqk', qb, kw).astype(jnp.float32) * scale
    s_win = jnp.where(band[None, :, None, None], s_win, NEG)
    s_ctx = jnp.einsum('bnqhgd,bkhd->bnhgqk', qb, kc).astype(jnp.float32) * scale
    snk = jnp.broadcast_to(sink.astype(jnp.float32)[None, None, :, :, None, None], s_ctx.shape[:-1] + (1,))
    p = jax.nn.softmax(jnp.concatenate([snk, s_ctx, s_win], axis=-1), axis=-1).astype(vl.dtype)
    p_ctx, p_win = p[..., 1:1 + lc], p[..., 1 + lc:]
    o = jnp.einsum('bnhgqk,bnkhd->bnqhgd', p_win, vw) + jnp.einsum('bnhgqk,bkhd->bnqhgd', p_ctx, vc)
    out_l = o.reshape(bsz, n, A_HEADS * HEAD_DIM)
    if last:
        return None, out_l
    qcg = rms_norm(split_heads(qc, A_HEADS), g_q).reshape(bsz, lc, A_KV_HEADS, grp, HEAD_DIM)
    return context_attention(qcg, kc, vc, sink), out_l


def neighbourhood_attn(pc, pl, g_q, g_k, relpos, last):
    qc, kc, vc = pc
    ql, kl, vl = pl
    bsz, n, _ = ql.shape
    lc = qc.shape[1]
    rows = n // GRID_W
    kr_n = min(NA_ROWS, rows)
    kc = rms_norm(split_heads(kc, B_HEADS), g_k)
    vc = split_heads(vc, B_HEADS)
    qg = rms_norm(split_heads(ql, B_HEADS), g_q).reshape(bsz, rows, GRID_W, B_HEADS, HEAD_DIM)
    kg = rms_norm(split_heads(kl, B_HEADS), g_k).reshape(bsz, rows, GRID_W, B_HEADS, HEAD_DIM)
    vg = split_heads(vl, B_HEADS).reshape(bsz, rows, GRID_W, B_HEADS, HEAD_DIM)
    r = jnp.arange(rows)
    row_idx = jnp.clip(r - kr_n // 2, 0, rows - kr_n)[:, None] + jnp.arange(kr_n)[None, :]
    col = jnp.arange(GRID_W)
    col_start = jnp.clip(col - NA_COLS // 2, 0, GRID_W - NA_COLS)
    col_in = (col[None, :] >= col_start[:, None]) & (col[None, :] < col_start[:, None] + NA_COLS)
    kr, vr = kg[:, row_idx], vg[:, row_idx]
    dr = row_idx - r[:, None] + NA_ROWS - 1
    dc = jnp.clip(col[None, :] - col[:, None], -(NA_COLS - 1), NA_COLS - 1) + NA_COLS - 1
    bias = relpos[:, dr[:, None, :, None], dc[None, :, None, :]]
    bias = jnp.moveaxis(bias, 0, 1)[None].astype(jnp.float32)
    scale = HEAD_DIM ** -0.5
    s_nb = jnp.einsum('brqhd,brjkhd->brhqjk', qg, kr).astype(jnp.float32) * scale + bias
    s_nb = jnp.where(col_in[:, None, :], s_nb, NEG).reshape(bsz, rows, B_HEADS, GRID_W, kr_n * GRID_W)
    s_ctx = jnp.einsum('brqhd,bkhd->brhqk', qg, kc).astype(jnp.float32) * scale
    p = jax.nn.softmax(jnp.concatenate([s_ctx, s_nb], axis=-1), axis=-1).astype(vl.dtype)
    p_ctx = p[..., :lc]
    p_nb = p[..., lc:].reshape(bsz, rows, B_HEADS, GRID_W, kr_n, GRID_W)
    o = jnp.einsum('brhqjk,brjkhd->brqhd', p_nb, vr) + jnp.einsum('brhqk,bkhd->brqhd', p_ctx, vc)
    out_l = o.reshape(bsz, n, B_HEADS * HEAD_DIM)
    if last:
        return None, out_l
    qcg = rms_norm(split_heads(qc, B_HEADS), g_q)[:, :, :, None, :]
    return context_attention(qcg, kc, vc, None), out_l


def centred_conv(x, w, b):
    t = x.shape[1]
    pad = C_CONV // 2
    xp = jnp.pad(x, ((0, 0), (pad, pad), (0, 0)))
    out = b
    for j in range(C_CONV):
        out = out + w[j] * xp[:, j:j + t]
    return out


def mlstm_prep(p, conv_w, conv_b, gate_b):
    q, k, v, o, g = p
    bsz, t, _ = q.shape
    qk = jax.nn.silu(centred_conv(jnp.concatenate([q, k], axis=-1), conv_w, conv_b))
    q, k = jnp.split(qk, 2, axis=-1)
    q, k, v = (split_heads(a, C_HEADS).astype(jnp.float32) for a in (q, k, v))
    g = (g.reshape(bsz, t, 4, C_HEADS) + gate_b).astype(jnp.float32)
    return q, k, v, o, g


def mlstm_scan(q, k, v, ig, fg, state0, want_out):
    bsz, t, h, d = q.shape
    nc = t // C_CHUNK

    def chunks(a):
        return jnp.moveaxis(a.reshape(bsz, nc, C_CHUNK, h, -1), 3, 1)

    qc, kc, vc = chunks(q), chunks(k), chunks(v)
    log_i = jnp.moveaxis(ig.reshape(bsz, nc, C_CHUNK, h), 3, 1)
    log_f = jax.nn.log_sigmoid(jnp.moveaxis(fg.reshape(bsz, nc, C_CHUNK, h), 3, 1))
    bcum = jnp.cumsum(log_f, axis=-1)
    b_last = bcum[..., -1]
    a = b_last[..., None] - bcum + log_i
    m_loc = jnp.max(a, axis=-1)
    w = jnp.exp(a - m_loc[..., None])
    c_loc = jnp.einsum('bhcl,bhcld,bhcle->bhcde', w, kc, vc)
    n_loc = jnp.einsum('bhcl,bhcld->bhcd', w, kc)

    def step(carry, inp):
        c_prev, n_prev, m_prev = carry
        c_l, n_l, m_l, b_l = inp
        m_new = jnp.maximum(b_l + m_prev, m_l)
        s_prev = jnp.exp(b_l + m_prev - m_new)
        s_loc = jnp.exp(m_l - m_new)
        c_new = s_prev[..., None, None] * c_prev + s_loc[..., None, None] * c_l
        n_new = s_prev[..., None] * n_prev + s_loc[..., None] * n_l
        return (c_new, n_new, m_new), (c_prev, n_prev, m_prev)

    xs = (jnp.moveaxis(c_loc, 2, 0), jnp.moveaxis(n_loc, 2, 0), jnp.moveaxis(m_loc, 2, 0), jnp.moveaxis(b_last, 2, 0))
    final, starts = lax.scan(step, state0, xs)
    if not want_out:
        return None, final
    c0, n0, m0 = (jnp.moveaxis(s, 0, 2) for s in starts)
    qs = qc * d ** -0.5
    dmat = bcum[..., :, None] - bcum[..., None, :] + log_i[..., None, :]
    tril = jnp.tril(jnp.ones((C_CHUNK, C_CHUNK), dtype=bool))
    dmat = jnp.where(tril, dmat, NEG)
    inter = bcum + m0[..., None]
    m = jnp.maximum(jnp.max(dmat, axis=-1), inter)
    sw = jnp.exp(dmat - m[..., None]) * jnp.einsum('bhcid,bhcjd->bhcij', qs, kc)
    si = jnp.exp(inter - m)
    num = jnp.einsum('bhcij,bhcje->bhcie', sw, vc) + si[..., None] * jnp.einsum('bhcid,bhcde->bhcie', qs, c0)
    den = jnp.sum(sw, axis=-1) + si * jnp.einsum('bhcid,bhcd->bhci', qs, n0)
    hout = num / jnp.maximum(jnp.abs(den), jnp.exp(-m))[..., None]
    return jnp.moveaxis(hout, 1, 3).reshape(bsz, t, h, d), final


def mlstm_out(hsum, o, g):
    hn = rms_norm(hsum, g.reshape(C_HEADS, HEAD_DIM))
    y = hn.reshape(hsum.shape[0], hsum.shape[1], -1) * jax.nn.sigmoid(o.astype(jnp.float32))
    return y.astype(o.dtype)


def bidir_mlstm(pc, pl, conv_w, conv_b, gate_b, norm_g, last):
    qc, kc, vc, oc, gc = mlstm_prep(pc, conv_w, conv_b, gate_b)
    ql, kl, vl, ol, gl = mlstm_prep(pl, conv_w, conv_b, gate_b)
    bsz = ql.shape[0]
    f32 = jnp.float32
    st0 = (jnp.zeros((bsz, C_HEADS, HEAD_DIM, HEAD_DIM), f32), jnp.zeros((bsz, C_HEADS, HEAD_DIM), f32),
           jnp.zeros((bsz, C_HEADS), f32))

    def rev(a):
        return a[:, ::-1]

    hc_f, sc_f = mlstm_scan(qc, kc, vc, gc[:, :, 0], gc[:, :, 1], st0, not last)
    hc_b, sc_b = mlstm_scan(rev(qc), rev(kc), rev(vc), rev(gc[:, :, 2]), rev(gc[:, :, 3]), st0, not last)
    hl_f, _ = mlstm_scan(ql, kl, vl, gl[:, :, 0], gl[:, :, 1], sc_f, True)
    hl_b, _ = mlstm_scan(rev(ql), rev(kl), rev(vl), rev(gl[:, :, 2]), rev(gl[:, :, 3]), sc_b, True)
    out_l = mlstm_out(hl_f + rev(hl_b), ol, norm_g)
    if last:
        return None, out_l
    return mlstm_out(hc_f + rev(hc_b), oc, norm_g), out_l


def gated_merge(u, branches, w_gate, b_gate, w_branch, w_out):
    y = None
    for j, o in enumerate(branches):
        term = jax.nn.sigmoid(u @ w_gate[j] + b_gate[j]) * (o @ w_branch[j])
        y = term if y is None else y + term
    return y @ w_out


def setup_inputs(seed: int = 0) -> dict:
    key = jax.random.key(seed)
    ks = jax.random.split(key, 24)
    f32 = jnp.float32
    d = D_MODEL

    def nrm(k, shape, scale):
        return jax.random.normal(k, shape, f32) * scale

    f_base = jnp.array([0.0, 1.0, 0.0, 1.0], f32)[:, None] * jnp.linspace(3.0, 6.0, C_HEADS, dtype=f32)[None, :]
    return {
        'x': nrm(ks[0], (BATCH, SEQ, d), 1.0),
        'c': nrm(ks[1], (BATCH, d), 1.0),
        'ctx': nrm(ks[2], (BATCH, CTX_LEN, d), 1.0),
        'c_ctx': nrm(ks[3], (d,), 1.0),
        'w_mod': nrm(ks[4], (DEPTH, d, N_MOD * d), 0.5 * d ** -0.5),
        'b_mod': nrm(ks[5], (DEPTH, N_MOD * d), 0.02),
        'norm_g': 1.0 + nrm(ks[6], (DEPTH, 3, d), 0.02),
        'ffn_w1': nrm(ks[7], (DEPTH, 2, d, D_FF), d ** -0.5),
        'ffn_w3': nrm(ks[8], (DEPTH, 2, d, D_FF), d ** -0.5),
        'ffn_w2': nrm(ks[9], (DEPTH, 2, D_FF, d), D_FF ** -0.5),
        'w_in': nrm(ks[10], (DEPTH, d, IN_WIDTH), d ** -0.5),
        'qk_g': 1.0 + nrm(ks[11], (DEPTH, 4, HEAD_DIM), 0.02),
        'attn_sink': nrm(ks[12], (DEPTH, A_HEADS), 0.5),
        'na_relpos': nrm(ks[13], (DEPTH, B_HEADS, 2 * NA_ROWS - 1, 2 * NA_COLS - 1), 0.5),
        'mlstm_conv_w': nrm(ks[14], (DEPTH, C_CONV, 2 * C_HEADS * HEAD_DIM), C_CONV ** -0.5),
        'mlstm_conv_b': nrm(ks[15], (DEPTH, 2 * C_HEADS * HEAD_DIM), 0.02),
        'mlstm_gate_b': f_base[None] + nrm(ks[16], (DEPTH, 4, C_HEADS), 0.1),
        'mlstm_norm_g': 1.0 + nrm(ks[17], (DEPTH, C_HEADS * HEAD_DIM), 0.02),
        'w_gate': nrm(ks[18], (DEPTH, N_BRANCH, d, d), d ** -0.5),
        'b_gate': nrm(ks[19], (DEPTH, N_BRANCH, d), 0.02),
        'w_branch': nrm(ks[20], (DEPTH, N_BRANCH, BRANCH_W, d), BRANCH_W ** -0.5),
        'w_out': nrm(ks[21], (DEPTH, d, d), d ** -0.5),
    }


def reference(x, c, ctx, c_ctx, w_mod, b_mod, norm_g, ffn_w1, ffn_w3, ffn_w2, w_in, qk_g, attn_sink,
              na_relpos, mlstm_conv_w, mlstm_conv_b, mlstm_gate_b, mlstm_norm_g, w_gate, b_gate, w_branch, w_out):
    cos, sin = rope_tables(x.shape[1])
    splits = [int(s) for s in np.cumsum(IN_SPLITS)[:-1]]
    xc, xl = ctx, x
    for i in range(DEPTH):
        last = i == DEPTH - 1
        g = norm_g[i]
        mod_l = (jax.nn.silu(c) @ w_mod[i] + b_mod[i])[:, None, :]
        mod_c = (jax.nn.silu(c_ctx) @ w_mod[i] + b_mod[i])[None, None, :]
        ml = jnp.split(mod_l, N_MOD, axis=-1)
        mc = jnp.split(mod_c, N_MOD, axis=-1)
        xc = xc + 0.5 * mc[2] * swiglu(modulate(xc, g[0], mc[0], mc[1]), ffn_w1[i, 0], ffn_w3[i, 0], ffn_w2[i, 0])
        xl = xl + 0.5 * ml[2] * swiglu(modulate(xl, g[0], ml[0], ml[1]), ffn_w1[i, 0], ffn_w3[i, 0], ffn_w2[i, 0])
        uc = modulate(xc, g[1], mc[3], mc[4])
        ul = modulate(xl, g[1], ml[3], ml[4])
        pc = jnp.split(uc @ w_in[i], splits, axis=-1)
        pl = jnp.split(ul @ w_in[i], splits, axis=-1)
        a_c, a_l = windowed_gqa(pc[0:3], pl[0:3], qk_g[i, 0], qk_g[i, 1], attn_sink[i], cos, sin, last)
        n_c, n_l = neighbourhood_attn(pc[3:6], pl[3:6], qk_g[i, 2], qk_g[i, 3], na_relpos[i], last)
        m_c, m_l = bidir_mlstm(pc[6:11], pl[6:11], mlstm_conv_w[i], mlstm_conv_b[i], mlstm_gate_b[i],
                               mlstm_norm_g[i], last)
        xl = xl + ml[5] * gated_merge(ul, (a_l, n_l, m_l), w_gate[i], b_gate[i], w_branch[i], w_out[i])
        xl = xl + 0.5 * ml[8] * swiglu(modulate(xl, g[2], ml[6], ml[7]), ffn_w1[i, 1], ffn_w3[i, 1], ffn_w2[i, 1])
        if not last:
            xc = xc + mc[5] * gated_merge(uc, (a_c, n_c, m_c), w_gate[i], b_gate[i], w_branch[i], w_out[i])
            xc = xc + 0.5 * mc[8] * swiglu(modulate(xc, g[2], mc[6], mc[7]), ffn_w1[i, 1], ffn_w3[i, 1],
                                           ffn_w2[i, 1])
    return xl
```

```python
import numpy as np
import concourse.bass as bass
import concourse.mybir as mybir

F32 = mybir.dt.float32
BF16 = mybir.dt.bfloat16
AF = mybir.ActivationFunctionType
ALU = mybir.AluOpType
AX = mybir.AxisListType

COMPUTE = ("pe", "act", "dve", "pool")
DMAQ = ("sp", "gq")


class Op:
    __slots__ = ("eng", "fn", "reads", "writes", "idx", "sig", "deps", "dma_n")

    def __init__(self, eng, fn, reads, writes):
        self.eng = eng
        self.fn = fn
        self.reads = reads
        self.writes = writes
        self.sig = None
        self.deps = ()
        self.dma_n = None


class SemPool:
    def __init__(self, nc, es, ring=20, ngq=16):
        self.sems = {e: es.enter_context(nc.semaphore(f"sem_{e}")) for e in COMPUTE}
        self.dsp = [es.enter_context(nc.semaphore(f"dsem_sp_{i}")) for i in range(ring)]
        self.gq = [es.enter_context(nc.semaphore(f"gsem_{i}")) for i in range(ngq)]


class Prog:
    _uid = [0]

    def __init__(self, nc, pool):
        self.nc = nc
        self.ops = []
        self.pool = pool
        self.ring = len(pool.dsp)

    def op(self, eng, fn, reads=(), writes=()):
        o = Op(eng, fn, tuple(reads), tuple(writes))
        o.idx = len(self.ops)
        self.ops.append(o)
        return o

    @staticmethod
    def stream(eng):
        return "pool" if eng == "gq" else eng

    def analyze(self):
        last_w = {}
        readers = {}
        need_sig = set()
        for o in self.ops:
            deps = set()
            for k in o.reads:
                w = last_w.get(k)
                if w is not None:
                    deps.add(w)
            for k in o.writes:
                w = last_w.get(k)
                if w is not None:
                    deps.add(w)
                r = readers.get(k)
                if r:
                    for v in r.values():
                        if isinstance(v, list):
                            deps.update(v)
                        else:
                            deps.add(v)
            deps.discard(o.idx)
            fin = []
            for d in deps:
                p = self.ops[d]
                if p.eng == "pe" and o.eng == "pe":
                    continue
                fin.append(d)
                need_sig.add(d)
            o.deps = fin
            for k in o.writes:
                last_w[k] = o.idx
                readers[k] = {}
            for k in o.reads:
                r = readers.setdefault(k, {})
                if o.eng in DMAQ:
                    r.setdefault(o.eng, []).append(o.idx)
                    if len(r[o.eng]) > 64:
                        r[o.eng] = r[o.eng][-64:]
                else:
                    r[o.eng] = o.idx
        cnt = {e: 0 for e in COMPUTE}
        dcnt = {e: 0 for e in DMAQ}
        self.gqgen = {}
        for o in self.ops:
            if o.eng in DMAQ:
                o.dma_n = dcnt[o.eng]
                dcnt[o.eng] += 1
            elif o.idx in need_sig:
                cnt[o.eng] += 1
                o.sig = cnt[o.eng]
        self.nsig = cnt
        self.ndma = dcnt

    def emit(self, final_wait_ops=()):
        nc = self.nc
        self.analyze()
        from contextlib import ExitStack
        K = self.ring
        with ExitStack() as es:
            sp_ = self.pool
            sems = sp_.sems
            dsem = {"sp": sp_.dsp}
            dsem["gq"] = sp_.gq
            KQ = {"sp": len(sp_.dsp), "gq": len(sp_.gq)}
            es2 = ExitStack()
            block = es2.enter_context(nc.Block())
            engobj = {"pe": "tensor", "act": "scalar", "dve": "vector", "pool": "gpsimd", "sp": "sync"}
            streams = {s: [] for s in engobj}
            for o in self.ops:
                streams[self.stream(o.eng)].append(o)
            ops = self.ops

            def dma_target(o):
                kq = KQ[o.eng]
                return dsem[o.eng][o.dma_n % kq], 16 * (o.dma_n // kq + 1)

            def run_stream(sname, eng):
                seen = {}

                def wait(sem, val):
                    key = id(sem)
                    if isinstance(val, tuple):
                        if seen.get(key, 0) >= val[1]:
                            return
                        eng.wait_ge(sem, 16)
                        seen[key] = val[1]
                        return
                    if seen.get(key, 0) >= val:
                        return
                    eng.wait_ge(sem, val)
                    seen[key] = val

                for o in streams[sname]:
                    cw = {}
                    for d in o.deps:
                        p = ops[d]
                        if p.eng in DMAQ:
                            s, v = dma_target(p)
                            wait(s, v)
                        else:
                            cw[p.eng] = max(cw.get(p.eng, 0), p.sig)
                    for e, v in cw.items():
                        wait(sems[e], v)
                    if o.eng in DMAQ and o.dma_n >= KQ[o.eng]:
                        kq = KQ[o.eng]
                        wait(dsem[o.eng][o.dma_n % kq], 16 * (o.dma_n // kq))
                    ins = o.fn(eng)
                    if o.eng in DMAQ:
                        s, v = dma_target(o)
                        ins.then_inc(s, 16)
                    elif o.sig is not None:
                        ins.then_inc(sems[o.eng], 1)
                if sname == "sp":
                    lst = [o for o in ops if o.eng == "sp"][-KQ["sp"]:] + [o for o in ops if o.eng == "gq"][-KQ["gq"]:]
                    for o in lst:
                        s, v = dma_target(o)
                        wait(s, v)

            for sname, attr in engobj.items():
                if not streams[sname] and sname != "sp":
                    continue
                dec = getattr(block, attr)

                def body(eng, sname=sname):
                    run_stream(sname, eng)
                dec(body)
            es2.close()
            allsems = list(sems.values()) + dsem["sp"] + dsem["gq"]
            with nc.Block() as b2:
                def clr(eng):
                    for sm in allsems:
                        eng.sem_clear(sm)
                b2.sync(clr)


import os
from contextlib import ExitStack

HD = 128
NH = 8
GRID_W = 64
EPS = 1e-6


class Cfg:
    def __init__(self, D, SEQ, CTX, DFF, L):
        self.D, self.SEQ, self.CTX, self.DFF, self.L = D, SEQ, CTX, DFF, L
        self.KC = D // 128
        self.JC = DFF // 128
        self.NTOK = CTX + SEQ
        self.ROWS = SEQ // GRID_W
        self.INW = 8 * 128 + 2 * 128 * 2 + 3 * 8 * 128 + 4 * 8 * 128 + 32
        self.tiles = []
        s = 0
        while s < CTX:
            t = min(512, CTX - s)
            self.tiles.append((s, t, True))
            s += t
        while s < self.NTOK:
            t = min(512, self.NTOK - s)
            self.tiles.append((s, t, False))
            s += t


FULL = Cfg(4096, 8192, 256, 7168, 2)


class K:
    def __init__(self, cfg, stop_after=None, dbg=False):
        self.cfg = cfg
        self.stop_after = stop_after
        nc = bass.Bass("TRN2", target_bir_lowering=False)
        self.nc = nc
        c = cfg
        L = c.L

        def din(name, shape, dt=F32):
            return nc.dram_tensor(name, list(shape), dt, kind="ExternalInput").ap()

        self.x = din("x", [c.SEQ, c.D])
        self.ctx = din("ctx", [c.CTX, c.D])
        self.cvec = din("cvec", [128, c.KC, 2])
        self.w_mod = din("w_mod", [L, c.D, 9 * c.D])
        self.b_modT = din("b_modT", [L, 128, 9 * c.KC])
        self.norm_gT = din("norm_gT", [L, 128, 3, c.KC])
        self.ffn_w1 = din("ffn_w1", [L, 2, c.D, c.DFF])
        self.ffn_w3 = din("ffn_w3", [L, 2, c.D, c.DFF])
        self.ffn_w2 = din("ffn_w2", [L, 2, c.DFF, c.D])
        self.ident = din("ident", [128, 128])
        self.w_in = din("w_in", [L, c.D, c.INW])
        self.w_gate = din("w_gate", [L, 3, c.D, c.D])
        self.w_branch = din("w_branch", [L, 3, 1024, c.D])
        self.w_out = din("w_out", [L, c.D, c.D])
        self.b_gateT = din("b_gateT", [L, 128, 3, c.KC])
        self.mc = mixer_consts(c)
        ncase = len(self.mc["case_list"])
        self.cosT = din("cosT", [128, c.NTOK])
        self.sinT = din("sinT", [128, c.NTOK])
        self.RmD = din("RmD", [128, 128])
        self.MAD = din("MAD", [128, 2, 128])
        self.MCD = din("MCD", [128, 2, 128])
        self.mnegBD = din("mnegBD", [128, ncase, 128])
        self.rbD = din("rbD", [L, 8, 128, ncase, 128])
        self.qk_gT = din("qk_gT", [L, 128, 4])
        self.sinkB = din("sinkB", [L, 128, 8])
        self.mresD = din("mresD", [128, 512])
        self.negfD = din("negfD", [128, 512])
        self.negbD = din("negbD", [128, 512])
        self.conv_wT = din("conv_wT", [L, 128, 16, 5])
        self.conv_bT = din("conv_bT", [L, 128, 16])
        self.gate_bB = din("gate_bB", [L, 128, 32])
        self.mngT = din("mngT", [L, 128, 8])
        self.hscr = nc.dram_tensor("hscr", [c.NTOK, 128], F32, kind="Internal").ap()
        skind = "ExternalOutput" if dbg else "Internal"
        self.NA = 36 * 128
        self.NC = c.INW - self.NA
        self.pTa = nc.dram_tensor("pTa", [self.NA, c.NTOK], F32, kind=skind).ap()
        self.pTc = nc.dram_tensor("pTc", [self.NC, c.NTOK], F32, kind=skind).ap()
        self.oT = nc.dram_tensor("oT", [3 * 1024, c.NTOK], BF16, kind=skind).ap()
        self.pc1 = min(256, 8192 // c.KC)
        self.pc2 = max(128, min(512, (8192 // c.JC) // 128 * 128))
        self.bf = {}

        def reg(name, key, src2d, pc):
            Kr, N = src2d.shape
            kch = Kr // 128
            npieces = (N + pc - 1) // pc
            per = max(1, (200 * 1024 * 1024) // (128 * kch * pc * 2))
            tens = []
            for t0 in range(0, npieces, per):
                n = min(per, npieces - t0)
                tens.append(nc.dram_tensor(f"{name}_bf_{'_'.join(map(str, key))}_{t0}", [n, 128, kch * pc], BF16, kind="Internal").ap())
            self.bf[(name,) + tuple(key)] = dict(tens=tens, per=per, pc=pc, kch=kch, N=N, src=src2d, npieces=npieces)

        for l in range(L):
            reg("w_mod", (l,), self.w_mod[l], 256)
            for w in range(2):
                reg("ffn_w1", (l, w), self.ffn_w1[l, w], self.pc1)
                reg("ffn_w3", (l, w), self.ffn_w3[l, w], self.pc1)
                reg("ffn_w2", (l, w), self.ffn_w2[l, w], self.pc2)
            reg("w_in", (l,), self.w_in[l], self.pc1)
            for b in range(3):
                reg("w_gate", (l, b), self.w_gate[l, b], self.pc1)
                reg("w_branch", (l, b), self.w_branch[l, b], self.pc1)
            reg("w_out", (l,), self.w_out[l], self.pc1)

        self.out = nc.dram_tensor("out", [c.SEQ, c.D], F32, kind="ExternalOutput").ap()
        self.xT = nc.dram_tensor("xT", [c.D, c.NTOK], F32, kind="Internal").ap()

    def piece(self, wkey, i):
        d = self.bf[wkey]
        t = d["tens"][i // d["per"]]
        ncols = min(d["pc"], d["N"] - i * d["pc"])
        return t[i % d["per"]], ncols, d["kch"], d["pc"]

    def wload_piece(self, P, wkey, i, buf, bkey):
        ap, ncols, kch, pc = self.piece(wkey, i)
        tot = kch * pc
        keys = []
        if ncols < pc:
            bv = buf[:, 0:tot].rearrange("p (k n) -> p k n", n=pc)
            av = ap.rearrange("p (k n) -> p k n", n=pc)
            for k8 in range(0, kch, 8):
                k9 = min(kch, k8 + 8)
                P.op("sp", lambda e, k8=k8, k9=k9: e.dma_start(out=bv[:, k8:k9, 0:ncols], in_=av[:, k8:k9, 0:ncols]), writes=[bkey + (k8,)])
                keys.append(bkey + (k8,))
            return bv, keys
        nsp = 2 if tot >= 2048 else 1
        step = tot // nsp
        for q in range(nsp):
            P.op("sp", lambda e, q=q: e.dma_start(out=buf[:, q * step:(q + 1) * step], in_=ap[:, q * step:(q + 1) * step]), writes=[bkey + (q,)])
            keys.append(bkey + (q,))
        return buf[:, 0:tot].rearrange("p (k n) -> p k n", n=pc), keys

    _uid = [0]

    def sb(self, es, name, shape, dt=F32):
        K._uid[0] += 1
        return es.enter_context(self.nc.sbuf_tensor(f"{name}_{K._uid[0]}", list(shape), dt))

    def psum_banks(self, es, n=8):
        K._uid[0] += 1
        return [es.enter_context(self.nc.psum_tensor(f"ps{i}_{K._uid[0]}", [128, 512], F32)) for i in range(n)]

    def build(self):
        c = self.cfg
        with ExitStack() as es:
            self.pool = SemPool(self.nc, es)
            self.identS = self.sb(es, "identS", [128, 128])
            self.onesF = self.sb(es, "onesF", [128, 128])
            self.modS = self.sb(es, "modS", [128, 9 * c.KC, 2])
            self.GS = self.sb(es, "GS", [128, 2, 3, c.KC])
            self.SH = self.sb(es, "SH", [128, 2, 3, c.KC])
            self.GT = self.sb(es, "GT", [128, 2, 3, c.KC])
            import os
            if not os.environ.get("SKIP_CAST"):
                self.phase_cast()
            self.phase_in()
            for l in range(c.L):
                if os.environ.get("SKIP_MOD"):
                    break
                self.phase_mod(l)
                if self.stop_after == ("mod", l):
                    break
                self.phase_ffn(l, 0)
                if self.stop_after == ("ffn1", l):
                    break
                if os.environ.get("FFN2"):
                    self.phase_ffn(l, 1)
                    continue
                last = (l == c.L - 1)
                self.phase_proj(l)
                if self.stop_after == ("proj", l):
                    break
                self.phase_mix(l, last)
                if self.stop_after == ("mix", l):
                    break
                self.phase_merge(l, last)
                if self.stop_after == ("merge", l):
                    break
                self.phase_ffn(l, 1, skip_ctx=last)
            self.phase_out()
        return self.nc

    def phase_cast(self):
        nc = self.nc
        jobs = []
        for wkey, d in self.bf.items():
            pc, kch = d["pc"], d["kch"]
            srcv = d["src"].rearrange("(kc p) n -> p kc n", p=128)
            for i in range(d["npieces"]):
                ap, ncols, _, _ = self.piece(wkey, i)
                dv = ap.rearrange("p (k n) -> p k n", n=pc)
                for k8 in range(0, kch, 8):
                    k9 = min(kch, k8 + 8)
                    jobs.append((dv[:, k8:k9, 0:ncols], srcv[:, k8:k9, i * pc:i * pc + ncols]))
        P = Prog(nc, self.pool)
        for i, (d, s_) in enumerate(jobs):
            P.op("gq", lambda e, d=d, s_=s_: e.dma_start(out=d, in_=s_), writes=[("cast", i)])
        P.emit()

    def phase_in(self):
        c, nc = self.cfg, self.nc
        with ExitStack() as es:
            P = Prog(nc, self.pool)
            xtok = [self.sb(es, f"xtok{i}", [128, c.D]) for i in range(2)]
            stg = [self.sb(es, f"stg{i}", [128, c.KC, 128]) for i in range(2)]
            ps = self.psum_banks(es)
            P.op("sp", lambda e: e.dma_start(out=self.identS[:], in_=self.ident[:, :]), writes=["ident"])
            P.op("dve", lambda e: e.memset(self.onesF[:], 1.0), writes=["ones"])
            nsub = c.NTOK // 128
            xTv = self.xT.rearrange("(kc p) t -> p kc t", p=128)
            for s in range(nsub):
                t0 = s * 128
                src = self.ctx[t0:t0 + 128, :] if t0 < c.CTX else self.x[t0 - c.CTX:t0 - c.CTX + 128, :]
                xb = xtok[s % 2]
                sg = stg[s % 2]
                P.op("sp", lambda e, xb=xb, src=src: e.dma_start(out=xb[:], in_=src), writes=[("xtok", s % 2)])
                for kc in range(c.KC):
                    bank = ps[(kc // 4) % 8]
                    sl = bank[:, (kc % 4) * 128:(kc % 4) * 128 + 128]
                    P.op("pe", lambda e, sl=sl, xb=xb, kc=kc: e.transpose(out=sl, in_=xb[:, kc * 128:(kc + 1) * 128], identity=self.identS[:]),
                         reads=[("xtok", s % 2), "ident"], writes=[("psb", (kc // 4) % 8, kc % 4)])
                    if kc % 4 == 3 or kc == c.KC - 1:
                        k0 = (kc // 4) * 4
                        n = kc - k0 + 1
                        eng = "act" if (kc // 4) % 2 == 0 else "dve"
                        if eng == "act":
                            fn = lambda e, sg=sg, k0=k0, n=n, bank=bank: e.activation(out=sg[:, k0:k0 + n, :], in_=bank[:, 0:n * 128].rearrange("p (a b) -> p a b", b=128), func=AF.Copy)
                        else:
                            fn = lambda e, sg=sg, k0=k0, n=n, bank=bank: e.tensor_copy(out=sg[:, k0:k0 + n, :], in_=bank[:, 0:n * 128].rearrange("p (a b) -> p a b", b=128))
                        P.op(eng, fn, reads=[("psb", (kc // 4) % 8, q) for q in range(n)], writes=[("stg", s % 2, k0)])
                for k8 in range(0, c.KC, 8):
                    k9 = min(c.KC, k8 + 8)
                    P.op("sp", lambda e, sg=sg, t0=t0, k8=k8, k9=k9: e.dma_start(out=xTv[:, k8:k9, t0:t0 + 128], in_=sg[:, k8:k9, :]),
                         reads=[("stg", s % 2, k0) for k0 in range(k8, k9, 4)], writes=[("xT", s, k8)])
            P.emit()

    def phase_out(self):
        c, nc = self.cfg, self.nc
        with ExitStack() as es:
            P = Prog(nc, self.pool)
            xtok = [self.sb(es, f"oxtok{i}", [128, c.D]) for i in range(2)]
            stg = [self.sb(es, f"ostg{i}", [128, c.KC, 128]) for i in range(2)]
            ps = self.psum_banks(es)
            xTv = self.xT.rearrange("(kc p) t -> p kc t", p=128)
            nsub = c.SEQ // 128
            for s in range(nsub):
                t0 = c.CTX + s * 128
                xb = xtok[s % 2]
                sg = stg[s % 2]
                for k8 in range(0, c.KC, 8):
                    k9 = min(c.KC, k8 + 8)
                    P.op("sp", lambda e, sg=sg, t0=t0, k8=k8, k9=k9: e.dma_start(out=sg[:, k8:k9, :], in_=xTv[:, k8:k9, t0:t0 + 128]), writes=[("stg", s % 2, k8)])
                for kc in range(c.KC):
                    bank = ps[(kc // 4) % 8]
                    sl = bank[:, (kc % 4) * 128:(kc % 4) * 128 + 128]
                    P.op("pe", lambda e, sl=sl, sg=sg, kc=kc: e.transpose(out=sl, in_=sg[:, kc, :], identity=self.identS[:]),
                         reads=[("stg", s % 2, (kc // 8) * 8)], writes=[("psb", (kc // 4) % 8, kc % 4)])
                    if kc % 4 == 3 or kc == c.KC - 1:
                        k0 = (kc // 4) * 4
                        n = kc - k0 + 1
                        eng = "act" if (kc // 4) % 2 == 0 else "dve"
                        if eng == "act":
                            fn = lambda e, xb=xb, k0=k0, n=n, bank=bank: e.activation(out=xb[:, k0 * 128:(k0 + n) * 128], in_=bank[:, 0:n * 128], func=AF.Copy)
                        else:
                            fn = lambda e, xb=xb, k0=k0, n=n, bank=bank: e.tensor_copy(out=xb[:, k0 * 128:(k0 + n) * 128], in_=bank[:, 0:n * 128])
                        P.op(eng, fn, reads=[("psb", (kc // 4) % 8, q) for q in range(n)], writes=[("xtok", s % 2, k0)])
                P.op("sp", lambda e, xb=xb, s=s: e.dma_start(out=self.out[s * 128:(s + 1) * 128, :], in_=xb[:]),
                     reads=[("xtok", s % 2, k0) for k0 in range(0, c.KC, 4)], writes=[("out", s)])
            P.emit()

    def phase_mod(self, l):
        c, nc = self.cfg, self.nc
        NF = 9 * c.KC
        PC = 256
        with ExitStack() as es:
            P = Prog(nc, self.pool)
            ps = self.psum_banks(es)
            cv = self.sb(es, "cv", [128, c.KC, 2])
            sg = self.sb(es, "sg", [128, c.KC, 2])
            scb = self.sb(es, "scb", [128, c.KC, 2], BF16)
            bm = self.sb(es, "bm", [128, NF])
            ng = self.sb(es, "ng", [128, 3, c.KC])
            wbf = [self.sb(es, f"wm{i}", [128, c.KC * PC], BF16) for i in range(3)]
            P.op("sp", lambda e: e.dma_start(out=cv[:], in_=self.cvec[:, :, :]), writes=["cv"])
            P.op("sp", lambda e: e.dma_start(out=bm[:], in_=self.b_modT[l, :, :]), writes=["bm"])
            P.op("sp", lambda e: e.dma_start(out=ng[:], in_=self.norm_gT[l, :, :, :]), writes=["ng"])
            P.op("act", lambda e: e.activation(out=sg[:], in_=cv[:], func=AF.Sigmoid), reads=["cv"], writes=["sg"])
            P.op("dve", lambda e: e.tensor_tensor(out=scb[:], in0=cv[:], in1=sg[:], op=ALU.mult), reads=["cv", "sg"], writes=["scb"])
            npieces = (9 * c.D) // PC
            for pi in range(npieces):
                w, wmkeys = self.wload_piece(P, ("w_mod", l), pi, wbf[pi % 3], ("wm", pi % 3))
                for q in range(PC // 128):
                    f = pi * (PC // 128) + q
                    bank = ps[(f * 2) // 512]
                    o = (f * 2) % 512
                    for kc in range(c.KC):
                        P.op("pe", lambda e, w=w, q=q, kc=kc, bank=bank, o=o: e.matmul(bank[:, o:o + 2], lhsT=w[:, kc, q * 128:(q + 1) * 128], rhs=scb[:, kc, :], start=(kc == 0), stop=(kc == c.KC - 1)),
                             reads=wmkeys + ["scb"], writes=[("macc", (f * 2) // 512)])
            nb = (NF * 2 + 511) // 512
            for b in range(nb):
                f0 = b * 256
                f1 = min(NF, f0 + 256)
                P.op("dve", lambda e, b=b, f0=f0, f1=f1: e.tensor_tensor(out=self.modS[:, f0:f1, :], in0=ps[b][:, 0:(f1 - f0) * 2].rearrange("p (f t) -> p f t", t=2),
                                                                     in1=bm[:, f0:f1].unsqueeze(2).to_broadcast([128, f1 - f0, 2]), op=ALU.add),
                     reads=[("macc", b), "bm"], writes=["modS"])
            KC = c.KC
            for t in range(2):
                for s in range(3):
                    sh = self.modS[:, (3 * s) * KC:(3 * s + 1) * KC, t]
                    sc = self.modS[:, (3 * s + 1) * KC:(3 * s + 2) * KC, t]
                    gt = self.modS[:, (3 * s + 2) * KC:(3 * s + 3) * KC, t]
                    P.op("dve", lambda e, t=t, s=s, sc=sc: e.scalar_tensor_tensor(out=self.GS[:, t, s, :], in0=sc, scalar=1.0, in1=ng[:, s, :], op0=ALU.add, op1=ALU.mult),
                         reads=["modS", "ng"], writes=["GS"])
                    P.op("dve", lambda e, t=t, s=s, sh=sh: e.tensor_copy(out=self.SH[:, t, s, :], in_=sh), reads=["modS"], writes=["SH"])
                    P.op("dve", lambda e, t=t, s=s, gt=gt: e.tensor_scalar(out=self.GT[:, t, s, :], in0=gt, scalar1=(1.0 if s == 1 else 0.5), scalar2=None, op0=ALU.mult),
                         reads=["modS"], writes=["GT"])
            P.emit()

    def norm_mod(self, P, R, tile, sub, dst):
        c = self.cfg
        t0, T, isctx = tile
        ty = 1 if isctx else 0
        xTv = self.xT.rearrange("(kc p) t -> p kc t", p=128)
        ssq = R["ps"][0]
        for kc in range(c.KC):
            xb = R["xc"][kc % 4]
            sq = R["sq"][kc % 2]
            P.op("sp", lambda e, xb=xb, kc=kc: e.dma_start(out=xb[:, :T], in_=xTv[:, kc, t0:t0 + T]), reads=[("xT", t0, kc)], writes=[("xc", kc % 4)])
            P.op("act", lambda e, xb=xb, sq=sq: e.activation(out=sq[:, :T], in_=xb[:, :T], func=AF.Square), reads=[("xc", kc % 4)], writes=[("sq", kc % 2)])
            P.op("pe", lambda e, sq=sq, kc=kc: e.matmul(ssq[:, :T], lhsT=self.onesF[:], rhs=sq[:, :T], start=(kc == 0), stop=(kc == c.KC - 1)),
                 reads=[("sq", kc % 2), "ones"], writes=[("ps", 0)])
        rs = R["rstd"]
        P.op("act", lambda e: e.activation(out=rs[:, :T], in_=ssq[:, :T], func=AF.Sqrt, scale=1.0 / c.D, bias=R["epsb"][:, 0:1]), reads=[("ps", 0), "epsb"], writes=["rstd"])
        P.op("dve", lambda e: e.reciprocal(out=rs[:, :T], in_=rs[:, :T]), reads=["rstd"], writes=["rstd"])
        for kc in range(c.KC):
            xb = R["xc"][kc % 4]
            tmp = R["tmp"][kc % 2]
            P.op("sp", lambda e, xb=xb, kc=kc: e.dma_start(out=xb[:, :T], in_=xTv[:, kc, t0:t0 + T]), reads=[("xT", t0, kc)], writes=[("xc", kc % 4)])
            P.op("dve", lambda e, xb=xb, tmp=tmp, kc=kc: e.scalar_tensor_tensor(out=tmp[:, :T], in0=xb[:, :T], scalar=self.GS[:, ty, sub, kc:kc + 1], in1=rs[:, :T], op0=ALU.mult, op1=ALU.mult),
                 reads=[("xc", kc % 4), "rstd", "GS"], writes=[("tmp", kc % 2)])
            P.op("act", lambda e, tmp=tmp, kc=kc: e.activation(out=dst[:, kc, :T], in_=tmp[:, :T], func=AF.Identity, bias=self.SH[:, ty, sub, kc:kc + 1]),
                 reads=[("tmp", kc % 2), "SH"], writes=[("xn", kc)])

    def alloc_dense(self, es):
        c = self.cfg
        R = {}
        R["ps"] = self.psum_banks(es)
        R["xc"] = [self.sb(es, f"xc{i}", [128, 512]) for i in range(4)]
        R["sq"] = [self.sb(es, f"sq{i}", [128, 512]) for i in range(2)]
        R["tmp"] = [self.sb(es, f"tmp{i}", [128, 512]) for i in range(2)]
        R["ob"] = [self.sb(es, f"ob{i}", [128, 512]) for i in range(2)]
        R["rstd"] = self.sb(es, "rstd", [128, 512])
        R["epsb"] = self.sb(es, "epsb", [128, 1])
        R["xn"] = self.sb(es, "xn", [128, c.KC, 512], BF16)
        return R

    def phase_ffn(self, l, which, skip_ctx=False):
        c, nc = self.cfg, self.nc
        sub = 0 if which == 0 else 2
        PC = 256
        with ExitStack() as es:
            P = Prog(nc, self.pool)
            R = self.alloc_dense(es)
            ps = R["ps"]
            h = self.sb(es, "h", [128, c.JC, 512], BF16)
            NW = 4
            wb = [self.sb(es, f"wb{i}", [128, 8192], BF16) for i in range(NW)]
            P.op("dve", lambda e: e.memset(R["epsb"][:], EPS), writes=["epsb"])
            xTv = self.xT.rearrange("(kc p) t -> p kc t", p=128)
            wi = [0]

            def wload(wkey, col0, pc):
                i = wi[0] % NW
                wi[0] += 1
                return self.wload_piece(P, wkey, col0 // pc, wb[i], ("wb", i))

            pcol = self.pc1
            p2col = self.pc2
            ty_of = lambda tile: 1 if tile[2] else 0
            def do_tile(tile):
                t0, T, isctx = tile
                ty = ty_of(tile)
                self.norm_mod(P, R, tile, sub, R["xn"])
                xn = R["xn"]
                xnkeys = [("xn", kc) for kc in range(c.KC)]
                nA = 0
                for j0 in range(0, c.DFF, pcol):
                    b1, k1 = wload(("ffn_w1", l, which), j0, pcol)
                    b3, k3 = wload(("ffn_w3", l, which), j0, pcol)
                    for q in range(pcol // 128):
                        j = j0 // 128 + q
                        p1 = ps[1 + nA % 2]
                        p3 = ps[3 + nA % 2]
                        tm = R["tmp"][nA % 2]
                        for kc in range(c.KC):
                            P.op("pe", lambda e, b1=b1, q=q, kc=kc, p1=p1: e.matmul(p1[:, :T], lhsT=b1[:, kc, q * 128:(q + 1) * 128], rhs=xn[:, kc, :T], start=(kc == 0), stop=(kc == c.KC - 1)),
                                 reads=k1 + xnkeys, writes=[("ps", 1 + nA % 2)])
                        for kc in range(c.KC):
                            P.op("pe", lambda e, b3=b3, q=q, kc=kc, p3=p3: e.matmul(p3[:, :T], lhsT=b3[:, kc, q * 128:(q + 1) * 128], rhs=xn[:, kc, :T], start=(kc == 0), stop=(kc == c.KC - 1)),
                                 reads=k3 + xnkeys, writes=[("ps", 3 + nA % 2)])
                        P.op("act", lambda e, tm=tm, p1=p1: e.activation(out=tm[:, :T], in_=p1[:, :T], func=AF.Silu), reads=[("ps", 1 + nA % 2)], writes=[("tmp", nA % 2)])
                        P.op("dve", lambda e, tm=tm, p3=p3, j=j: e.tensor_tensor(out=h[:, j, :T], in0=tm[:, :T], in1=p3[:, :T], op=ALU.mult),
                             reads=[("tmp", nA % 2), ("ps", 3 + nA % 2)], writes=[("h", j)])
                        nA += 1
                hkeys = [("h", j) for j in range(c.JC)]
                nB = 0
                for d0 in range(0, c.D, p2col):
                    b2, k2 = wload(("ffn_w2", l, which), d0, p2col)
                    for q in range(p2col // 128):
                        dc = d0 // 128 + q
                        pb = ps[5 + nB % 3]
                        for jc in range(c.JC):
                            P.op("pe", lambda e, b2=b2, q=q, jc=jc, pb=pb: e.matmul(pb[:, :T], lhsT=b2[:, jc, q * 128:(q + 1) * 128], rhs=h[:, jc, :T], start=(jc == 0), stop=(jc == c.JC - 1)),
                                 reads=k2 + hkeys, writes=[("ps", 5 + nB % 3)])
                        xb = R["xc"][nB % 4]
                        ob = R["ob"][nB % 2]
                        P.op("sp", lambda e, xb=xb, dc=dc: e.dma_start(out=xb[:, :T], in_=xTv[:, dc, t0:t0 + T]), reads=[("xT", t0, dc)], writes=[("xc", nB % 4)])
                        P.op("dve", lambda e, xb=xb, ob=ob, pb=pb, dc=dc: e.scalar_tensor_tensor(out=ob[:, :T], in0=pb[:, :T], scalar=self.GT[:, ty, sub, dc:dc + 1], in1=xb[:, :T], op0=ALU.mult, op1=ALU.add),
                             reads=[("ps", 5 + nB % 3), ("xc", nB % 4), "GT"], writes=[("ob", nB % 2)])
                        P.op("sp", lambda e, ob=ob, dc=dc: e.dma_start(out=xTv[:, dc, t0:t0 + T], in_=ob[:, :T]), reads=[("ob", nB % 2)], writes=[("xT", t0, dc)])
                        nB += 1

            for tile in c.tiles:
                if skip_ctx and tile[2]:
                    continue
                do_tile(tile)
            P.emit()

    def phase_proj(self, l):
        c, nc = self.cfg, self.nc
        with ExitStack() as es:
            P = Prog(nc, self.pool)
            R = self.alloc_dense(es)
            ps = R["ps"]
            NW = 4
            wb = [self.sb(es, f"wbp{i}", [128, 8192], BF16) for i in range(NW)]
            stg = [self.sb(es, f"pst{i}", [128, 512]) for i in range(4)]
            P.op("dve", lambda e: e.memset(R["epsb"][:], EPS), writes=["epsb"])
            pcol = self.pc1
            wi = [0]

            def wload(col0, ncols):
                i = wi[0] % NW
                wi[0] += 1
                return self.wload_piece(P, ("w_in", l), col0 // pcol, wb[i], ("wb", i))

            def do_tile(tile):
                t0, T, isctx = tile
                self.norm_mod(P, R, tile, 1, R["xn"])
                xn = R["xn"]
                xnkeys = [("xn", kc) for kc in range(c.KC)]
                n = 0
                for c0 in range(0, c.INW, pcol):
                    nco = min(pcol, c.INW - c0)
                    bw, kw = wload(c0, nco)
                    for q0 in range(0, nco, 128):
                        m = min(128, nco - q0)
                        col = c0 + q0
                        pb = ps[1 + n % 4]
                        sg = stg[n % 4]
                        for kc in range(c.KC):
                            P.op("pe", lambda e, bw=bw, q0=q0, m=m, kc=kc, pb=pb: e.matmul(pb[:m, :T], lhsT=bw[:, kc, q0:q0 + m], rhs=xn[:, kc, :T], start=(kc == 0), stop=(kc == c.KC - 1)),
                                 reads=kw + xnkeys, writes=[("ps", 1 + n % 4)])
                        eng = "act" if n % 2 == 0 else "dve"
                        if eng == "act":
                            P.op("act", lambda e, sg=sg, pb=pb, m=m: e.activation(out=sg[:m, :T], in_=pb[:m, :T], func=AF.Copy), reads=[("ps", 1 + n % 4)], writes=[("pst", n % 4)])
                        else:
                            P.op("dve", lambda e, sg=sg, pb=pb, m=m: e.tensor_copy(out=sg[:m, :T], in_=pb[:m, :T]), reads=[("ps", 1 + n % 4)], writes=[("pst", n % 4)])
                        if col < self.NA:
                            dst = self.pTa[col:col + m, t0:t0 + T]
                        else:
                            dst = self.pTc[col - self.NA:col - self.NA + m, t0:t0 + T]
                        P.op("sp", lambda e, sg=sg, dst=dst, m=m: e.dma_start(out=dst, in_=sg[:m, :T]), reads=[("pst", n % 4)], writes=[("pT", col, t0)])
                        n += 1
            for tile in c.tiles:
                do_tile(tile)
            P.emit()

    def phase_merge(self, l, last):
        c, nc = self.cfg, self.nc
        with ExitStack() as es:
            P = Prog(nc, self.pool)
            R = self.alloc_dense(es)
            ps = R["ps"]
            NW = 3
            wb = [self.sb(es, f"wbm{i}", [128, 8192], BF16) for i in range(NW)]
            wbb = [self.sb(es, f"wbb{i}", [128, 2048], BF16) for i in range(6)]
            osb = self.sb(es, "osb", [128, 24, 512], BF16)
            ysb = self.sb(es, "ysb", [128, c.KC, 512], BF16)
            sgs = [self.sb(es, f"sgs{i}", [128, 512]) for i in range(2)]
            yac = [self.sb(es, f"yac{i}", [128, 512]) for i in range(2)]
            bg = self.sb(es, "bg", [128, 3, c.KC])
            P.op("dve", lambda e: e.memset(R["epsb"][:], EPS), writes=["epsb"])
            P.op("sp", lambda e: e.dma_start(out=bg[:], in_=self.b_gateT[l, :, :, :]), writes=["bg"])
            oTv = self.oT.rearrange("(r p) t -> p r t", p=128)
            xTv = self.xT.rearrange("(kc p) t -> p kc t", p=128)
            pcol = self.pc1
            wi = [0]
            wj = [0]

            def wload(wkey, col0, small=False):
                if small:
                    i = wj[0] % 6
                    wj[0] += 1
                    return self.wload_piece(P, wkey, col0 // pcol, wbb[i], ("wbb", i))
                i = wi[0] % NW
                wi[0] += 1
                return self.wload_piece(P, wkey, col0 // pcol, wb[i], ("wb", i))

            def do_tile(tile):
                t0, T, isctx = tile
                ty = 1 if isctx else 0
                self.norm_mod(P, R, tile, 1, R["xn"])
                xn = R["xn"]
                xnkeys = [("xn", kc) for kc in range(c.KC)]
                for r8 in range(0, 24, 8):
                    P.op("sp", lambda e, r8=r8: e.dma_start(out=osb[:, r8:r8 + 8, :T], in_=oTv[:, r8:r8 + 8, t0:t0 + T]), reads=[("oT", t0)], writes=[("osb", r8)])
                n = 0
                for d0 in range(0, c.D, pcol):
                    gw = [wload(("w_gate", l, b), d0) for b in range(3)]
                    bwl = [wload(("w_branch", l, b), d0, small=True) for b in range(3)]
                    for q in range(pcol // 128):
                        dc = d0 // 128 + q
                        ya = yac[n % 2]
                        for b in range(3):
                            pg = ps[1 + (n * 3 + b) % 3]
                            pt = ps[4 + (n * 3 + b) % 3]
                            gb, gk = gw[b]
                            bb, bk = bwl[b]
                            sg = sgs[(n * 3 + b) % 2]
                            for kc in range(c.KC):
                                P.op("pe", lambda e, gb=gb, q=q, kc=kc, pg=pg: e.matmul(pg[:, :T], lhsT=gb[:, kc, q * 128:(q + 1) * 128], rhs=xn[:, kc, :T], start=(kc == 0), stop=(kc == c.KC - 1)),
                                     reads=gk + xnkeys, writes=[("ps", 1 + (n * 3 + b) % 3)])
                            for kc in range(8):
                                P.op("pe", lambda e, bb=bb, q=q, kc=kc, pt=pt, b=b: e.matmul(pt[:, :T], lhsT=bb[:, kc, q * 128:(q + 1) * 128], rhs=osb[:, b * 8 + kc, :T], start=(kc == 0), stop=(kc == 7)),
                                     reads=bk + [("osb", b * 8)], writes=[("ps", 4 + (n * 3 + b) % 3)])
                            P.op("act", lambda e, sg=sg, pg=pg, b=b, dc=dc: e.activation(out=sg[:, :T], in_=pg[:, :T], func=AF.Sigmoid, bias=bg[:, b, dc:dc + 1]),
                                 reads=[("ps", 1 + (n * 3 + b) % 3), "bg"], writes=[("sgs", (n * 3 + b) % 2)])
                            if b == 0:
                                P.op("dve", lambda e, sg=sg, pt=pt, ya=ya: e.tensor_tensor(out=ya[:, :T], in0=sg[:, :T], in1=pt[:, :T], op=ALU.mult),
                                     reads=[("sgs", (n * 3 + b) % 2), ("ps", 4 + (n * 3 + b) % 3)], writes=[("yac", n % 2)])
                            else:
                                P.op("dve", lambda e, sg=sg, pt=pt: e.tensor_tensor(out=sg[:, :T], in0=sg[:, :T], in1=pt[:, :T], op=ALU.mult),
                                     reads=[("sgs", (n * 3 + b) % 2), ("ps", 4 + (n * 3 + b) % 3)], writes=[("sgs", (n * 3 + b) % 2)])
                                if b == 1:
                                    P.op("dve", lambda e, sg=sg, ya=ya: e.tensor_tensor(out=ya[:, :T], in0=ya[:, :T], in1=sg[:, :T], op=ALU.add),
                                         reads=[("sgs", (n * 3 + b) % 2), ("yac", n % 2)], writes=[("yac", n % 2)])
                                else:
                                    P.op("dve", lambda e, sg=sg, ya=ya, dc=dc: e.tensor_tensor(out=ysb[:, dc, :T], in0=ya[:, :T], in1=sg[:, :T], op=ALU.add),
                                         reads=[("sgs", (n * 3 + b) % 2), ("yac", n % 2)], writes=[("ysb", dc)])
                        n += 1
                ykeys = [("ysb", dc) for dc in range(c.KC)]
                nB = 0
                for d0 in range(0, c.D, pcol):
                    ow, ok = wload(("w_out", l), d0)
                    for q in range(pcol // 128):
                        dc = d0 // 128 + q
                        pb = ps[7]
                        for kc in range(c.KC):
                            P.op("pe", lambda e, ow=ow, q=q, kc=kc, pb=pb: e.matmul(pb[:, :T], lhsT=ow[:, kc, q * 128:(q + 1) * 128], rhs=ysb[:, kc, :T], start=(kc == 0), stop=(kc == c.KC - 1)),
                                 reads=ok + ykeys, writes=[("ps", 7)])
                        xb = R["xc"][nB % 4]
                        ob = R["ob"][nB % 2]
                        P.op("sp", lambda e, xb=xb, dc=dc: e.dma_start(out=xb[:, :T], in_=xTv[:, dc, t0:t0 + T]), reads=[("xT", t0, dc)], writes=[("xc", nB % 4)])
                        P.op("dve", lambda e, xb=xb, ob=ob, pb=pb, dc=dc: e.scalar_tensor_tensor(out=ob[:, :T], in0=pb[:, :T], scalar=self.GT[:, ty, 1, dc:dc + 1], in1=xb[:, :T], op0=ALU.mult, op1=ALU.add),
                             reads=[("ps", 7), ("xc", nB % 4), "GT"], writes=[("ob", nB % 2)])
                        P.op("sp", lambda e, ob=ob, dc=dc: e.dma_start(out=xTv[:, dc, t0:t0 + T], in_=ob[:, :T]), reads=[("ob", nB % 2)], writes=[("xT", t0, dc)])
                        nB += 1
            for tile in c.tiles:
                if last and tile[2]:
                    continue
                do_tile(tile)
            P.emit()


    def phase_mix(self, l, last):
        self.mix_attn(l, last)
        if not os.environ.get("NO_MLSTM"):
            self.mix_mlstm(l, last)

    def prep_qk(self, P, W, src, dst, gcol, rope, tagn):
        c = self.cfg
        ps = W["ps"]
        for bi, b0 in enumerate(range(0, c.NTOK, 512)):
            n = min(512, c.NTOK - b0)
            ld = W["ld"][bi % 2]
            sq = W["sq"][bi % 2]
            rs = W["rs"][bi % 2]
            xn = W["xn"][bi % 2]
            k = bi % 2
            P.op("sp", lambda e, ld=ld, b0=b0, n=n: e.dma_start(out=ld[:, :n], in_=src[:, b0:b0 + n]), writes=[("ld", k)])
            P.op("act", lambda e, ld=ld, sq=sq, n=n: e.activation(out=sq[:, :n], in_=ld[:, :n], func=AF.Square), reads=[("ld", k)], writes=[("sq", k)])
            P.op("pe", lambda e, sq=sq, n=n: e.matmul(ps[6][:, :n], lhsT=self.onesF[:], rhs=sq[:, :n], start=True, stop=True), reads=[("sq", k)], writes=[("ps", 6)])
            P.op("act", lambda e, rs=rs, n=n: e.activation(out=rs[:, :n], in_=ps[6][:, :n], func=AF.Sqrt, scale=1.0 / 128, bias=W["epsb"][:, 0:1]), reads=[("ps", 6), "epsb"], writes=[("rs", k)])
            P.op("dve", lambda e, rs=rs, n=n: e.reciprocal(out=rs[:, :n], in_=rs[:, :n]), reads=[("rs", k)], writes=[("rs", k)])
            if rope:
                cs = W["cs"][bi % 2]
                sn = W["sn"][bi % 2]
                P.op("dve", lambda e, ld=ld, rs=rs, xn=xn, n=n: e.scalar_tensor_tensor(out=xn[:, :n], in0=ld[:, :n], scalar=gcol, in1=rs[:, :n], op0=ALU.mult, op1=ALU.mult),
                     reads=[("ld", k), ("rs", k), "qkg"], writes=[("xn", k)])
                P.op("sp", lambda e, cs=cs, b0=b0, n=n: e.dma_start(out=cs[:, :n], in_=self.cosT[:, b0:b0 + n]), writes=[("cs", k)])
                P.op("sp", lambda e, sn=sn, b0=b0, n=n: e.dma_start(out=sn[:, :n], in_=self.sinT[:, b0:b0 + n]), writes=[("sn", k)])
                P.op("pe", lambda e, xn=xn, n=n: e.matmul(ps[7][:, :n], lhsT=W["Rm"][:], rhs=xn[:, :n], start=True, stop=True), reads=[("xn", k), "Rm"], writes=[("ps", 7)])
                P.op("dve", lambda e, sn=sn, n=n: e.tensor_tensor(out=sn[:, :n], in0=ps[7][:, :n], in1=sn[:, :n], op=ALU.mult), reads=[("ps", 7), ("sn", k)], writes=[("sn", k)])
                P.op("dve", lambda e, cs=cs, xn=xn, n=n: e.tensor_tensor(out=cs[:, :n], in0=xn[:, :n], in1=cs[:, :n], op=ALU.mult), reads=[("xn", k), ("cs", k)], writes=[("cs", k)])
                P.op("dve", lambda e, cs=cs, sn=sn, b0=b0, n=n: e.tensor_tensor(out=dst[:, b0:b0 + n], in0=cs[:, :n], in1=sn[:, :n], op=ALU.add), reads=[("cs", k), ("sn", k)], writes=[(tagn, bi)])
            else:
                P.op("dve", lambda e, ld=ld, rs=rs, b0=b0, n=n: e.scalar_tensor_tensor(out=dst[:, b0:b0 + n], in0=ld[:, :n], scalar=gcol, in1=rs[:, :n], op0=ALU.mult, op1=ALU.mult),
                     reads=[("ld", k), ("rs", k), "qkg"], writes=[(tagn, bi)])
        return [(tagn, bi) for bi in range((c.NTOK + 511) // 512)]

    def prep_tok(self, P, W, src, dst, tagn, scale_cols=None):
        c = self.cfg
        ps = W["ps"]
        for bi, b0 in enumerate(range(0, c.NTOK, 512)):
            n = min(512, c.NTOK - b0)
            k = bi % 2
            ld = W["ld"][k]
            P.op("sp", lambda e, ld=ld, b0=b0, n=n: e.dma_start(out=ld[:, :n], in_=src[:, b0:b0 + n]), writes=[("ld", k)])
            nt = n // 128
            for q in range(nt):
                P.op("pe", lambda e, ld=ld, q=q: e.transpose(out=ps[7][:, q * 128:(q + 1) * 128], in_=ld[:, q * 128:(q + 1) * 128], identity=self.identS[:]),
                     reads=[("ld", k)], writes=[("ps", 7)])
            P.op("act", lambda e, b0=b0, nt=nt: e.activation(out=dst[:, b0 // 128:b0 // 128 + nt, :], in_=ps[7][:, :nt * 128].rearrange("p (a b) -> p a b", b=128), func=AF.Copy),
                 reads=[("ps", 7)], writes=[(tagn, bi)])
        return [(tagn, bi) for bi in range((c.NTOK + 511) // 512)]

    def attn(self, P, W, qb, kb, vb, rkeys, sched, sinkcol, ob):
        ps = W["ps"]
        scale = 128 ** -0.5
        cnt = 0
        for qi, (qt, klist) in enumerate(sched):
            po = ps[2 + qi % 2]
            pd = ps[4 + qi % 2]
            nk = len(klist)
            for ki, (kt, bias, bkey) in enumerate(klist):
                pss = ps[cnt % 2]
                pt = W["pt"][cnt % 3]
                P.op("pe", lambda e, pss=pss, kt=kt, qt=qt: e.matmul(pss[:, :128], lhsT=kb[:, kt * 128:(kt + 1) * 128], rhs=qb[:, qt * 128:(qt + 1) * 128], start=True, stop=True),
                     reads=rkeys, writes=[("ps", cnt % 2)])
                if bias is None:
                    P.op("act", lambda e, pss=pss, pt=pt: e.activation(out=pt[:], in_=pss[:, :128], func=AF.Exp, scale=scale), reads=[("ps", cnt % 2)], writes=[("pt", cnt % 3)])
                else:
                    tb = W["tb"][cnt % 2]
                    P.op("dve", lambda e, pss=pss, tb=tb, bias=bias: e.scalar_tensor_tensor(out=tb[:], in0=pss[:, :128], scalar=scale, in1=bias, op0=ALU.mult, op1=ALU.add),
                         reads=[("ps", cnt % 2), bkey], writes=[("tb", cnt % 2)])
                    P.op("act", lambda e, tb=tb, pt=pt: e.activation(out=pt[:], in_=tb[:], func=AF.Exp), reads=[("tb", cnt % 2)], writes=[("pt", cnt % 3)])
                P.op("pe", lambda e, po=po, pt=pt, kt=kt, ki=ki, nk=nk: e.matmul(po[:, :128], lhsT=vb[:, kt, :], rhs=pt[:], start=(ki == 0), stop=(ki == nk - 1)),
                     reads=rkeys + [("pt", cnt % 3)], writes=[("ps", 2 + qi % 2)])
                P.op("pe", lambda e, pd=pd, pt=pt, ki=ki, nk=nk: e.matmul(pd[:, :128], lhsT=W["onesB"][:], rhs=pt[:], start=(ki == 0), stop=(ki == nk - 1)),
                     reads=[("pt", cnt % 3), "onesB"], writes=[("ps", 4 + qi % 2)])
                cnt += 1
            rc = W["rc"][qi % 2]
            if sinkcol is not None:
                P.op("dve", lambda e, rc=rc, pd=pd: e.tensor_scalar(out=rc[:], in0=pd[:, :128], scalar1=sinkcol, scalar2=None, op0=ALU.add), reads=[("ps", 4 + qi % 2), "sink"], writes=[("rc", qi % 2)])
                P.op("dve", lambda e, rc=rc: e.reciprocal(out=rc[:], in_=rc[:]), reads=[("rc", qi % 2)], writes=[("rc", qi % 2)])
            else:
                P.op("dve", lambda e, rc=rc, pd=pd: e.reciprocal(out=rc[:], in_=pd[:, :128]), reads=[("ps", 4 + qi % 2)], writes=[("rc", qi % 2)])
            P.op("dve", lambda e, rc=rc, po=po, qt=qt: e.tensor_tensor(out=ob[:, qt * 128:(qt + 1) * 128], in0=po[:, :128], in1=rc[:], op=ALU.mult),
                 reads=[("ps", 2 + qi % 2), ("rc", qi % 2)], writes=[("ob", qt)])

    def mix_attn(self, l, last):
        c, nc = self.cfg, self.nc
        MCN = self.mc
        ncase = len(MCN["case_list"])
        NT = c.NTOK
        ntile = NT // 128
        nctx = c.CTX // 128
        nblk = c.SEQ // 128
        with ExitStack() as es:
            P = Prog(nc, self.pool)
            W = {}
            W["ps"] = self.psum_banks(es)
            for nm in ("ld", "sq", "rs", "xn", "cs", "sn"):
                W[nm] = [self.sb(es, f"{nm}{i}", [128, 512]) for i in range(2)]
            W["epsb"] = self.sb(es, "epsb", [128, 1])
            W["Rm"] = self.sb(es, "Rm", [128, 128])
            W["onesB"] = self.sb(es, "onesB", [128, 128], BF16)
            W["pt"] = [self.sb(es, f"pt{i}", [128, 128], BF16) for i in range(3)]
            W["tb"] = [self.sb(es, f"tb{i}", [128, 128]) for i in range(2)]
            W["rc"] = [self.sb(es, f"rc{i}", [128, 128]) for i in range(2)]
            MA = self.sb(es, "MA", [128, 2, 128])
            mnegB = self.sb(es, "mnegB", [128, ncase, 128])
            RBM = self.sb(es, "RBM", [128, ncase, 128])
            qkg = self.sb(es, "qkg", [128, 4])
            snk = self.sb(es, "snk", [128, 8])
            qb = self.sb(es, "qb", [128, NT], BF16)
            kb = self.sb(es, "kb", [128, NT], BF16)
            vb = self.sb(es, "vb", [128, ntile, 128], BF16)
            ob = self.sb(es, "ob", [128, NT], BF16)
            P.op("dve", lambda e: e.memset(W["epsb"][:], EPS), writes=["epsb"])
            P.op("dve", lambda e: e.memset(W["onesB"][:], 1.0), writes=["onesB"])
            P.op("sp", lambda e: e.dma_start(out=W["Rm"][:], in_=self.RmD[:, :]), writes=["Rm"])
            P.op("sp", lambda e: e.dma_start(out=MA[:], in_=self.MAD[:, :, :]), writes=["MA"])
            P.op("sp", lambda e: e.dma_start(out=mnegB[:], in_=self.mnegBD[:, :, :]), writes=["mnegB"])
            P.op("sp", lambda e: e.dma_start(out=qkg[:], in_=self.qk_gT[l, :, :]), writes=["qkg"])
            P.op("sp", lambda e: e.dma_start(out=snk[:], in_=self.sinkB[l, :, :]), writes=["snk0"])
            P.op("act", lambda e: e.activation(out=snk[:], in_=snk[:], func=AF.Exp), reads=["snk0"], writes=["sink"])
            qtiles_ctx = [] if last else list(range(nctx))
            for kv in range(2):
                kk = self.prep_qk(P, W, self.pTa[1024 + kv * 128:1024 + (kv + 1) * 128, :], kb, qkg[:, 1:2], True, "kb")
                vk = self.prep_tok(P, W, self.pTa[1280 + kv * 128:1280 + (kv + 1) * 128, :], vb, "vb")
                for g in range(4):
                    h = kv * 4 + g
                    qk = self.prep_qk(P, W, self.pTa[h * 128:(h + 1) * 128, :], qb, qkg[:, 0:1], True, "qb")
                    sched = []
                    for qt in qtiles_ctx:
                        sched.append((qt, [(kt, None, None) for kt in range(nctx)]))
                    for n in range(nblk):
                        kl = [(kt, None, None) for kt in range(nctx)]
                        if n > 0:
                            kl.append((nctx + n - 1, MA[:, 0, :], "MA"))
                        kl.append((nctx + n, None, None))
                        if n < nblk - 1:
                            kl.append((nctx + n + 1, MA[:, 1, :], "MA"))
                        sched.append((nctx + n, kl))
                    self.attn(P, W, qb, kb, vb, kk + vk + qk, sched, snk[:, h:h + 1], ob)
                    t0o = 0 if not last else c.CTX
                    P.op("sp", lambda e, h=h, t0o=t0o: e.dma_start(out=self.oT[h * 128:(h + 1) * 128, t0o:], in_=ob[:, t0o:]),
                         reads=[("ob", qt) for qt, _ in sched], writes=[("oT", "A", h)])
            for h in range(8):
                P.op("sp", lambda e, h=h: e.dma_start(out=RBM[:], in_=self.rbD[l, h, :, :, :]), writes=["RBM"])
                P.op("dve", lambda e: e.tensor_tensor(out=RBM[:], in0=RBM[:], in1=mnegB[:], op=ALU.add), reads=["RBM", "mnegB"], writes=["RBM"])
                kk = self.prep_qk(P, W, self.pTa[2560 + h * 128:2560 + (h + 1) * 128, :], kb, qkg[:, 3:4], False, "kb")
                vk = self.prep_tok(P, W, self.pTa[3584 + h * 128:3584 + (h + 1) * 128, :], vb, "vb")
                qk = self.prep_qk(P, W, self.pTa[1536 + h * 128:1536 + (h + 1) * 128, :], qb, qkg[:, 2:3], False, "qb")
                sched = []
                for qt in qtiles_ctx:
                    sched.append((qt, [(kt, None, None) for kt in range(nctx)]))
                for n in range(nblk):
                    kl = [(kt, None, None) for kt in range(nctx)]
                    for (a, cs_) in MCN["sched"][n]:
                        kl.append((nctx + a, RBM[:, cs_, :], "RBM"))
                    sched.append((nctx + n, kl))
                self.attn(P, W, qb, kb, vb, kk + vk + qk, sched, None, ob)
                t0o = 0 if not last else c.CTX
                P.op("sp", lambda e, h=h, t0o=t0o: e.dma_start(out=self.oT[1024 + h * 128:1024 + (h + 1) * 128, t0o:], in_=ob[:, t0o:]),
                     reads=[("ob", qt) for qt, _ in sched], writes=[("oT", "B", h)])
            P.emit()


    def mix_mlstm(self, l, last):
        c, nc = self.cfg, self.nc
        NT = c.NTOK
        ntile = NT // 128
        NCH = NT // 64
        ncc = c.CTX // 64
        nct = c.CTX // 128
        isq = 128 ** -0.5
        blocks = [(0, c.CTX)] if c.CTX <= 512 else [(b, min(512, c.CTX - b)) for b in range(0, c.CTX, 512)]
        blocks += [(b, min(512, NT - b)) for b in range(c.CTX, NT, 512)]
        with ExitStack() as es:
            P = Prog(nc, self.pool)
            ps = self.psum_banks(es)
            T = {}
            for nm in ("ta", "tb", "tc", "td", "te", "tf", "tg", "th"):
                T[nm] = self.sb(es, "m" + nm, [128, 516])
            epsb = self.sb(es, "epsb", [128, 1])
            lnsc = self.sb(es, "lnsc", [128, 1])
            MC = self.sb(es, "MC", [128, 2, 128])
            mres = self.sb(es, "mres", [128, 512])
            negf = self.sb(es, "negf", [128, 512])
            negb = self.sb(es, "negb", [128, 512])
            cw = self.sb(es, "cw", [128, 16, 5])
            cb = self.sb(es, "cb", [128, 16])
            gb = self.sb(es, "gb", [128, 32])
            ngb = self.sb(es, "ngb", [128, 32])
            mng = self.sb(es, "mng", [128, 8])
            CUM = self.sb(es, "CUM", [128, NT])
            CMX = self.sb(es, "CMX", [128, NT])
            qcT = self.sb(es, "qcT", [128, NT], BF16)
            kcT = self.sb(es, "kcT", [128, NT], BF16)
            ktok = self.sb(es, "ktok", [128, ntile, 128], BF16)
            vaug = self.sb(es, "vaug", [128, ntile, 130], BF16)
            obC = self.sb(es, "obC", [128, NT], BF16)
            cols = {nm: self.sb(es, nm, [128, ntile]) for nm in ("wcol", "a2col", "sicol", "emcol")}
            ch = {nm: self.sb(es, nm, [128, NCH]) for nm in ("BL", "MLOC", "MAF", "M0", "SP", "SL", "tch")}
            Cst = self.sb(es, "Cst", [128, 130])
            C0c = [self.sb(es, f"C0c{i}", [128, 130], BF16) for i in range(4)]
            ctmp = [self.sb(es, f"ctmp{i}", [128, 130]) for i in range(2)]
            kw = [self.sb(es, f"kw{i}", [128, 128], BF16) for i in range(2)]
            et = [self.sb(es, f"et{i}", [128, 128]) for i in range(2)]
            swT = [self.sb(es, f"swT{i}", [128, 128], BF16) for i in range(2)]
            Hs = [self.sb(es, f"Hs{i}", [128, 130]) for i in range(2)]
            hf = [self.sb(es, f"hf{i}", [128, 128]) for i in range(2)]
            hn = [self.sb(es, f"hn{i}", [128, 128]) for i in range(2)]
            sc1 = [self.sb(es, f"sc1{i}", [128, 2]) for i in range(2)]
            sgo = [self.sb(es, f"sgo{i}", [128, 128]) for i in range(2)]
            ident = self.identS
            hscr = self.hscr
            ldv = [self.sb(es, f"ldv{i}", [128, 512]) for i in range(2)]
            P.op("dve", lambda e: e.memset(epsb[:], EPS), writes=["epsb"])
            P.op("dve", lambda e: e.memset(lnsc[:], float(np.log(isq))), writes=["lnsc"])
            P.op("dve", lambda e: e.memset(vaug[:, :, 128:130], 1.0), writes=["vones"])
            P.op("sp", lambda e: e.dma_start(out=MC[:], in_=self.MCD[:, :, :]), writes=["MC"])
            P.op("sp", lambda e: e.dma_start(out=mres[:], in_=self.mresD[:, 0:512]), writes=["mres"])
            P.op("sp", lambda e: e.dma_start(out=negf[:], in_=self.negfD[:, 0:512]), writes=["negf"])
            P.op("sp", lambda e: e.dma_start(out=negb[:], in_=self.negbD[:, 0:512]), writes=["negb"])
            P.op("sp", lambda e: e.dma_start(out=cw[:], in_=self.conv_wT[l, :, :, :]), writes=["cw"])
            P.op("sp", lambda e: e.dma_start(out=cb[:], in_=self.conv_bT[l, :, :]), writes=["cb"])
            P.op("sp", lambda e: e.dma_start(out=gb[:], in_=self.gate_bB[l, :, :]), writes=["gb"])
            P.op("dve", lambda e: e.tensor_scalar(out=ngb[:], in0=gb[:], scalar1=-1.0, scalar2=None, op0=ALU.mult), reads=["gb"], writes=["ngb"])
            P.op("sp", lambda e: e.dma_start(out=mng[:], in_=self.mngT[l, :, :]), writes=["mng"])

            def diag(X, n, dst, t0):
                nt = n // 128
                tmp = T["th"]
                P.op("dve", lambda e: e.tensor_tensor(out=tmp[:, :n].rearrange("p (a b) -> p a b", b=128), in0=X[:, :n].rearrange("p (a b) -> p a b", b=128),
                                                      in1=ident[:].unsqueeze(1).to_broadcast([128, nt, 128]), op=ALU.mult), reads=["X", "ident"], writes=["th"])
                P.op("dve", lambda e: e.tensor_reduce(out=dst[:, t0:t0 + nt], in_=tmp[:, :n].rearrange("p (a b) -> p a b", b=128), axis=AX.X, op=ALU.add), reads=["th"], writes=["cols"])

            def do_head(h):
                def do_conv(which, dstT):
                    src = self.pTc[which * 1024 + h * 128:which * 1024 + (h + 1) * 128, :]
                    ci = which * 8 + h
                    for (b0, n) in blocks:
                        seg0, seg1 = (0, c.CTX) if b0 < c.CTX else (c.CTX, NT)
                        lo = max(seg0, b0 - 2)
                        hi = min(seg1, b0 + n + 2)
                        ld = T["ta"]
                        acc = T["tb"]
                        P.op("dve", lambda e, ld=ld: e.memset(ld[:], 0.0), writes=["ta"])
                        P.op("sp", lambda e, ld=ld, lo=lo, hi=hi, b0=b0: e.dma_start(out=ld[:, lo - (b0 - 2):hi - (b0 - 2)], in_=src[:, lo:hi]), writes=["ta"])
                        P.op("dve", lambda e, ld=ld, acc=acc, n=n, ci=ci: e.tensor_scalar(out=acc[:, :n], in0=ld[:, 0:n], scalar1=cw[:, ci, 0:1], scalar2=cb[:, ci:ci + 1], op0=ALU.mult, op1=ALU.add),
                             reads=["ta", "cw", "cb"], writes=["tb"])
                        for j in range(1, 5):
                            P.op("dve", lambda e, ld=ld, acc=acc, n=n, ci=ci, j=j: e.scalar_tensor_tensor(out=acc[:, :n], in0=ld[:, j:j + n], scalar=cw[:, ci, j:j + 1], in1=acc[:, :n], op0=ALU.mult, op1=ALU.add),
                                 reads=["ta", "tb", "cw"], writes=["tb"])
                        P.op("act", lambda e, acc=acc, b0=b0, n=n: e.activation(out=dstT[:, b0:b0 + n], in_=acc[:, :n], func=AF.Silu), reads=["tb"], writes=[("qk", which)])
                do_conv(0, qcT)
                do_conv(1, kcT)
                for ti in range(ntile):
                    kf = T["tc"]
                    P.op("dve", lambda e, ti=ti: e.tensor_copy(out=kf[:, :128], in_=kcT[:, ti * 128:(ti + 1) * 128]), reads=[("qk", 1)], writes=["tc"])
                    P.op("pe", lambda e: e.transpose(out=ps[7][:, :128], in_=kf[:, :128], identity=ident[:]), reads=["tc"], writes=[("ps", 7)])
                    P.op("act", lambda e, ti=ti: e.activation(out=ktok[:, ti, :], in_=ps[7][:, :128], func=AF.Copy), reads=[("ps", 7)], writes=["ktok"])
                W = {"ps": ps, "ld": ldv}
                self.prep_tok(P, W, self.pTc[2048 + h * 128:2048 + (h + 1) * 128, :], vaug[:, :, 0:128], "vaugw")
                P.op("dve", lambda e: e.tensor_copy(out=T["tc"][:, 0:1], in_=T["tc"][:, 0:1]), reads=[("vaugw", bi) for bi in range((NT + 511) // 512)] + ["vones"], writes=["vaug"])

                def do_dir(dr):
                    gi_row = 4096 + (2 * dr) * 8 + h
                    gf_row = 4096 + (2 * dr + 1) * 8 + h
                    gi_c = (2 * dr) * 8 + h
                    gf_c = (2 * dr + 1) * 8 + h
                    for (b0, n) in blocks:
                        nchb = n // 64
                        c0 = b0 // 64
                        LI, LF, CP, A_, TMP = T["ta"], T["tb"], T["tc"], T["td"], T["te"]
                        P.op("sp", lambda e, b0=b0, n=n: e.dma_start(out=LI[:, :n], in_=self.pTc[gi_row:gi_row + 1, b0:b0 + n].to_broadcast([128, n])), writes=["ta"])
                        P.op("sp", lambda e, b0=b0, n=n: e.dma_start(out=LF[:, :n], in_=self.pTc[gf_row:gf_row + 1, b0:b0 + n].to_broadcast([128, n])), writes=["tb"])
                        P.op("act", lambda e, n=n: e.activation(out=LI[:, :n], in_=LI[:, :n], func=AF.Identity, bias=gb[:, gi_c:gi_c + 1]), reads=["ta", "gb"], writes=["ta"])
                        P.op("act", lambda e, n=n: e.activation(out=LF[:, :n], in_=LF[:, :n], func=AF.Exp, scale=-1.0, bias=ngb[:, gf_c:gf_c + 1]), reads=["tb", "ngb"], writes=["tb"])
                        P.op("dve", lambda e, n=n: e.tensor_scalar(out=LF[:, :n], in0=LF[:, :n], scalar1=1.0, scalar2=None, op0=ALU.add), reads=["tb"], writes=["tb"])
                        P.op("act", lambda e, n=n: e.activation(out=LF[:, :n], in_=LF[:, :n], func=AF.Ln), reads=["tb"], writes=["tb"])
                        P.op("dve", lambda e, n=n: e.tensor_scalar(out=LF[:, :n], in0=LF[:, :n], scalar1=-1.0, scalar2=None, op0=ALU.mult), reads=["tb"], writes=["tb"])
                        P.op("dve", lambda e, n=n: e.tensor_tensor_scan(out=CP[:, :n], data0=mres[:, :n], data1=LF[:, :n], initial=0.0, op0=ALU.mult, op1=ALU.add), reads=["tb", "mres"], writes=["tc"])
                        P.op("dve", lambda e, n=n, c0=c0, nchb=nchb: e.tensor_copy(out=ch["BL"][:, c0:c0 + nchb], in_=CP[:, :n].rearrange("p (a b) -> p a b", b=64)[:, :, 63]), reads=["tc"], writes=["BL"])
                        blbc = lambda c0=c0, nchb=nchb: ch["BL"][:, c0:c0 + nchb].unsqueeze(2).to_broadcast([128, nchb, 64])
                        v3 = lambda X, n=n: X[:, :n].rearrange("p (a b) -> p a b", b=64)
                        cumb = CUM[:, b0:b0 + n]
                        if dr == 0:
                            P.op("dve", lambda e, n=n, cumb=cumb: e.tensor_copy(out=cumb, in_=CP[:, :n]), reads=["tc"], writes=["CUM"])
                        else:
                            P.op("dve", lambda e, n=n, blbc=blbc, v3=v3: e.tensor_tensor(out=v3(TMP), in0=blbc(), in1=v3(CP), op=ALU.subtract), reads=["tc", "BL"], writes=["te"])
                            P.op("dve", lambda e, n=n, cumb=cumb: e.tensor_tensor(out=cumb, in0=TMP[:, :n], in1=LF[:, :n], op=ALU.add), reads=["te", "tb"], writes=["CUM"])
                        P.op("dve", lambda e, n=n, cumb=cumb: e.tensor_tensor(out=TMP[:, :n], in0=LI[:, :n], in1=cumb, op=ALU.subtract), reads=["ta", "CUM"], writes=["te"])
                        P.op("dve", lambda e, n=n, blbc=blbc, v3=v3: e.tensor_tensor(out=v3(A_), in0=v3(TMP), in1=blbc(), op=ALU.add), reads=["te", "BL"], writes=["td"])
                        P.op("dve", lambda e, n=n, c0=c0, nchb=nchb, v3=v3: e.tensor_reduce(out=ch["MLOC"][:, c0:c0 + nchb], in_=v3(A_), axis=AX.X, op=ALU.max), reads=["td"], writes=["MLOC"])
                        P.op("dve", lambda e, n=n, c0=c0, nchb=nchb, v3=v3: e.tensor_tensor(out=v3(A_), in0=v3(A_), in1=ch["MLOC"][:, c0:c0 + nchb].unsqueeze(2).to_broadcast([128, nchb, 64]), op=ALU.subtract),
                             reads=["td", "MLOC"], writes=["td"])
                        P.op("act", lambda e, n=n: e.activation(out=A_[:, :n], in_=A_[:, :n], func=AF.Exp), reads=["td"], writes=["X"])
                        diag(A_, n, cols["wcol"], b0 // 128)
                        P.op("dve", lambda e, n=n: e.tensor_copy(out=A_[:, :n], in_=TMP[:, :n]), reads=["te", "cols", "th"], writes=["X"])
                        diag(A_, n, cols["a2col"], b0 // 128)
                        cmxb = CMX[:, b0:b0 + n]
                        if dr == 0:
                            P.op("dve", lambda e, n=n, cmxb=cmxb: e.tensor_tensor_scan(out=cmxb, data0=negf[:, :n], data1=TMP[:, :n], initial=-1e30, op0=ALU.add, op1=ALU.max), reads=["te", "negf"], writes=["CMX"])
                        else:
                            P.op("dve", lambda e, n=n, b0=b0: e.tensor_tensor_scan(out=CMX[:, b0 + n - 1:b0 - 1 if b0 > 0 else None:-1], data0=negb[:, n - 1::-1], data1=TMP[:, n - 1::-1], initial=-1e30, op0=ALU.add, op1=ALU.max),
                                 reads=["te", "negb"], writes=["CMX"])
                    BL, MLOC, MAF, M0, SP, SL, tch = (ch[k] for k in ("BL", "MLOC", "MAF", "M0", "SP", "SL", "tch"))
                    if dr == 0:
                        P.op("dve", lambda e: e.tensor_tensor_scan(out=MAF[:, :], data0=BL[:, :], data1=MLOC[:, :], initial=0.0, op0=ALU.add, op1=ALU.max), reads=["BL", "MLOC"], writes=["MAF"])
                        P.op("dve", lambda e: e.memset(M0[:, 0:1], 0.0), writes=["M0"])
                        P.op("dve", lambda e: e.tensor_copy(out=M0[:, 1:NCH], in_=MAF[:, 0:NCH - 1]), reads=["MAF"], writes=["M0"])
                    else:
                        P.op("dve", lambda e: e.tensor_tensor_scan(out=MAF[:, ncc - 1::-1], data0=BL[:, ncc - 1::-1], data1=MLOC[:, ncc - 1::-1], initial=0.0, op0=ALU.add, op1=ALU.max), reads=["BL", "MLOC"], writes=["MAF"])
                        P.op("dve", lambda e: e.tensor_tensor_scan(out=MAF[:, NCH - 1:ncc - 1:-1], data0=BL[:, NCH - 1:ncc - 1:-1], data1=MLOC[:, NCH - 1:ncc - 1:-1], initial=MAF[:, 0:1], op0=ALU.add, op1=ALU.max),
                             reads=["BL", "MLOC", "MAF"], writes=["MAF"])
                        P.op("dve", lambda e: e.memset(M0[:, ncc - 1:ncc], 0.0), writes=["M0"])
                        if ncc > 1:
                            P.op("dve", lambda e: e.tensor_copy(out=M0[:, 0:ncc - 1], in_=MAF[:, 1:ncc]), reads=["MAF"], writes=["M0"])
                        P.op("dve", lambda e: e.tensor_copy(out=M0[:, ncc:NCH - 1], in_=MAF[:, ncc + 1:NCH]), reads=["MAF"], writes=["M0"])
                        P.op("dve", lambda e: e.tensor_copy(out=M0[:, NCH - 1:NCH], in_=MAF[:, 0:1]), reads=["MAF"], writes=["M0"])
                    P.op("dve", lambda e: e.tensor_tensor(out=tch[:, :], in0=BL[:, :], in1=M0[:, :], op=ALU.add), reads=["BL", "M0"], writes=["tch"])
                    P.op("dve", lambda e: e.tensor_tensor(out=tch[:, :], in0=tch[:, :], in1=MAF[:, :], op=ALU.subtract), reads=["tch", "MAF"], writes=["tch"])
                    P.op("act", lambda e: e.activation(out=SP[:, :], in_=tch[:, :], func=AF.Exp), reads=["tch"], writes=["SP"])
                    P.op("dve", lambda e: e.tensor_tensor(out=tch[:, :], in0=MLOC[:, :], in1=MAF[:, :], op=ALU.subtract), reads=["MLOC", "MAF", "SP"], writes=["tch"])
                    P.op("act", lambda e: e.activation(out=SL[:, :], in_=tch[:, :], func=AF.Exp), reads=["tch"], writes=["SL"])
                    for (b0, n) in blocks:
                        nchb = n // 64
                        c0 = b0 // 64
                        Z, X1, X2 = T["ta"], T["tb"], T["td"]
                        v3 = lambda X, n=n: X[:, :n].rearrange("p (a b) -> p a b", b=64)
                        m0bc = lambda c0=c0, nchb=nchb: M0[:, c0:c0 + nchb].unsqueeze(2).to_broadcast([128, nchb, 64])
                        cmxb = CMX[:, b0:b0 + n]
                        cumb = CUM[:, b0:b0 + n]
                        P.op("dve", lambda e, n=n, cmxb=cmxb, m0bc=m0bc, v3=v3: e.tensor_tensor(out=v3(Z), in0=cmxb.rearrange("p (a b) -> p a b", b=64), in1=m0bc(), op=ALU.max), reads=["CMX", "M0"], writes=["ta"])
                        P.op("dve", lambda e, n=n, m0bc=m0bc, v3=v3: e.tensor_tensor(out=v3(X1), in0=m0bc(), in1=v3(Z), op=ALU.subtract), reads=["ta", "M0"], writes=["tb"])
                        P.op("act", lambda e, n=n: e.activation(out=X2[:, :n], in_=X1[:, :n], func=AF.Exp, bias=lnsc[:, 0:1]), reads=["tb", "lnsc"], writes=["X"])
                        diag(X2, n, cols["sicol"], b0 // 128)
                        P.op("dve", lambda e, n=n, cumb=cumb: e.tensor_tensor(out=X1[:, :n], in0=cumb, in1=Z[:, :n], op=ALU.add), reads=["ta", "CUM", "cols", "th"], writes=["tb"])
                        P.op("act", lambda e, n=n: e.activation(out=X2[:, :n], in_=X1[:, :n], func=AF.Exp, scale=-1.0), reads=["tb"], writes=["X"])
                        diag(X2, n, cols["emcol"], b0 // 128)
                        P.op("dve", lambda e, n=n, cmxb=cmxb: e.tensor_scalar(out=cmxb, in0=Z[:, :n], scalar1=-1.0, scalar2=None, op0=ALU.mult), reads=["ta", "cols"], writes=["ROWP"])
                    if dr == 0:
                        order = list(range(ntile))
                    else:
                        order = list(range(nct - 1, -1, -1)) + list(range(ntile - 1, nct - 1, -1))
                    P.op("dve", lambda e: e.memset(Cst[:], 0.0), writes=["Cst"])
                    for oi, ti in enumerate(order):
                        isctx = ti < nct
                        chunks = (2 * ti, 2 * ti + 1) if dr == 0 else (2 * ti + 1, 2 * ti)
                        kwt = kw[oi % 2]
                        P.op("dve", lambda e, kwt=kwt, ti=ti: e.tensor_scalar(out=kwt[:], in0=ktok[:, ti, :], scalar1=cols["wcol"][:, ti:ti + 1], scalar2=None, op0=ALU.mult),
                             reads=["ktok", "cols"], writes=[("kw", oi % 2)])
                        c0s = []
                        for cc in chunks:
                            hb = (cc % 2) * 64
                            c0b = C0c[(2 * oi + (cc % 2)) % 4]
                            c0s.append((hb, c0b, (2 * oi + (cc % 2)) % 4))
                            P.op("act", lambda e, c0b=c0b: e.activation(out=c0b[:, 0:129], in_=Cst[:, 0:129], func=AF.Copy), reads=["Cst"], writes=[("C0c", (2 * oi + (cc % 2)) % 4)])
                            P.op("pe", lambda e, kwt=kwt, hb=hb, ti=ti: e.matmul(ps[6][:, 0:129], lhsT=kwt[hb:hb + 64, :], rhs=vaug[hb:hb + 64, ti, 0:129], start=True, stop=True),
                                 reads=[("kw", oi % 2), "vaug"], writes=[("ps", 6)])
                            ct = ctmp[cc % 2]
                            P.op("act", lambda e, ct=ct, cc=cc: e.activation(out=ct[:, 0:129], in_=ps[6][:, 0:129], func=AF.Copy, scale=SL[:, cc:cc + 1]), reads=[("ps", 6), "SL"], writes=[("ctmp", cc % 2)])
                            P.op("dve", lambda e, ct=ct, cc=cc: e.scalar_tensor_tensor(out=Cst[:, 0:129], in0=Cst[:, 0:129], scalar=SP[:, cc:cc + 1], in1=ct[:, 0:129], op0=ALU.mult, op1=ALU.add),
                                 reads=[("ctmp", cc % 2), "SP", "Cst"], writes=["Cst"])
                        if isctx and last:
                            continue
                        t0 = ti * 128
                        pS = ps[oi % 2]
                        pN = ps[2 + oi % 2]
                        pI = ps[4 + oi % 2]
                        e_t = et[oi % 2]
                        sw = swT[oi % 2]
                        P.op("pe", lambda e, pS=pS, t0=t0: e.matmul(pS[:, :128], lhsT=kcT[:, t0:t0 + 128], rhs=qcT[:, t0:t0 + 128], start=True, stop=True), reads=[("qk", 0), ("qk", 1)], writes=[("ps", oi % 2)])
                        P.op("dve", lambda e, e_t=e_t, t0=t0, ti=ti: e.scalar_tensor_tensor(out=e_t[:], in0=CMX[:, t0:t0 + 128], scalar=cols["a2col"][:, ti:ti + 1], in1=MC[:, dr, :], op0=ALU.add, op1=ALU.add),
                             reads=["ROWP", "cols", "MC"], writes=[("et", oi % 2)])
                        P.op("act", lambda e, e_t=e_t: e.activation(out=e_t[:], in_=e_t[:], func=AF.Exp), reads=[("et", oi % 2)], writes=[("et", oi % 2)])
                        P.op("dve", lambda e, e_t=e_t, sw=sw, pS=pS: e.scalar_tensor_tensor(out=sw[:], in0=pS[:, :128], scalar=isq, in1=e_t[:], op0=ALU.mult, op1=ALU.mult),
                             reads=[("ps", oi % 2), ("et", oi % 2)], writes=[("sw", oi % 2)])
                        P.op("pe", lambda e, pN=pN, sw=sw, ti=ti: e.matmul(pN[:, 0:129], lhsT=sw[:], rhs=vaug[:, ti, 0:129], start=True, stop=True), reads=[("sw", oi % 2), "vaug"], writes=[("ps", 2 + oi % 2)])
                        for (hb, c0b, ck) in c0s:
                            P.op("pe", lambda e, pI=pI, hb=hb, c0b=c0b, t0=t0: e.matmul(pI[hb:hb + 64, 0:129], lhsT=qcT[:, t0 + hb:t0 + hb + 64], rhs=c0b[:, 0:129], start=True, stop=True),
                                 reads=[("qk", 0), ("C0c", ck)], writes=[("ps", 4 + oi % 2, hb)])
                        H = Hs[oi % 2]
                        P.op("act", lambda e, H=H, pI=pI, ti=ti: e.activation(out=H[:, 0:129], in_=pI[:, 0:129], func=AF.Copy, scale=cols["sicol"][:, ti:ti + 1]),
                             reads=[("ps", 4 + oi % 2, 0), ("ps", 4 + oi % 2, 64), "cols"], writes=[("H", oi % 2)])
                        P.op("dve", lambda e, H=H, pN=pN: e.tensor_tensor(out=H[:, 0:129], in0=H[:, 0:129], in1=pN[:, 0:129], op=ALU.add), reads=[("H", oi % 2), ("ps", 2 + oi % 2)], writes=[("H", oi % 2)])
                        s1 = sc1[oi % 2]
                        P.op("dve", lambda e, H=H, s1=s1: e.tensor_scalar(out=s1[:, 0:1], in0=H[:, 128:129], scalar1=-1.0, scalar2=None, op0=ALU.mult), reads=[("H", oi % 2)], writes=[("sc1", oi % 2)])
                        P.op("dve", lambda e, H=H, s1=s1: e.tensor_tensor(out=s1[:, 0:1], in0=s1[:, 0:1], in1=H[:, 128:129], op=ALU.max), reads=[("H", oi % 2), ("sc1", oi % 2)], writes=[("sc1", oi % 2)])
                        P.op("dve", lambda e, s1=s1, ti=ti: e.tensor_scalar(out=s1[:, 0:1], in0=s1[:, 0:1], scalar1=cols["emcol"][:, ti:ti + 1], scalar2=None, op0=ALU.max),
                             reads=[("sc1", oi % 2), "cols"], writes=[("sc1", oi % 2)])
                        P.op("dve", lambda e, s1=s1: e.reciprocal(out=s1[:, 0:1], in_=s1[:, 0:1]), reads=[("sc1", oi % 2)], writes=[("sc1", oi % 2)])
                        hft = hf[oi % 2]
                        if dr == 0:
                            P.op("dve", lambda e, H=H, s1=s1, hft=hft: e.tensor_scalar(out=hft[:], in0=H[:, 0:128], scalar1=s1[:, 0:1], scalar2=None, op0=ALU.mult), reads=[("H", oi % 2), ("sc1", oi % 2)], writes=[("hf", oi % 2)])
                            P.op("sp", lambda e, hft=hft, ti=ti: e.dma_start(out=hscr[ti * 128:(ti + 1) * 128, :], in_=hft[:]), reads=[("hf", oi % 2)], writes=[("hscr", ti)])
                        else:
                            P.op("sp", lambda e, hft=hft, ti=ti: e.dma_start(out=hft[:], in_=hscr[ti * 128:(ti + 1) * 128, :]), reads=[("hscr", ti)], writes=[("hf", oi % 2)])
                            P.op("dve", lambda e, H=H, s1=s1, hft=hft: e.scalar_tensor_tensor(out=hft[:], in0=H[:, 0:128], scalar=s1[:, 0:1], in1=hft[:], op0=ALU.mult, op1=ALU.add),
                                 reads=[("H", oi % 2), ("sc1", oi % 2), ("hf", oi % 2)], writes=[("hf", oi % 2)])
                            hnt = hn[oi % 2]
                            P.op("act", lambda e, hft=hft, hnt=hnt, s1=s1: e.activation(out=hnt[:], in_=hft[:], func=AF.Square, accum_out=s1[:, 1:2]), reads=[("hf", oi % 2)], writes=[("hn", oi % 2), ("sc1", oi % 2)])
                            P.op("act", lambda e, s1=s1: e.activation(out=s1[:, 1:2], in_=s1[:, 1:2], func=AF.Sqrt, scale=1.0 / 128, bias=epsb[:, 0:1]), reads=[("sc1", oi % 2), "epsb"], writes=[("sc1", oi % 2)])
                            P.op("dve", lambda e, s1=s1: e.reciprocal(out=s1[:, 1:2], in_=s1[:, 1:2]), reads=[("sc1", oi % 2)], writes=[("sc1", oi % 2)])
                            P.op("dve", lambda e, hft=hft, hnt=hnt, s1=s1: e.tensor_scalar(out=hnt[:], in0=hft[:], scalar1=s1[:, 1:2], scalar2=None, op0=ALU.mult), reads=[("hf", oi % 2), ("sc1", oi % 2), ("hn", oi % 2)], writes=[("hn", oi % 2)])
                            P.op("pe", lambda e, hnt=hnt: e.transpose(out=ps[7][:, :128], in_=hnt[:], identity=ident[:]), reads=[("hn", oi % 2)], writes=[("ps", 7)])
                            so = sgo[oi % 2]
                            P.op("sp", lambda e, so=so, t0=t0: e.dma_start(out=so[:], in_=self.pTc[3072 + h * 128:3072 + (h + 1) * 128, t0:t0 + 128]), writes=[("sgo", oi % 2)])
                            P.op("act", lambda e, so=so: e.activation(out=so[:], in_=so[:], func=AF.Sigmoid), reads=[("sgo", oi % 2)], writes=[("sgo", oi % 2)])
                            P.op("dve", lambda e, so=so, t0=t0: e.scalar_tensor_tensor(out=obC[:, t0:t0 + 128], in0=ps[7][:, :128], scalar=mng[:, h:h + 1], in1=so[:], op0=ALU.mult, op1=ALU.mult),
                                 reads=[("ps", 7), ("sgo", oi % 2), "mng"], writes=[("obC", ti)])
                for dr_ in range(2):
                    do_dir(dr_)
                t0o = 0 if not last else c.CTX
                P.op("sp", lambda e, h=h, t0o=t0o: e.dma_start(out=self.oT[2048 + h * 128:2048 + (h + 1) * 128, t0o:], in_=obC[:, t0o:]),
                     reads=[("obC", ti) for ti in range(t0o // 128, ntile)], writes=[("oT", "C", h)])
            for h_ in range(8):
                do_head(h_)
            P.emit()


def host_inputs(cfg, inp, b):
    c = cfg
    f = np.float32

    def fm(v):
        return np.ascontiguousarray(v.reshape(c.KC, 128).T)
    cvec = np.stack([fm(inp["c"][b]), fm(inp["c_ctx"])], axis=-1)
    m = {
        "x": np.ascontiguousarray(inp["x"][b]),
        "ctx": np.ascontiguousarray(inp["ctx"][b]),
        "cvec": np.ascontiguousarray(cvec, dtype=f),
        "w_mod": inp["w_mod"],
        "b_modT": np.ascontiguousarray(inp["b_mod"].reshape(c.L, 9 * c.KC, 128).transpose(0, 2, 1)),
        "norm_gT": np.ascontiguousarray(inp["norm_g"].reshape(c.L, 3, c.KC, 128).transpose(0, 3, 1, 2)),
        "ffn_w1": inp["ffn_w1"], "ffn_w3": inp["ffn_w3"], "ffn_w2": inp["ffn_w2"],
        "ident": np.eye(128, dtype=f),
        "w_in": inp["w_in"], "w_gate": inp["w_gate"], "w_branch": inp["w_branch"], "w_out": inp["w_out"],
        "b_gateT": np.ascontiguousarray(inp["b_gate"].reshape(c.L, 3, c.KC, 128).transpose(0, 3, 1, 2)),
    }
    mc = mixer_consts(c)
    m["cosT"], m["sinT"], m["RmD"] = mc["cosT"], mc["sinT"], mc["Rm"]
    m["MAD"] = np.ascontiguousarray(mc["MA"].transpose(1, 0, 2))
    m["MCD"] = np.ascontiguousarray(mc["MC"].transpose(1, 0, 2))
    m["mnegBD"] = np.ascontiguousarray(mc["mnegB"].transpose(1, 0, 2))
    rp = inp["na_relpos"]
    rb = np.stack([rp[:, :, dr, dc] for (_, dr, dc) in mc["case_list"]], axis=2)
    m["rbD"] = np.ascontiguousarray(rb.transpose(0, 1, 3, 2, 4), dtype=f)
    m["qk_gT"] = np.ascontiguousarray(inp["qk_g"].transpose(0, 2, 1))
    m["mresD"], m["negfD"], m["negbD"] = mc["mres"][:, :512].copy(), mc["negf"][:, :512].copy(), mc["negb"][:, :512].copy()
    m["conv_wT"] = np.ascontiguousarray(inp["mlstm_conv_w"].reshape(c.L, 5, 16, 128).transpose(0, 3, 2, 1))
    m["conv_bT"] = np.ascontiguousarray(inp["mlstm_conv_b"].reshape(c.L, 16, 128).transpose(0, 2, 1))
    m["gate_bB"] = np.ascontiguousarray(np.broadcast_to(inp["mlstm_gate_b"].reshape(c.L, 1, 32), (c.L, 128, 32)))
    m["mngT"] = np.ascontiguousarray(inp["mlstm_norm_g"].reshape(c.L, 8, 128).transpose(0, 2, 1))
    m["sinkB"] = np.ascontiguousarray(np.broadcast_to(inp["attn_sink"][:, None, :], (c.L, 128, 8)))
    return m


def mixer_consts(cfg):
    c = cfg
    f = np.float32
    NT = c.NTOK
    d = np.arange(128)
    axis = d // 64
    half = (d % 64) // 32
    p = d % 32
    inv = (10000.0 ** (-np.arange(32, dtype=np.float32) / 32)).astype(np.float32)
    cosT = np.ones((128, NT), f)
    sinT = np.zeros((128, NT), f)
    t = np.arange(c.SEQ)
    row = (t // GRID_W).astype(np.float32)
    col = (t % GRID_W).astype(np.float32)
    pos = np.where(axis[:, None] == 0, row[None, :], col[None, :]).astype(np.float32)
    ang = pos * inv[p][:, None]
    cosT[:, c.CTX:] = np.cos(ang)
    sinT[:, c.CTX:] = np.sin(ang)
    Rm = np.zeros((128, 128), f)
    for m in range(128):
        if half[m] == 0:
            Rm[m + 32, m] = -1.0
        else:
            Rm[m - 32, m] = 1.0
    NEG = -30000.0
    j = np.arange(128)[:, None]
    i = np.arange(128)[None, :]
    MA = np.stack([np.where(j >= i, 0.0, NEG), np.where(j <= i, 0.0, NEG)], 0).astype(f)
    same = (j // 64) == (i // 64)
    MC = np.stack([np.where(same & (j <= i), 0.0, NEG), np.where(same & (j >= i), 0.0, NEG)], 0).astype(f)
    rows = c.ROWS
    cases = {}
    case_list = []
    sched = []
    for b in range(rows // 2):
        lst = []
        rq = 2 * b + (np.arange(128) // 64)
        cq = np.arange(128) % 64
        r0 = np.clip(rq - 4, 0, rows - 8)
        cs = np.clip(cq - 8, 0, GRID_W - 16)
        for a in range(rows // 2):
            rk = 2 * a + (np.arange(128) // 64)
            ck = np.arange(128) % 64
            ok = (rk[:, None] >= r0[None, :]) & (rk[:, None] < r0[None, :] + 8) & (ck[:, None] >= cs[None, :]) & (ck[:, None] < cs[None, :] + 16)
            if not ok.any():
                continue
            dr = np.clip(rk[:, None] - rq[None, :] + 7, 0, 14)
            dc = np.clip(ck[:, None] - cq[None, :], -15, 15) + 15
            key = (ok.tobytes(), dr.tobytes(), dc.tobytes())
            if key not in cases:
                cases[key] = len(case_list)
                case_list.append((ok, dr, dc))
            lst.append((a, cases[key]))
        sched.append(lst)
    mnegB = np.stack([np.where(ok, 0.0, NEG) for ok, _, _ in case_list], 0).astype(f)
    tt = np.arange(NT)
    mres = np.where(tt % 64 == 0, 0.0, 1.0).astype(f)
    negf = np.where(tt % 64 == 0, -1e30, 0.0).astype(f)
    negb = np.where(tt % 64 == 63, -1e30, 0.0).astype(f)
    return dict(cosT=cosT, sinT=sinT, Rm=Rm, MA=MA, MC=MC, mnegB=mnegB, case_list=case_list, sched=sched,
                mres=np.broadcast_to(mres, (128, NT)).copy(), negf=np.broadcast_to(negf, (128, NT)).copy(), negb=np.broadcast_to(negb, (128, NT)).copy())


def kernel(**inp):
    from concourse.bass_utils import run_bass_kernel_spmd
    inp = {k: np.asarray(v) for k, v in inp.items()}
    cfg = FULL
    kb = K(cfg)
    nc = kb.build()
    maps = [host_inputs(cfg, inp, b) for b in range(2)]
    res = run_bass_kernel_spmd(nc, maps, core_ids=[0, 1])
    return np.stack([r["out"] for r in res.results]).astype(np.float32)
```

```python
import numpy as np
import concourse.bass as bass
import concourse.mybir as mybir

F32 = mybir.dt.float32
BF16 = mybir.dt.bfloat16
AF = mybir.ActivationFunctionType
ALU = mybir.AluOpType
AX = mybir.AxisListType

COMPUTE = ("pe", "act", "dve", "pool")
DMAQ = ("sp", "gq")


class Op:
    __slots__ = ("eng", "fn", "reads", "writes", "idx", "sig", "deps", "dma_n")

    def __init__(self, eng, fn, reads, writes):
        self.eng = eng
        self.fn = fn
        self.reads = reads
        self.writes = writes
        self.sig = None
        self.deps = ()
        self.dma_n = None


class SemPool:
    def __init__(self, nc, es, ring=20, ngq=16):
        self.sems = {e: es.enter_context(nc.semaphore(f"sem_{e}")) for e in COMPUTE}
        self.dsp = [es.enter_context(nc.semaphore(f"dsem_sp_{i}")) for i in range(ring)]
        self.gq = [es.enter_context(nc.semaphore(f"gsem_{i}")) for i in range(ngq)]


class Prog:
    _uid = [0]

    def __init__(self, nc, pool):
        self.nc = nc
        self.ops = []
        self.pool = pool
        self.ring = len(pool.dsp)

    def op(self, eng, fn, reads=(), writes=()):
        o = Op(eng, fn, tuple(reads), tuple(writes))
        o.idx = len(self.ops)
        self.ops.append(o)
        return o

    @staticmethod
    def stream(eng):
        return "pool" if eng == "gq" else eng

    def analyze(self):
        last_w = {}
        readers = {}
        need_sig = set()
        for o in self.ops:
            deps = set()
            for k in o.reads:
                w = last_w.get(k)
                if w is not None:
                    deps.add(w)
            for k in o.writes:
                w = last_w.get(k)
                if w is not None:
                    deps.add(w)
                r = readers.get(k)
                if r:
                    for v in r.values():
                        if isinstance(v, list):
                            deps.update(v)
                        else:
                            deps.add(v)
            deps.discard(o.idx)
            fin = []
            for d in deps:
                p = self.ops[d]
                if p.eng == "pe" and o.eng == "pe":
                    continue
                fin.append(d)
                need_sig.add(d)
            o.deps = fin
            for k in o.writes:
                last_w[k] = o.idx
                readers[k] = {}
            for k in o.reads:
                r = readers.setdefault(k, {})
                if o.eng in DMAQ:
                    r.setdefault(o.eng, []).append(o.idx)
                    if len(r[o.eng]) > 64:
                        r[o.eng] = r[o.eng][-64:]
                else:
                    r[o.eng] = o.idx
        cnt = {e: 0 for e in COMPUTE}
        dcnt = {e: 0 for e in DMAQ}
        self.gqgen = {}
        for o in self.ops:
            if o.eng in DMAQ:
                o.dma_n = dcnt[o.eng]
                dcnt[o.eng] += 1
            elif o.idx in need_sig:
                cnt[o.eng] += 1
                o.sig = cnt[o.eng]
        self.nsig = cnt
        self.ndma = dcnt

    def emit(self, final_wait_ops=()):
        nc = self.nc
        self.analyze()
        from contextlib import ExitStack
        K = self.ring
        with ExitStack() as es:
            sp_ = self.pool
            sems = sp_.sems
            dsem = {"sp": sp_.dsp}
            dsem["gq"] = sp_.gq
            KQ = {"sp": len(sp_.dsp), "gq": len(sp_.gq)}
            es2 = ExitStack()
            block = es2.enter_context(nc.Block())
            engobj = {"pe": "tensor", "act": "scalar", "dve": "vector", "pool": "gpsimd", "sp": "sync"}
            streams = {s: [] for s in engobj}
            for o in self.ops:
                streams[self.stream(o.eng)].append(o)
            ops = self.ops

            def dma_target(o):
                kq = KQ[o.eng]
                return dsem[o.eng][o.dma_n % kq], 16 * (o.dma_n // kq + 1)

            def run_stream(sname, eng):
                seen = {}

                def wait(sem, val):
                    key = id(sem)
                    if isinstance(val, tuple):
                        if seen.get(key, 0) >= val[1]:
                            return
                        eng.wait_ge(sem, 16)
                        seen[key] = val[1]
                        return
                    if seen.get(key, 0) >= val:
                        return
                    eng.wait_ge(sem, val)
                    seen[key] = val

                for o in streams[sname]:
                    cw = {}
                    for d in o.deps:
                        p = ops[d]
                        if p.eng in DMAQ:
                            s, v = dma_target(p)
                            wait(s, v)
                        else:
                            cw[p.eng] = max(cw.get(p.eng, 0), p.sig)
                    for e, v in cw.items():
                        wait(sems[e], v)
                    if o.eng in DMAQ and o.dma_n >= KQ[o.eng]:
                        kq = KQ[o.eng]
                        wait(dsem[o.eng][o.dma_n % kq], 16 * (o.dma_n // kq))
                    ins = o.fn(eng)
                    if o.eng in DMAQ:
                        s, v = dma_target(o)
                        ins.then_inc(s, 16)
                    elif o.sig is not None:
                        ins.then_inc(sems[o.eng], 1)
                if sname == "sp":
                    lst = [o for o in ops if o.eng == "sp"][-KQ["sp"]:] + [o for o in ops if o.eng == "gq"][-KQ["gq"]:]
                    for o in lst:
                        s, v = dma_target(o)
                        wait(s, v)

            for sname, attr in engobj.items():
                if not streams[sname] and sname != "sp":
                    continue
                dec = getattr(block, attr)

                def body(eng, sname=sname):
                    run_stream(sname, eng)
                dec(body)
            es2.close()
            allsems = list(sems.values()) + dsem["sp"] + dsem["gq"]
            with nc.Block() as b2:
                def clr(eng):
                    for sm in allsems:
                        eng.sem_clear(sm)
                b2.sync(clr)


import os
from contextlib import ExitStack

HD = 128
NH = 8
GRID_W = 64
EPS = 1e-6


class Cfg:
    def __init__(self, D, SEQ, CTX, DFF, L):
        self.D, self.SEQ, self.CTX, self.DFF, self.L = D, SEQ, CTX, DFF, L
        self.KC = D // 128
        self.JC = DFF // 128
        self.NTOK = CTX + SEQ
        self.ROWS = SEQ // GRID_W
        self.INW = 8 * 128 + 2 * 128 * 2 + 3 * 8 * 128 + 4 * 8 * 128 + 32
        self.tiles = []
        s = 0
        while s < CTX:
            t = min(512, CTX - s)
            self.tiles.append((s, t, True))
            s += t
        while s < self.NTOK:
            t = min(512, self.NTOK - s)
            self.tiles.append((s, t, False))
            s += t


FULL = Cfg(4096, 8192, 256, 7168, 2)


class K:
    def __init__(self, cfg, stop_after=None, dbg=False):
        self.cfg = cfg
        self.stop_after = stop_after
        nc = bass.Bass("TRN2", target_bir_lowering=False)
        self.nc = nc
        c = cfg
        L = c.L

        def din(name, shape, dt=F32):
            return nc.dram_tensor(name, list(shape), dt, kind="ExternalInput").ap()

        self.x = din("x", [c.SEQ, c.D])
        self.ctx = din("ctx", [c.CTX, c.D])
        self.cvec = din("cvec", [128, c.KC, 2])
        self.w_mod = din("w_mod", [L, c.D, 9 * c.D])
        self.b_modT = din("b_modT", [L, 128, 9 * c.KC])
        self.norm_gT = din("norm_gT", [L, 128, 3, c.KC])
        self.ffn_w1 = din("ffn_w1", [L, 2, c.D, c.DFF])
        self.ffn_w3 = din("ffn_w3", [L, 2, c.D, c.DFF])
        self.ffn_w2 = din("ffn_w2", [L, 2, c.DFF, c.D])
        self.ident = din("ident", [128, 128])
        self.w_in = din("w_in", [L, c.D, c.INW])
        self.w_gate = din("w_gate", [L, 3, c.D, c.D])
        self.w_branch = din("w_branch", [L, 3, 1024, c.D])
        self.w_out = din("w_out", [L, c.D, c.D])
        self.b_gateT = din("b_gateT", [L, 128, 3, c.KC])
        self.mc = mixer_consts(c)
        ncase = len(self.mc["case_list"])
        self.cosT = din("cosT", [128, c.NTOK])
        self.sinT = din("sinT", [128, c.NTOK])
        self.RmD = din("RmD", [128, 128])
        self.MAD = din("MAD", [128, 2, 128])
        self.MCD = din("MCD", [128, 2, 128])
        self.mnegBD = din("mnegBD", [128, ncase, 128])
        self.rbD = din("rbD", [L, 8, 128, ncase, 128])
        self.qk_gT = din("qk_gT", [L, 128, 4])
        self.sinkB = din("sinkB", [L, 128, 8])
        self.mresD = din("mresD", [128, 512])
        self.negfD = din("negfD", [128, 512])
        self.negbD = din("negbD", [128, 512])
        self.conv_wT = din("conv_wT", [L, 128, 16, 5])
        self.conv_bT = din("conv_bT", [L, 128, 16])
        self.gate_bB = din("gate_bB", [L, 128, 32])
        self.mngT = din("mngT", [L, 128, 8])
        self.hscr = nc.dram_tensor("hscr", [c.NTOK, 128], F32, kind="Internal").ap()
        skind = "ExternalOutput" if dbg else "Internal"
        self.NA = 36 * 128
        self.NC = c.INW - self.NA
        self.pTa = nc.dram_tensor("pTa", [self.NA, c.NTOK], F32, kind=skind).ap()
        self.pTc = nc.dram_tensor("pTc", [self.NC, c.NTOK], F32, kind=skind).ap()
        self.oT = nc.dram_tensor("oT", [3 * 1024, c.NTOK], BF16, kind=skind).ap()
        self.pc1 = min(256, 8192 // c.KC)
        self.pc2 = max(128, min(512, (8192 // c.JC) // 128 * 128))
        self.bf = {}

        def reg(name, key, src2d, pc):
            Kr, N = src2d.shape
            kch = Kr // 128
            npieces = (N + pc - 1) // pc
            per = max(1, (200 * 1024 * 1024) // (128 * kch * pc * 2))
            tens = []
            for t0 in range(0, npieces, per):
                n = min(per, npieces - t0)
                tens.append(nc.dram_tensor(f"{name}_bf_{'_'.join(map(str, key))}_{t0}", [n, 128, kch * pc], BF16, kind="Internal").ap())
            self.bf[(name,) + tuple(key)] = dict(tens=tens, per=per, pc=pc, kch=kch, N=N, src=src2d, npieces=npieces)

        for l in range(L):
            reg("w_mod", (l,), self.w_mod[l], 256)
            for w in range(2):
                reg("ffn_w1", (l, w), self.ffn_w1[l, w], self.pc1)
                reg("ffn_w3", (l, w), self.ffn_w3[l, w], self.pc1)
                reg("ffn_w2", (l, w), self.ffn_w2[l, w], self.pc2)
            reg("w_in", (l,), self.w_in[l], self.pc1)
            for b in range(3):
                reg("w_gate", (l, b), self.w_gate[l, b], self.pc1)
                reg("w_branch", (l, b), self.w_branch[l, b], self.pc1)
            reg("w_out", (l,), self.w_out[l], self.pc1)

        self.out = nc.dram_tensor("out", [c.SEQ, c.D], F32, kind="ExternalOutput").ap()
        self.xT = nc.dram_tensor("xT", [c.D, c.NTOK], F32, kind="Internal").ap()

    def piece(self, wkey, i):
        d = self.bf[wkey]
        t = d["tens"][i // d["per"]]
        ncols = min(d["pc"], d["N"] - i * d["pc"])
        return t[i % d["per"]], ncols, d["kch"], d["pc"]

    def wload_piece(self, P, wkey, i, buf, bkey):
        ap, ncols, kch, pc = self.piece(wkey, i)
        tot = kch * pc
        keys = []
        if ncols < pc:
            bv = buf[:, 0:tot].rearrange("p (k n) -> p k n", n=pc)
            av = ap.rearrange("p (k n) -> p k n", n=pc)
            for k8 in range(0, kch, 8):
                k9 = min(kch, k8 + 8)
                P.op("sp", lambda e, k8=k8, k9=k9: e.dma_start(out=bv[:, k8:k9, 0:ncols], in_=av[:, k8:k9, 0:ncols]), writes=[bkey + (k8,)])
                keys.append(bkey + (k8,))
            return bv, keys
        nsp = 2 if tot >= 2048 else 1
        step = tot // nsp
        for q in range(nsp):
            P.op("sp", lambda e, q=q: e.dma_start(out=buf[:, q * step:(q + 1) * step], in_=ap[:, q * step:(q + 1) * step]), writes=[bkey + (q,)])
            keys.append(bkey + (q,))
        return buf[:, 0:tot].rearrange("p (k n) -> p k n", n=pc), keys

    _uid = [0]

    def sb(self, es, name, shape, dt=F32):
        K._uid[0] += 1
        return es.enter_context(self.nc.sbuf_tensor(f"{name}_{K._uid[0]}", list(shape), dt))

    def psum_banks(self, es, n=8):
        K._uid[0] += 1
        return [es.enter_context(self.nc.psum_tensor(f"ps{i}_{K._uid[0]}", [128, 512], F32)) for i in range(n)]

    def build(self):
        c = self.cfg
        with ExitStack() as es:
            self.pool = SemPool(self.nc, es)
            self.identS = self.sb(es, "identS", [128, 128])
            self.onesF = self.sb(es, "onesF", [128, 128])
            self.modS = self.sb(es, "modS", [128, 9 * c.KC, 2])
            self.GS = self.sb(es, "GS", [128, 2, 3, c.KC])
            self.SH = self.sb(es, "SH", [128, 2, 3, c.KC])
            self.GT = self.sb(es, "GT", [128, 2, 3, c.KC])
            import os
            if not os.environ.get("SKIP_CAST"):
                self.phase_cast()
            self.phase_in()
            for l in range(c.L):
                if os.environ.get("SKIP_MOD"):
                    break
                self.phase_mod(l)
                if self.stop_after == ("mod", l):
                    break
                self.phase_ffn(l, 0)
                if self.stop_after == ("ffn1", l):
                    break
                if os.environ.get("FFN2"):
                    self.phase_ffn(l, 1)
                    continue
                last = (l == c.L - 1)
                self.phase_proj(l)
                if self.stop_after == ("proj", l):
                    break
                self.phase_mix(l, last)
                if self.stop_after == ("mix", l):
                    break
                self.phase_merge(l, last)
                if self.stop_after == ("merge", l):
                    break
                self.phase_ffn(l, 1, skip_ctx=last)
            self.phase_out()
        return self.nc

    def phase_cast(self):
        nc = self.nc
        jobs = []
        for wkey, d in self.bf.items():
            pc, kch = d["pc"], d["kch"]
            srcv = d["src"].rearrange("(kc p) n -> p kc n", p=128)
            for i in range(d["npieces"]):
                ap, ncols, _, _ = self.piece(wkey, i)
                dv = ap.rearrange("p (k n) -> p k n", n=pc)
                for k8 in range(0, kch, 8):
                    k9 = min(kch, k8 + 8)
                    jobs.append((dv[:, k8:k9, 0:ncols], srcv[:, k8:k9, i * pc:i * pc + ncols]))
        P = Prog(nc, self.pool)
        for i, (d, s_) in enumerate(jobs):
            P.op("gq", lambda e, d=d, s_=s_: e.dma_start(out=d, in_=s_), writes=[("cast", i)])
        P.emit()

    def phase_in(self):
        c, nc = self.cfg, self.nc
        with ExitStack() as es:
            P = Prog(nc, self.pool)
            xtok = [self.sb(es, f"xtok{i}", [128, c.D]) for i in range(2)]
            stg = [self.sb(es, f"stg{i}", [128, c.KC, 128]) for i in range(2)]
            ps = self.psum_banks(es)
            P.op("sp", lambda e: e.dma_start(out=self.identS[:], in_=self.ident[:, :]), writes=["ident"])
            P.op("dve", lambda e: e.memset(self.onesF[:], 1.0), writes=["ones"])
            nsub = c.NTOK // 128
            xTv = self.xT.rearrange("(kc p) t -> p kc t", p=128)
            for s in range(nsub):
                t0 = s * 128
                src = self.ctx[t0:t0 + 128, :] if t0 < c.CTX else self.x[t0 - c.CTX:t0 - c.CTX + 128, :]
                xb = xtok[s % 2]
                sg = stg[s % 2]
                P.op("sp", lambda e, xb=xb, src=src: e.dma_start(out=xb[:], in_=src), writes=[("xtok", s % 2)])
                for kc in range(c.KC):
                    bank = ps[(kc // 4) % 8]
                    sl = bank[:, (kc % 4) * 128:(kc % 4) * 128 + 128]
                    P.op("pe", lambda e, sl=sl, xb=xb, kc=kc: e.transpose(out=sl, in_=xb[:, kc * 128:(kc + 1) * 128], identity=self.identS[:]),
                         reads=[("xtok", s % 2), "ident"], writes=[("psb", (kc // 4) % 8, kc % 4)])
                    if kc % 4 == 3 or kc == c.KC - 1:
                        k0 = (kc // 4) * 4
                        n = kc - k0 + 1
                        eng = "act" if (kc // 4) % 2 == 0 else "dve"
                        if eng == "act":
                            fn = lambda e, sg=sg, k0=k0, n=n, bank=bank: e.activation(out=sg[:, k0:k0 + n, :], in_=bank[:, 0:n * 128].rearrange("p (a b) -> p a b", b=128), func=AF.Copy)
                        else:
                            fn = lambda e, sg=sg, k0=k0, n=n, bank=bank: e.tensor_copy(out=sg[:, k0:k0 + n, :], in_=bank[:, 0:n * 128].rearrange("p (a b) -> p a b", b=128))
                        P.op(eng, fn, reads=[("psb", (kc // 4) % 8, q) for q in range(n)], writes=[("stg", s % 2, k0)])
                for k8 in range(0, c.KC, 8):
                    k9 = min(c.KC, k8 + 8)
                    P.op("sp", lambda e, sg=sg, t0=t0, k8=k8, k9=k9: e.dma_start(out=xTv[:, k8:k9, t0:t0 + 128], in_=sg[:, k8:k9, :]),
                         reads=[("stg", s % 2, k0) for k0 in range(k8, k9, 4)], writes=[("xT", s, k8)])
            P.emit()

    def phase_out(self):
        c, nc = self.cfg, self.nc
        with ExitStack() as es:
            P = Prog(nc, self.pool)
            xtok = [self.sb(es, f"oxtok{i}", [128, c.D]) for i in range(2)]
            stg = [self.sb(es, f"ostg{i}", [128, c.KC, 128]) for i in range(2)]
            ps = self.psum_banks(es)
            xTv = self.xT.rearrange("(kc p) t -> p kc t", p=128)
            nsub = c.SEQ // 128
            for s in range(nsub):
                t0 = c.CTX + s * 128
                xb = xtok[s % 2]
                sg = stg[s % 2]
                for k8 in range(0, c.KC, 8):
                    k9 = min(c.KC, k8 + 8)
                    P.op("sp", lambda e, sg=sg, t0=t0, k8=k8, k9=k9: e.dma_start(out=sg[:, k8:k9, :], in_=xTv[:, k8:k9, t0:t0 + 128]), writes=[("stg", s % 2, k8)])
                for kc in range(c.KC):
                    bank = ps[(kc // 4) % 8]
                    sl = bank[:, (kc % 4) * 128:(kc % 4) * 128 + 128]
                    P.op("pe", lambda e, sl=sl, sg=sg, kc=kc: e.transpose(out=sl, in_=sg[:, kc, :], identity=self.identS[:]),
                         reads=[("stg", s % 2, (kc // 8) * 8)], writes=[("psb", (kc // 4) % 8, kc % 4)])
                    if kc % 4 == 3 or kc == c.KC - 1:
                        k0 = (kc // 4) * 4
                        n = kc - k0 + 1
                        eng = "act" if (kc // 4) % 2 == 0 else "dve"
                        if eng == "act":
                            fn = lambda e, xb=xb, k0=k0, n=n, bank=bank: e.activation(out=xb[:, k0 * 128:(k0 + n) * 128], in_=bank[:, 0:n * 128], func=AF.Copy)
                        else:
                            fn = lambda e, xb=xb, k0=k0, n=n, bank=bank: e.tensor_copy(out=xb[:, k0 * 128:(k0 + n) * 128], in_=bank[:, 0:n * 128])
                        P.op(eng, fn, reads=[("psb", (kc // 4) % 8, q) for q in range(n)], writes=[("xtok", s % 2, k0)])
                P.op("sp", lambda e, xb=xb, s=s: e.dma_start(out=self.out[s * 128:(s + 1) * 128, :], in_=xb[:]),
                     reads=[("xtok", s % 2, k0) for k0 in range(0, c.KC, 4)], writes=[("out", s)])
            P.emit()

    def phase_mod(self, l):
        c, nc = self.cfg, self.nc
        NF = 9 * c.KC
        PC = 256
        with ExitStack() as es:
            P = Prog(nc, self.pool)
            ps = self.psum_banks(es)
            cv = self.sb(es, "cv", [128, c.KC, 2])
            sg = self.sb(es, "sg", [128, c.KC, 2])
            scb = self.sb(es, "scb", [128, c.KC, 2], BF16)
            bm = self.sb(es, "bm", [128, NF])
            ng = self.sb(es, "ng", [128, 3, c.KC])
            wbf = [self.sb(es, f"wm{i}", [128, c.KC * PC], BF16) for i in range(3)]
            P.op("sp", lambda e: e.dma_start(out=cv[:], in_=self.cvec[:, :, :]), writes=["cv"])
            P.op("sp", lambda e: e.dma_start(out=bm[:], in_=self.b_modT[l, :, :]), writes=["bm"])
            P.op("sp", lambda e: e.dma_start(out=ng[:], in_=self.norm_gT[l, :, :, :]), writes=["ng"])
            P.op("act", lambda e: e.activation(out=sg[:], in_=cv[:], func=AF.Sigmoid), reads=["cv"], writes=["sg"])
            P.op("dve", lambda e: e.tensor_tensor(out=scb[:], in0=cv[:], in1=sg[:], op=ALU.mult), reads=["cv", "sg"], writes=["scb"])
            npieces = (9 * c.D) // PC
            for pi in range(npieces):
                w, wmkeys = self.wload_piece(P, ("w_mod", l), pi, wbf[pi % 3], ("wm", pi % 3))
                for q in range(PC // 128):
                    f = pi * (PC // 128) + q
                    bank = ps[(f * 2) // 512]
                    o = (f * 2) % 512
                    for kc in range(c.KC):
                        P.op("pe", lambda e, w=w, q=q, kc=kc, bank=bank, o=o: e.matmul(bank[:, o:o + 2], lhsT=w[:, kc, q * 128:(q + 1) * 128], rhs=scb[:, kc, :], start=(kc == 0), stop=(kc == c.KC - 1)),
                             reads=wmkeys + ["scb"], writes=[("macc", (f * 2) // 512)])
            nb = (NF * 2 + 511) // 512
            for b in range(nb):
                f0 = b * 256
                f1 = min(NF, f0 + 256)
                P.op("dve", lambda e, b=b, f0=f0, f1=f1: e.tensor_tensor(out=self.modS[:, f0:f1, :], in0=ps[b][:, 0:(f1 - f0) * 2].rearrange("p (f t) -> p f t", t=2),
                                                                     in1=bm[:, f0:f1].unsqueeze(2).to_broadcast([128, f1 - f0, 2]), op=ALU.add),
                     reads=[("macc", b), "bm"], writes=["modS"])
            KC = c.KC
            for t in range(2):
                for s in range(3):
                    sh = self.modS[:, (3 * s) * KC:(3 * s + 1) * KC, t]
                    sc = self.modS[:, (3 * s + 1) * KC:(3 * s + 2) * KC, t]
                    gt = self.modS[:, (3 * s + 2) * KC:(3 * s + 3) * KC, t]
                    P.op("dve", lambda e, t=t, s=s, sc=sc: e.scalar_tensor_tensor(out=self.GS[:, t, s, :], in0=sc, scalar=1.0, in1=ng[:, s, :], op0=ALU.add, op1=ALU.mult),
                         reads=["modS", "ng"], writes=["GS"])
                    P.op("dve", lambda e, t=t, s=s, sh=sh: e.tensor_copy(out=self.SH[:, t, s, :], in_=sh), reads=["modS"], writes=["SH"])
                    P.op("dve", lambda e, t=t, s=s, gt=gt: e.tensor_scalar(out=self.GT[:, t, s, :], in0=gt, scalar1=(1.0 if s == 1 else 0.5), scalar2=None, op0=ALU.mult),
                         reads=["modS"], writes=["GT"])
            P.emit()

    def norm_mod(self, P, R, tile, sub, dst):
        c = self.cfg
        t0, T, isctx = tile
        ty = 1 if isctx else 0
        xTv = self.xT.rearrange("(kc p) t -> p kc t", p=128)
        ssq = R["ps"][0]
        for kc in range(c.KC):
            xb = R["xc"][kc % 4]
            sq = R["sq"][kc % 2]
            P.op("sp", lambda e, xb=xb, kc=kc: e.dma_start(out=xb[:, :T], in_=xTv[:, kc, t0:t0 + T]), reads=[("xT", t0, kc)], writes=[("xc", kc % 4)])
            P.op("act", lambda e, xb=xb, sq=sq: e.activation(out=sq[:, :T], in_=xb[:, :T], func=AF.Square), reads=[("xc", kc % 4)], writes=[("sq", kc % 2)])
            P.op("pe", lambda e, sq=sq, kc=kc: e.matmul(ssq[:, :T], lhsT=self.onesF[:], rhs=sq[:, :T], start=(kc == 0), stop=(kc == c.KC - 1)),
                 reads=[("sq", kc % 2), "ones"], writes=[("ps", 0)])
        rs = R["rstd"]
        P.op("act", lambda e: e.activation(out=rs[:, :T], in_=ssq[:, :T], func=AF.Sqrt, scale=1.0 / c.D, bias=R["epsb"][:, 0:1]), reads=[("ps", 0), "epsb"], writes=["rstd"])
        P.op("dve", lambda e: e.reciprocal(out=rs[:, :T], in_=rs[:, :T]), reads=["rstd"], writes=["rstd"])
        for kc in range(c.KC):
            xb = R["xc"][kc % 4]
            tmp = R["tmp"][kc % 2]
            P.op("sp", lambda e, xb=xb, kc=kc: e.dma_start(out=xb[:, :T], in_=xTv[:, kc, t0:t0 + T]), reads=[("xT", t0, kc)], writes=[("xc", kc % 4)])
            P.op("dve", lambda e, xb=xb, tmp=tmp, kc=kc: e.scalar_tensor_tensor(out=tmp[:, :T], in0=xb[:, :T], scalar=self.GS[:, ty, sub, kc:kc + 1], in1=rs[:, :T], op0=ALU.mult, op1=ALU.mult),
                 reads=[("xc", kc % 4), "rstd", "GS"], writes=[("tmp", kc % 2)])
            P.op("act", lambda e, tmp=tmp, kc=kc: e.activation(out=dst[:, kc, :T], in_=tmp[:, :T], func=AF.Identity, bias=self.SH[:, ty, sub, kc:kc + 1]),
                 reads=[("tmp", kc % 2), "SH"], writes=[("xn", kc)])

    def alloc_dense(self, es):
        c = self.cfg
        R = {}
        R["ps"] = self.psum_banks(es)
        R["xc"] = [self.sb(es, f"xc{i}", [128, 512]) for i in range(4)]
        R["sq"] = [self.sb(es, f"sq{i}", [128, 512]) for i in range(2)]
        R["tmp"] = [self.sb(es, f"tmp{i}", [128, 512]) for i in range(2)]
        R["ob"] = [self.sb(es, f"ob{i}", [128, 512]) for i in range(2)]
        R["rstd"] = self.sb(es, "rstd", [128, 512])
        R["epsb"] = self.sb(es, "epsb", [128, 1])
        R["xn"] = self.sb(es, "xn", [128, c.KC, 512], BF16)
        return R

    def phase_ffn(self, l, which, skip_ctx=False):
        c, nc = self.cfg, self.nc
        sub = 0 if which == 0 else 2
        PC = 256
        with ExitStack() as es:
            P = Prog(nc, self.pool)
            R = self.alloc_dense(es)
            ps = R["ps"]
            h = self.sb(es, "h", [128, c.JC, 512], BF16)
            NW = 4
            wb = [self.sb(es, f"wb{i}", [128, 8192], BF16) for i in range(NW)]
            P.op("dve", lambda e: e.memset(R["epsb"][:], EPS), writes=["epsb"])
            xTv = self.xT.rearrange("(kc p) t -> p kc t", p=128)
            wi = [0]

            def wload(wkey, col0, pc):
                i = wi[0] % NW
                wi[0] += 1
                return self.wload_piece(P, wkey, col0 // pc, wb[i], ("wb", i))

            pcol = self.pc1
            p2col = self.pc2
            ty_of = lambda tile: 1 if tile[2] else 0
            def do_tile(tile, nxt, first):
                t0, T, isctx = tile
                ty = ty_of(tile)
                if first:
                    self.norm_mod(P, R, tile, sub, R["xn"])
                xn = R["xn"]
                xnkeys = [("xn", kc) for kc in range(c.KC)]
                nA = 0
                for j0 in range(0, c.DFF, pcol):
                    b1, k1 = wload(("ffn_w1", l, which), j0, pcol)
                    b3, k3 = wload(("ffn_w3", l, which), j0, pcol)
                    for q in range(pcol // 128):
                        j = j0 // 128 + q
                        p1 = ps[1 + nA % 2]
                        p3 = ps[3 + nA % 2]
                        tm = R["tmp"][nA % 2]
                        for kc in range(c.KC):
                            P.op("pe", lambda e, b1=b1, q=q, kc=kc, p1=p1: e.matmul(p1[:, :T], lhsT=b1[:, kc, q * 128:(q + 1) * 128], rhs=xn[:, kc, :T], start=(kc == 0), stop=(kc == c.KC - 1)),
                                 reads=k1 + xnkeys, writes=[("ps", 1 + nA % 2)])
                        for kc in range(c.KC):
                            P.op("pe", lambda e, b3=b3, q=q, kc=kc, p3=p3: e.matmul(p3[:, :T], lhsT=b3[:, kc, q * 128:(q + 1) * 128], rhs=xn[:, kc, :T], start=(kc == 0), stop=(kc == c.KC - 1)),
                                 reads=k3 + xnkeys, writes=[("ps", 3 + nA % 2)])
                        P.op("act", lambda e, tm=tm, p1=p1: e.activation(out=tm[:, :T], in_=p1[:, :T], func=AF.Silu), reads=[("ps", 1 + nA % 2)], writes=[("tmp", nA % 2)])
                        P.op("dve", lambda e, tm=tm, p3=p3, j=j: e.tensor_tensor(out=h[:, j, :T], in0=tm[:, :T], in1=p3[:, :T], op=ALU.mult),
                             reads=[("tmp", nA % 2), ("ps", 3 + nA % 2)], writes=[("h", j)])
                        nA += 1
                if nxt is not None:
                    self.norm_mod(P, R, nxt, sub, R["xn"])
                hkeys = [("h", j) for j in range(c.JC)]
                nB = 0
                d0s = list(range(0, c.D, p2col))
                pre = {}

                def fetch(ix):
                    if ix < len(d0s) and ix not in pre:
                        pre[ix] = wload(("ffn_w2", l, which), d0s[ix], p2col)
                fetch(0)
                fetch(1)
                for ix, d0 in enumerate(d0s):
                    fetch(ix + 2)
                    b2, k2 = pre.pop(ix)
                    for q in range(p2col // 128):
                        dc = d0 // 128 + q
                        pb = ps[5 + nB % 3]
                        for jc in range(c.JC):
                            P.op("pe", lambda e, b2=b2, q=q, jc=jc, pb=pb: e.matmul(pb[:, :T], lhsT=b2[:, jc, q * 128:(q + 1) * 128], rhs=h[:, jc, :T], start=(jc == 0), stop=(jc == c.JC - 1)),
                                 reads=k2 + hkeys, writes=[("ps", 5 + nB % 3)])
                        xb = R["xc"][nB % 4]
                        ob = R["ob"][nB % 2]
                        P.op("sp", lambda e, xb=xb, dc=dc: e.dma_start(out=xb[:, :T], in_=xTv[:, dc, t0:t0 + T]), reads=[("xT", t0, dc)], writes=[("xc", nB % 4)])
                        P.op("dve", lambda e, xb=xb, ob=ob, pb=pb, dc=dc: e.scalar_tensor_tensor(out=ob[:, :T], in0=pb[:, :T], scalar=self.GT[:, ty, sub, dc:dc + 1], in1=xb[:, :T], op0=ALU.mult, op1=ALU.add),
                             reads=[("ps", 5 + nB % 3), ("xc", nB % 4), "GT"], writes=[("ob", nB % 2)])
                        P.op("sp", lambda e, ob=ob, dc=dc: e.dma_start(out=xTv[:, dc, t0:t0 + T], in_=ob[:, :T]), reads=[("ob", nB % 2)], writes=[("xT", t0, dc)])
                        nB += 1

            tl = [t for t in c.tiles if not (skip_ctx and t[2])]
            for i, tile in enumerate(tl):
                do_tile(tile, tl[i + 1] if i + 1 < len(tl) else None, i == 0)
            P.emit()

    def phase_proj(self, l):
        c, nc = self.cfg, self.nc
        with ExitStack() as es:
            P = Prog(nc, self.pool)
            R = self.alloc_dense(es)
            ps = R["ps"]
            NW = 4
            wb = [self.sb(es, f"wbp{i}", [128, 8192], BF16) for i in range(NW)]
            stg = [self.sb(es, f"pst{i}", [128, 512]) for i in range(4)]
            P.op("dve", lambda e: e.memset(R["epsb"][:], EPS), writes=["epsb"])
            pcol = self.pc1
            wi = [0]

            def wload(col0, ncols):
                i = wi[0] % NW
                wi[0] += 1
                return self.wload_piece(P, ("w_in", l), col0 // pcol, wb[i], ("wb", i))

            def do_tile(tile):
                t0, T, isctx = tile
                self.norm_mod(P, R, tile, 1, R["xn"])
                xn = R["xn"]
                xnkeys = [("xn", kc) for kc in range(c.KC)]
                n = 0
                c0s = list(range(0, c.INW, pcol))
                pre = {}

                def fetch(ix):
                    if ix < len(c0s) and ix not in pre:
                        pre[ix] = wload(c0s[ix], min(pcol, c.INW - c0s[ix]))
                fetch(0)
                fetch(1)
                for ix, c0 in enumerate(c0s):
                    nco = min(pcol, c.INW - c0)
                    fetch(ix + 2)
                    bw, kw = pre.pop(ix)
                    for q0 in range(0, nco, 128):
                        m = min(128, nco - q0)
                        col = c0 + q0
                        pb = ps[1 + n % 4]
                        sg = stg[n % 4]
                        for kc in range(c.KC):
                            P.op("pe", lambda e, bw=bw, q0=q0, m=m, kc=kc, pb=pb: e.matmul(pb[:m, :T], lhsT=bw[:, kc, q0:q0 + m], rhs=xn[:, kc, :T], start=(kc == 0), stop=(kc == c.KC - 1)),
                                 reads=kw + xnkeys, writes=[("ps", 1 + n % 4)])
                        eng = "act" if n % 2 == 0 else "dve"
                        if eng == "act":
                            P.op("act", lambda e, sg=sg, pb=pb, m=m: e.activation(out=sg[:m, :T], in_=pb[:m, :T], func=AF.Copy), reads=[("ps", 1 + n % 4)], writes=[("pst", n % 4)])
                        else:
                            P.op("dve", lambda e, sg=sg, pb=pb, m=m: e.tensor_copy(out=sg[:m, :T], in_=pb[:m, :T]), reads=[("ps", 1 + n % 4)], writes=[("pst", n % 4)])
                        if col < self.NA:
                            dst = self.pTa[col:col + m, t0:t0 + T]
                        else:
                            dst = self.pTc[col - self.NA:col - self.NA + m, t0:t0 + T]
                        P.op("sp", lambda e, sg=sg, dst=dst, m=m: e.dma_start(out=dst, in_=sg[:m, :T]), reads=[("pst", n % 4)], writes=[("pT", col, t0)])
                        n += 1
            for tile in c.tiles:
                do_tile(tile)
            P.emit()

    def phase_merge(self, l, last):
        c, nc = self.cfg, self.nc
        with ExitStack() as es:
            P = Prog(nc, self.pool)
            R = self.alloc_dense(es)
            ps = R["ps"]
            NW = 3
            wb = [self.sb(es, f"wbm{i}", [128, 8192], BF16) for i in range(NW)]
            wbb = [self.sb(es, f"wbb{i}", [128, 2048], BF16) for i in range(6)]
            osb = self.sb(es, "osb", [128, 24, 512], BF16)
            ysb = self.sb(es, "ysb", [128, c.KC, 512], BF16)
            sgs = [self.sb(es, f"sgs{i}", [128, 512]) for i in range(2)]
            yac = [self.sb(es, f"yac{i}", [128, 512]) for i in range(2)]
            bg = self.sb(es, "bg", [128, 3, c.KC])
            P.op("dve", lambda e: e.memset(R["epsb"][:], EPS), writes=["epsb"])
            P.op("sp", lambda e: e.dma_start(out=bg[:], in_=self.b_gateT[l, :, :, :]), writes=["bg"])
            oTv = self.oT.rearrange("(r p) t -> p r t", p=128)
            xTv = self.xT.rearrange("(kc p) t -> p kc t", p=128)
            pcol = self.pc1
            wi = [0]
            wj = [0]

            def wload(wkey, col0, small=False):
                if small:
                    i = wj[0] % 6
                    wj[0] += 1
                    return self.wload_piece(P, wkey, col0 // pcol, wbb[i], ("wbb", i))
                i = wi[0] % NW
                wi[0] += 1
                return self.wload_piece(P, wkey, col0 // pcol, wb[i], ("wb", i))

            def do_tile(tile):
                t0, T, isctx = tile
                ty = 1 if isctx else 0
                self.norm_mod(P, R, tile, 1, R["xn"])
                xn = R["xn"]
                xnkeys = [("xn", kc) for kc in range(c.KC)]
                for r8 in range(0, 24, 8):
                    P.op("sp", lambda e, r8=r8: e.dma_start(out=osb[:, r8:r8 + 8, :T], in_=oTv[:, r8:r8 + 8, t0:t0 + T]), reads=[("oT", t0)], writes=[("osb", r8)])
                n = 0
                for d0 in range(0, c.D, pcol):
                    gw = [wload(("w_gate", l, b), d0) for b in range(3)]
                    bwl = [wload(("w_branch", l, b), d0, small=True) for b in range(3)]
                    for q in range(pcol // 128):
                        dc = d0 // 128 + q
                        ya = yac[n % 2]
                        for b in range(3):
                            pg = ps[1 + (n * 3 + b) % 3]
                            pt = ps[4 + (n * 3 + b) % 3]
                            gb, gk = gw[b]
                            bb, bk = bwl[b]
                            sg = sgs[(n * 3 + b) % 2]
                            for kc in range(c.KC):
                                P.op("pe", lambda e, gb=gb, q=q, kc=kc, pg=pg: e.matmul(pg[:, :T], lhsT=gb[:, kc, q * 128:(q + 1) * 128], rhs=xn[:, kc, :T], start=(kc == 0), stop=(kc == c.KC - 1)),
                                     reads=gk + xnkeys, writes=[("ps", 1 + (n * 3 + b) % 3)])
                            for kc in range(8):
                                P.op("pe", lambda e, bb=bb, q=q, kc=kc, pt=pt, b=b: e.matmul(pt[:, :T], lhsT=bb[:, kc, q * 128:(q + 1) * 128], rhs=osb[:, b * 8 + kc, :T], start=(kc == 0), stop=(kc == 7)),
                                     reads=bk + [("osb", b * 8)], writes=[("ps", 4 + (n * 3 + b) % 3)])
                            P.op("act", lambda e, sg=sg, pg=pg, b=b, dc=dc: e.activation(out=sg[:, :T], in_=pg[:, :T], func=AF.Sigmoid, bias=bg[:, b, dc:dc + 1]),
                                 reads=[("ps", 1 + (n * 3 + b) % 3), "bg"], writes=[("sgs", (n * 3 + b) % 2)])
                            if b == 0:
                                P.op("dve", lambda e, sg=sg, pt=pt, ya=ya: e.tensor_tensor(out=ya[:, :T], in0=sg[:, :T], in1=pt[:, :T], op=ALU.mult),
                                     reads=[("sgs", (n * 3 + b) % 2), ("ps", 4 + (n * 3 + b) % 3)], writes=[("yac", n % 2)])
                            else:
                                P.op("dve", lambda e, sg=sg, pt=pt: e.tensor_tensor(out=sg[:, :T], in0=sg[:, :T], in1=pt[:, :T], op=ALU.mult),
                                     reads=[("sgs", (n * 3 + b) % 2), ("ps", 4 + (n * 3 + b) % 3)], writes=[("sgs", (n * 3 + b) % 2)])
                                if b == 1:
                                    P.op("dve", lambda e, sg=sg, ya=ya: e.tensor_tensor(out=ya[:, :T], in0=ya[:, :T], in1=sg[:, :T], op=ALU.add),
                                         reads=[("sgs", (n * 3 + b) % 2), ("yac", n % 2)], writes=[("yac", n % 2)])
                                else:
                                    P.op("dve", lambda e, sg=sg, ya=ya, dc=dc: e.tensor_tensor(out=ysb[:, dc, :T], in0=ya[:, :T], in1=sg[:, :T], op=ALU.add),
                                         reads=[("sgs", (n * 3 + b) % 2), ("yac", n % 2)], writes=[("ysb", dc)])
                        n += 1
                ykeys = [("ysb", dc) for dc in range(c.KC)]
                nB = 0
                d0s = list(range(0, c.D, pcol))
                pre = {}

                def fetch(ix):
                    if ix < len(d0s) and ix not in pre:
                        pre[ix] = wload(("w_out", l), d0s[ix])
                fetch(0)
                for ix, d0 in enumerate(d0s):
                    fetch(ix + 1)
                    ow, ok = pre.pop(ix)
                    for q in range(pcol // 128):
                        dc = d0 // 128 + q
                        pb = ps[7]
                        for kc in range(c.KC):
                            P.op("pe", lambda e, ow=ow, q=q, kc=kc, pb=pb: e.matmul(pb[:, :T], lhsT=ow[:, kc, q * 128:(q + 1) * 128], rhs=ysb[:, kc, :T], start=(kc == 0), stop=(kc == c.KC - 1)),
                                 reads=ok + ykeys, writes=[("ps", 7)])
                        xb = R["xc"][nB % 4]
                        ob = R["ob"][nB % 2]
                        P.op("sp", lambda e, xb=xb, dc=dc: e.dma_start(out=xb[:, :T], in_=xTv[:, dc, t0:t0 + T]), reads=[("xT", t0, dc)], writes=[("xc", nB % 4)])
                        P.op("dve", lambda e, xb=xb, ob=ob, pb=pb, dc=dc: e.scalar_tensor_tensor(out=ob[:, :T], in0=pb[:, :T], scalar=self.GT[:, ty, 1, dc:dc + 1], in1=xb[:, :T], op0=ALU.mult, op1=ALU.add),
                             reads=[("ps", 7), ("xc", nB % 4), "GT"], writes=[("ob", nB % 2)])
                        P.op("sp", lambda e, ob=ob, dc=dc: e.dma_start(out=xTv[:, dc, t0:t0 + T], in_=ob[:, :T]), reads=[("ob", nB % 2)], writes=[("xT", t0, dc)])
                        nB += 1
            for tile in c.tiles:
                if last and tile[2]:
                    continue
                do_tile(tile)
            P.emit()


    def phase_mix(self, l, last):
        self.mix_attn(l, last)
        if not os.environ.get("NO_MLSTM"):
            self.mix_mlstm(l, last)

    def prep_qk(self, P, W, src, dst, gcol, rope, tagn):
        c = self.cfg
        ps = W["ps"]
        for bi, b0 in enumerate(range(0, c.NTOK, 512)):
            n = min(512, c.NTOK - b0)
            ld = W["ld"][bi % 2]
            sq = W["sq"][bi % 2]
            rs = W["rs"][bi % 2]
            xn = W["xn"][bi % 2]
            k = bi % 2
            P.op("sp", lambda e, ld=ld, b0=b0, n=n: e.dma_start(out=ld[:, :n], in_=src[:, b0:b0 + n]), writes=[("ld", k)])
            P.op("act", lambda e, ld=ld, sq=sq, n=n: e.activation(out=sq[:, :n], in_=ld[:, :n], func=AF.Square), reads=[("ld", k)], writes=[("sq", k)])
            P.op("pe", lambda e, sq=sq, n=n: e.matmul(ps[6][:, :n], lhsT=self.onesF[:], rhs=sq[:, :n], start=True, stop=True), reads=[("sq", k)], writes=[("ps", 6)])
            P.op("act", lambda e, rs=rs, n=n: e.activation(out=rs[:, :n], in_=ps[6][:, :n], func=AF.Sqrt, scale=1.0 / 128, bias=W["epsb"][:, 0:1]), reads=[("ps", 6), "epsb"], writes=[("rs", k)])
            P.op("dve", lambda e, rs=rs, n=n: e.reciprocal(out=rs[:, :n], in_=rs[:, :n]), reads=[("rs", k)], writes=[("rs", k)])
            if rope:
                cs = W["cs"][bi % 2]
                sn = W["sn"][bi % 2]
                P.op("dve", lambda e, ld=ld, rs=rs, xn=xn, n=n: e.scalar_tensor_tensor(out=xn[:, :n], in0=ld[:, :n], scalar=gcol, in1=rs[:, :n], op0=ALU.mult, op1=ALU.mult),
                     reads=[("ld", k), ("rs", k), "qkg"], writes=[("xn", k)])
                P.op("sp", lambda e, cs=cs, b0=b0, n=n: e.dma_start(out=cs[:, :n], in_=self.cosT[:, b0:b0 + n]), writes=[("cs", k)])
                P.op("sp", lambda e, sn=sn, b0=b0, n=n: e.dma_start(out=sn[:, :n], in_=self.sinT[:, b0:b0 + n]), writes=[("sn", k)])
                P.op("pe", lambda e, xn=xn, n=n: e.matmul(ps[7][:, :n], lhsT=W["Rm"][:], rhs=xn[:, :n], start=True, stop=True), reads=[("xn", k), "Rm"], writes=[("ps", 7)])
                P.op("dve", lambda e, sn=sn, n=n: e.tensor_tensor(out=sn[:, :n], in0=ps[7][:, :n], in1=sn[:, :n], op=ALU.mult), reads=[("ps", 7), ("sn", k)], writes=[("sn", k)])
                P.op("dve", lambda e, cs=cs, xn=xn, n=n: e.tensor_tensor(out=cs[:, :n], in0=xn[:, :n], in1=cs[:, :n], op=ALU.mult), reads=[("xn", k), ("cs", k)], writes=[("cs", k)])
                P.op("dve", lambda e, cs=cs, sn=sn, b0=b0, n=n: e.tensor_tensor(out=dst[:, b0:b0 + n], in0=cs[:, :n], in1=sn[:, :n], op=ALU.add), reads=[("cs", k), ("sn", k)], writes=[(tagn, bi)])
            else:
                P.op("dve", lambda e, ld=ld, rs=rs, b0=b0, n=n: e.scalar_tensor_tensor(out=dst[:, b0:b0 + n], in0=ld[:, :n], scalar=gcol, in1=rs[:, :n], op0=ALU.mult, op1=ALU.mult),
                     reads=[("ld", k), ("rs", k), "qkg"], writes=[(tagn, bi)])
        return [(tagn, bi) for bi in range((c.NTOK + 511) // 512)]

    def prep_tok(self, P, W, src, dst, tagn, scale_cols=None):
        c = self.cfg
        ps = W["ps"]
        for bi, b0 in enumerate(range(0, c.NTOK, 512)):
            n = min(512, c.NTOK - b0)
            k = bi % 2
            ld = W["ld"][k]
            P.op("sp", lambda e, ld=ld, b0=b0, n=n: e.dma_start(out=ld[:, :n], in_=src[:, b0:b0 + n]), writes=[("ld", k)])
            nt = n // 128
            for q in range(nt):
                P.op("pe", lambda e, ld=ld, q=q: e.transpose(out=ps[7][:, q * 128:(q + 1) * 128], in_=ld[:, q * 128:(q + 1) * 128], identity=self.identS[:]),
                     reads=[("ld", k)], writes=[("ps", 7)])
            P.op("act", lambda e, b0=b0, nt=nt: e.activation(out=dst[:, b0 // 128:b0 // 128 + nt, :], in_=ps[7][:, :nt * 128].rearrange("p (a b) -> p a b", b=128), func=AF.Copy),
                 reads=[("ps", 7)], writes=[(tagn, bi)])
        return [(tagn, bi) for bi in range((c.NTOK + 511) // 512)]

    def attn(self, P, W, qb, kb, vb, rkeys, sched, sinkcol, ob):
        ps = W["ps"]
        scale = 128 ** -0.5
        steps = []
        for qi, (qt, klist) in enumerate(sched):
            nk = len(klist)
            for ki, (kt, bias, bkey) in enumerate(klist):
                steps.append((qi, qt, ki, nk, kt, bias, bkey))

        def front(cnt, st):
            qi, qt, ki, nk, kt, bias, bkey = st
            pss = ps[cnt % 2]
            pt = W["pt"][cnt % 3]
            P.op("pe", lambda e: e.matmul(pss[:, :128], lhsT=kb[:, kt * 128:(kt + 1) * 128], rhs=qb[:, qt * 128:(qt + 1) * 128], start=True, stop=True),
                 reads=rkeys, writes=[("ps", cnt % 2)])
            if bias is None:
                P.op("act", lambda e: e.activation(out=pt[:], in_=pss[:, :128], func=AF.Exp, scale=scale), reads=[("ps", cnt % 2)], writes=[("pt", cnt % 3)])
            else:
                tb = W["tb"][cnt % 2]
                P.op("dve", lambda e: e.scalar_tensor_tensor(out=tb[:], in0=pss[:, :128], scalar=scale, in1=bias, op0=ALU.mult, op1=ALU.add),
                     reads=[("ps", cnt % 2), bkey], writes=[("tb", cnt % 2)])
                P.op("act", lambda e: e.activation(out=pt[:], in_=tb[:], func=AF.Exp), reads=[("tb", cnt % 2)], writes=[("pt", cnt % 3)])

        def back(cnt, st):
            qi, qt, ki, nk, kt, bias, bkey = st
            po = ps[2 + qi % 2]
            pd = ps[4 + qi % 2]
            pt = W["pt"][cnt % 3]
            P.op("pe", lambda e: e.matmul(po[:, :128], lhsT=vb[:, kt, :], rhs=pt[:], start=(ki == 0), stop=(ki == nk - 1)),
                 reads=rkeys + [("pt", cnt % 3)], writes=[("ps", 2 + qi % 2)])
            P.op("pe", lambda e: e.matmul(pd[:, :128], lhsT=W["onesB"][:], rhs=pt[:], start=(ki == 0), stop=(ki == nk - 1)),
                 reads=[("pt", cnt % 3), "onesB"], writes=[("ps", 4 + qi % 2)])
            if ki == nk - 1:
                rc = W["rc"][qi % 2]
                if sinkcol is not None:
                    P.op("dve", lambda e: e.tensor_scalar(out=rc[:], in0=pd[:, :128], scalar1=sinkcol, scalar2=None, op0=ALU.add), reads=[("ps", 4 + qi % 2), "sink"], writes=[("rc", qi % 2)])
                    P.op("dve", lambda e: e.reciprocal(out=rc[:], in_=rc[:]), reads=[("rc", qi % 2)], writes=[("rc", qi % 2)])
                else:
                    P.op("dve", lambda e: e.reciprocal(out=rc[:], in_=pd[:, :128]), reads=[("ps", 4 + qi % 2)], writes=[("rc", qi % 2)])
                P.op("dve", lambda e: e.tensor_tensor(out=ob[:, qt * 128:(qt + 1) * 128], in0=po[:, :128], in1=rc[:], op=ALU.mult),
                     reads=[("ps", 2 + qi % 2), ("rc", qi % 2)], writes=[("ob", qt)])

        n = len(steps)
        for i in range(n + 1):
            if i < n:
                front(i, steps[i])
            if i >= 1:
                back(i - 1, steps[i - 1])

    def mix_attn(self, l, last):
        c, nc = self.cfg, self.nc
        MCN = self.mc
        ncase = len(MCN["case_list"])
        NT = c.NTOK
        ntile = NT // 128
        nctx = c.CTX // 128
        nblk = c.SEQ // 128
        with ExitStack() as es:
            P = Prog(nc, self.pool)
            W = {}
            W["ps"] = self.psum_banks(es)
            for nm in ("ld", "sq", "rs", "xn", "cs", "sn"):
                W[nm] = [self.sb(es, f"{nm}{i}", [128, 512]) for i in range(2)]
            W["epsb"] = self.sb(es, "epsb", [128, 1])
            W["Rm"] = self.sb(es, "Rm", [128, 128])
            W["onesB"] = self.sb(es, "onesB", [128, 128], BF16)
            W["pt"] = [self.sb(es, f"pt{i}", [128, 128], BF16) for i in range(3)]
            W["tb"] = [self.sb(es, f"tb{i}", [128, 128]) for i in range(2)]
            W["rc"] = [self.sb(es, f"rc{i}", [128, 128]) for i in range(2)]
            MA = self.sb(es, "MA", [128, 2, 128])
            mnegB = self.sb(es, "mnegB", [128, ncase, 128])
            RBM = self.sb(es, "RBM", [128, ncase, 128])
            qkg = self.sb(es, "qkg", [128, 4])
            snk = self.sb(es, "snk", [128, 8])
            qb = self.sb(es, "qb", [128, NT], BF16)
            kb = self.sb(es, "kb", [128, NT], BF16)
            vb = self.sb(es, "vb", [128, ntile, 128], BF16)
            ob = self.sb(es, "ob", [128, NT], BF16)
            P.op("dve", lambda e: e.memset(W["epsb"][:], EPS), writes=["epsb"])
            P.op("dve", lambda e: e.memset(W["onesB"][:], 1.0), writes=["onesB"])
            P.op("sp", lambda e: e.dma_start(out=W["Rm"][:], in_=self.RmD[:, :]), writes=["Rm"])
            P.op("sp", lambda e: e.dma_start(out=MA[:], in_=self.MAD[:, :, :]), writes=["MA"])
            P.op("sp", lambda e: e.dma_start(out=mnegB[:], in_=self.mnegBD[:, :, :]), writes=["mnegB"])
            P.op("sp", lambda e: e.dma_start(out=qkg[:], in_=self.qk_gT[l, :, :]), writes=["qkg"])
            P.op("sp", lambda e: e.dma_start(out=snk[:], in_=self.sinkB[l, :, :]), writes=["snk0"])
            P.op("act", lambda e: e.activation(out=snk[:], in_=snk[:], func=AF.Exp), reads=["snk0"], writes=["sink"])
            qtiles_ctx = [] if last else list(range(nctx))
            for kv in range(2):
                kk = self.prep_qk(P, W, self.pTa[1024 + kv * 128:1024 + (kv + 1) * 128, :], kb, qkg[:, 1:2], True, "kb")
                vk = self.prep_tok(P, W, self.pTa[1280 + kv * 128:1280 + (kv + 1) * 128, :], vb, "vb")
                for g in range(4):
                    h = kv * 4 + g
                    qk = self.prep_qk(P, W, self.pTa[h * 128:(h + 1) * 128, :], qb, qkg[:, 0:1], True, "qb")
                    sched = []
                    for qt in qtiles_ctx:
                        sched.append((qt, [(kt, None, None) for kt in range(nctx)]))
                    for n in range(nblk):
                        kl = [(kt, None, None) for kt in range(nctx)]
                        if n > 0:
                            kl.append((nctx + n - 1, MA[:, 0, :], "MA"))
                        kl.append((nctx + n, None, None))
                        if n < nblk - 1:
                            kl.append((nctx + n + 1, MA[:, 1, :], "MA"))
                        sched.append((nctx + n, kl))
                    self.attn(P, W, qb, kb, vb, kk + vk + qk, sched, snk[:, h:h + 1], ob)
                    t0o = 0 if not last else c.CTX
                    P.op("sp", lambda e, h=h, t0o=t0o: e.dma_start(out=self.oT[h * 128:(h + 1) * 128, t0o:], in_=ob[:, t0o:]),
                         reads=[("ob", qt) for qt, _ in sched], writes=[("oT", "A", h)])
            for h in range(8):
                P.op("sp", lambda e, h=h: e.dma_start(out=RBM[:], in_=self.rbD[l, h, :, :, :]), writes=["RBM"])
                P.op("dve", lambda e: e.tensor_tensor(out=RBM[:], in0=RBM[:], in1=mnegB[:], op=ALU.add), reads=["RBM", "mnegB"], writes=["RBM"])
                kk = self.prep_qk(P, W, self.pTa[2560 + h * 128:2560 + (h + 1) * 128, :], kb, qkg[:, 3:4], False, "kb")
                vk = self.prep_tok(P, W, self.pTa[3584 + h * 128:3584 + (h + 1) * 128, :], vb, "vb")
                qk = self.prep_qk(P, W, self.pTa[1536 + h * 128:1536 + (h + 1) * 128, :], qb, qkg[:, 2:3], False, "qb")
                sched = []
                for qt in qtiles_ctx:
                    sched.append((qt, [(kt, None, None) for kt in range(nctx)]))
                for n in range(nblk):
                    kl = [(kt, None, None) for kt in range(nctx)]
                    for (a, cs_) in MCN["sched"][n]:
                        kl.append((nctx + a, RBM[:, cs_, :], "RBM"))
                    sched.append((nctx + n, kl))
                self.attn(P, W, qb, kb, vb, kk + vk + qk, sched, None, ob)
                t0o = 0 if not last else c.CTX
                P.op("sp", lambda e, h=h, t0o=t0o: e.dma_start(out=self.oT[1024 + h * 128:1024 + (h + 1) * 128, t0o:], in_=ob[:, t0o:]),
                     reads=[("ob", qt) for qt, _ in sched], writes=[("oT", "B", h)])
            P.emit()


    def mix_mlstm(self, l, last):
        c, nc = self.cfg, self.nc
        NT = c.NTOK
        ntile = NT // 128
        NCH = NT // 64
        ncc = c.CTX // 64
        nct = c.CTX // 128
        isq = 128 ** -0.5
        blocks = [(0, c.CTX)] if c.CTX <= 512 else [(b, min(512, c.CTX - b)) for b in range(0, c.CTX, 512)]
        blocks += [(b, min(512, NT - b)) for b in range(c.CTX, NT, 512)]
        with ExitStack() as es:
            P = Prog(nc, self.pool)
            ps = self.psum_banks(es)
            T = {}
            for nm in ("ta", "tb", "tc", "td", "te", "tf", "tg", "th"):
                T[nm] = self.sb(es, "m" + nm, [128, 516])
            epsb = self.sb(es, "epsb", [128, 1])
            lnsc = self.sb(es, "lnsc", [128, 1])
            MC = self.sb(es, "MC", [128, 2, 128])
            mres = self.sb(es, "mres", [128, 512])
            negf = self.sb(es, "negf", [128, 512])
            negb = self.sb(es, "negb", [128, 512])
            cw = self.sb(es, "cw", [128, 16, 5])
            cb = self.sb(es, "cb", [128, 16])
            gb = self.sb(es, "gb", [128, 32])
            ngb = self.sb(es, "ngb", [128, 32])
            mng = self.sb(es, "mng", [128, 8])
            CUM = self.sb(es, "CUM", [128, NT])
            CMX = self.sb(es, "CMX", [128, NT])
            qcT = self.sb(es, "qcT", [128, NT], BF16)
            kcT = self.sb(es, "kcT", [128, NT], BF16)
            ktok = self.sb(es, "ktok", [128, ntile, 128], BF16)
            vaug = self.sb(es, "vaug", [128, ntile, 130], BF16)
            obC = self.sb(es, "obC", [128, NT], BF16)
            cols = {nm: self.sb(es, nm, [128, ntile]) for nm in ("wcol", "a2col", "sicol", "emcol")}
            ch = {nm: self.sb(es, nm, [128, NCH]) for nm in ("BL", "MLOC", "MAF", "M0", "SP", "SL", "tch")}
            Cst = self.sb(es, "Cst", [128, 130])
            C0c = [self.sb(es, f"C0c{i}", [128, 130], BF16) for i in range(4)]
            ctmp = [self.sb(es, f"ctmp{i}", [128, 130]) for i in range(2)]
            kw = [self.sb(es, f"kw{i}", [128, 128], BF16) for i in range(2)]
            et = [self.sb(es, f"et{i}", [128, 128]) for i in range(2)]
            swT = [self.sb(es, f"swT{i}", [128, 128], BF16) for i in range(2)]
            Hs = [self.sb(es, f"Hs{i}", [128, 130]) for i in range(2)]
            hf = [self.sb(es, f"hf{i}", [128, 128]) for i in range(2)]
            hn = [self.sb(es, f"hn{i}", [128, 128]) for i in range(2)]
            sc1 = [self.sb(es, f"sc1{i}", [128, 2]) for i in range(2)]
            sgo = [self.sb(es, f"sgo{i}", [128, 128]) for i in range(2)]
            ident = self.identS
            hscr = self.hscr
            ldv = [self.sb(es, f"ldv{i}", [128, 512]) for i in range(2)]
            P.op("dve", lambda e: e.memset(epsb[:], EPS), writes=["epsb"])
            P.op("dve", lambda e: e.memset(lnsc[:], float(np.log(isq))), writes=["lnsc"])
            P.op("dve", lambda e: e.memset(vaug[:, :, 128:130], 1.0), writes=["vones"])
            P.op("sp", lambda e: e.dma_start(out=MC[:], in_=self.MCD[:, :, :]), writes=["MC"])
            P.op("sp", lambda e: e.dma_start(out=mres[:], in_=self.mresD[:, 0:512]), writes=["mres"])
            P.op("sp", lambda e: e.dma_start(out=negf[:], in_=self.negfD[:, 0:512]), writes=["negf"])
            P.op("sp", lambda e: e.dma_start(out=negb[:], in_=self.negbD[:, 0:512]), writes=["negb"])
            P.op("sp", lambda e: e.dma_start(out=cw[:], in_=self.conv_wT[l, :, :, :]), writes=["cw"])
            P.op("sp", lambda e: e.dma_start(out=cb[:], in_=self.conv_bT[l, :, :]), writes=["cb"])
            P.op("sp", lambda e: e.dma_start(out=gb[:], in_=self.gate_bB[l, :, :]), writes=["gb"])
            P.op("dve", lambda e: e.tensor_scalar(out=ngb[:], in0=gb[:], scalar1=-1.0, scalar2=None, op0=ALU.mult), reads=["gb"], writes=["ngb"])
            P.op("sp", lambda e: e.dma_start(out=mng[:], in_=self.mngT[l, :, :]), writes=["mng"])

            def diag(X, n, dst, t0):
                nt = n // 128
                tmp = T["th"]
                P.op("dve", lambda e: e.tensor_tensor(out=tmp[:, :n].rearrange("p (a b) -> p a b", b=128), in0=X[:, :n].rearrange("p (a b) -> p a b", b=128),
                                                      in1=ident[:].unsqueeze(1).to_broadcast([128, nt, 128]), op=ALU.mult), reads=["X", "ident"], writes=["th"])
                P.op("dve", lambda e: e.tensor_reduce(out=dst[:, t0:t0 + nt], in_=tmp[:, :n].rearrange("p (a b) -> p a b", b=128), axis=AX.X, op=ALU.add), reads=["th"], writes=["cols"])

            def do_head(h):
                def do_conv(which, dstT):
                    src = self.pTc[which * 1024 + h * 128:which * 1024 + (h + 1) * 128, :]
                    ci = which * 8 + h
                    for (b0, n) in blocks:
                        seg0, seg1 = (0, c.CTX) if b0 < c.CTX else (c.CTX, NT)
                        lo = max(seg0, b0 - 2)
                        hi = min(seg1, b0 + n + 2)
                        ld = T["ta"]
                        acc = T["tb"]
                        P.op("dve", lambda e, ld=ld: e.memset(ld[:], 0.0), writes=["ta"])
                        P.op("sp", lambda e, ld=ld, lo=lo, hi=hi, b0=b0: e.dma_start(out=ld[:, lo - (b0 - 2):hi - (b0 - 2)], in_=src[:, lo:hi]), writes=["ta"])
                        P.op("dve", lambda e, ld=ld, acc=acc, n=n, ci=ci: e.tensor_scalar(out=acc[:, :n], in0=ld[:, 0:n], scalar1=cw[:, ci, 0:1], scalar2=cb[:, ci:ci + 1], op0=ALU.mult, op1=ALU.add),
                             reads=["ta", "cw", "cb"], writes=["tb"])
                        for j in range(1, 5):
                            P.op("dve", lambda e, ld=ld, acc=acc, n=n, ci=ci, j=j: e.scalar_tensor_tensor(out=acc[:, :n], in0=ld[:, j:j + n], scalar=cw[:, ci, j:j + 1], in1=acc[:, :n], op0=ALU.mult, op1=ALU.add),
                                 reads=["ta", "tb", "cw"], writes=["tb"])
                        P.op("act", lambda e, acc=acc, b0=b0, n=n: e.activation(out=dstT[:, b0:b0 + n], in_=acc[:, :n], func=AF.Silu), reads=["tb"], writes=[("qk", which)])
                do_conv(0, qcT)
                do_conv(1, kcT)
                for ti in range(ntile):
                    kf = T["tc"]
                    P.op("dve", lambda e, ti=ti: e.tensor_copy(out=kf[:, :128], in_=kcT[:, ti * 128:(ti + 1) * 128]), reads=[("qk", 1)], writes=["tc"])
                    P.op("pe", lambda e: e.transpose(out=ps[7][:, :128], in_=kf[:, :128], identity=ident[:]), reads=["tc"], writes=[("ps", 7)])
                    P.op("act", lambda e, ti=ti: e.activation(out=ktok[:, ti, :], in_=ps[7][:, :128], func=AF.Copy), reads=[("ps", 7)], writes=["ktok"])
                W = {"ps": ps, "ld": ldv}
                self.prep_tok(P, W, self.pTc[2048 + h * 128:2048 + (h + 1) * 128, :], vaug[:, :, 0:128], "vaugw")
                P.op("dve", lambda e: e.tensor_copy(out=T["tc"][:, 0:1], in_=T["tc"][:, 0:1]), reads=[("vaugw", bi) for bi in range((NT + 511) // 512)] + ["vones"], writes=["vaug"])

                def do_dir(dr):
                    gi_row = 4096 + (2 * dr) * 8 + h
                    gf_row = 4096 + (2 * dr + 1) * 8 + h
                    gi_c = (2 * dr) * 8 + h
                    gf_c = (2 * dr + 1) * 8 + h
                    for (b0, n) in blocks:
                        nchb = n // 64
                        c0 = b0 // 64
                        LI, LF, CP, A_, TMP = T["ta"], T["tb"], T["tc"], T["td"], T["te"]
                        P.op("sp", lambda e, b0=b0, n=n: e.dma_start(out=LI[:, :n], in_=self.pTc[gi_row:gi_row + 1, b0:b0 + n].to_broadcast([128, n])), writes=["ta"])
                        P.op("sp", lambda e, b0=b0, n=n: e.dma_start(out=LF[:, :n], in_=self.pTc[gf_row:gf_row + 1, b0:b0 + n].to_broadcast([128, n])), writes=["tb"])
                        P.op("act", lambda e, n=n: e.activation(out=LI[:, :n], in_=LI[:, :n], func=AF.Identity, bias=gb[:, gi_c:gi_c + 1]), reads=["ta", "gb"], writes=["ta"])
                        P.op("act", lambda e, n=n: e.activation(out=LF[:, :n], in_=LF[:, :n], func=AF.Exp, scale=-1.0, bias=ngb[:, gf_c:gf_c + 1]), reads=["tb", "ngb"], writes=["tb"])
                        P.op("dve", lambda e, n=n: e.tensor_scalar(out=LF[:, :n], in0=LF[:, :n], scalar1=1.0, scalar2=None, op0=ALU.add), reads=["tb"], writes=["tb"])
                        P.op("act", lambda e, n=n: e.activation(out=LF[:, :n], in_=LF[:, :n], func=AF.Ln), reads=["tb"], writes=["tb"])
                        P.op("dve", lambda e, n=n: e.tensor_scalar(out=LF[:, :n], in0=LF[:, :n], scalar1=-1.0, scalar2=None, op0=ALU.mult), reads=["tb"], writes=["tb"])
                        P.op("dve", lambda e, n=n: e.tensor_tensor_scan(out=CP[:, :n], data0=mres[:, :n], data1=LF[:, :n], initial=0.0, op0=ALU.mult, op1=ALU.add), reads=["tb", "mres"], writes=["tc"])
                        P.op("dve", lambda e, n=n, c0=c0, nchb=nchb: e.tensor_copy(out=ch["BL"][:, c0:c0 + nchb], in_=CP[:, :n].rearrange("p (a b) -> p a b", b=64)[:, :, 63]), reads=["tc"], writes=["BL"])
                        blbc = lambda c0=c0, nchb=nchb: ch["BL"][:, c0:c0 + nchb].unsqueeze(2).to_broadcast([128, nchb, 64])
                        v3 = lambda X, n=n: X[:, :n].rearrange("p (a b) -> p a b", b=64)
                        cumb = CUM[:, b0:b0 + n]
                        if dr == 0:
                            P.op("dve", lambda e, n=n, cumb=cumb: e.tensor_copy(out=cumb, in_=CP[:, :n]), reads=["tc"], writes=["CUM"])
                        else:
                            P.op("dve", lambda e, n=n, blbc=blbc, v3=v3: e.tensor_tensor(out=v3(TMP), in0=blbc(), in1=v3(CP), op=ALU.subtract), reads=["tc", "BL"], writes=["te"])
                            P.op("dve", lambda e, n=n, cumb=cumb: e.tensor_tensor(out=cumb, in0=TMP[:, :n], in1=LF[:, :n], op=ALU.add), reads=["te", "tb"], writes=["CUM"])
                        P.op("dve", lambda e, n=n, cumb=cumb: e.tensor_tensor(out=TMP[:, :n], in0=LI[:, :n], in1=cumb, op=ALU.subtract), reads=["ta", "CUM"], writes=["te"])
                        P.op("dve", lambda e, n=n, blbc=blbc, v3=v3: e.tensor_tensor(out=v3(A_), in0=v3(TMP), in1=blbc(), op=ALU.add), reads=["te", "BL"], writes=["td"])
                        P.op("dve", lambda e, n=n, c0=c0, nchb=nchb, v3=v3: e.tensor_reduce(out=ch["MLOC"][:, c0:c0 + nchb], in_=v3(A_), axis=AX.X, op=ALU.max), reads=["td"], writes=["MLOC"])
                        P.op("dve", lambda e, n=n, c0=c0, nchb=nchb, v3=v3: e.tensor_tensor(out=v3(A_), in0=v3(A_), in1=ch["MLOC"][:, c0:c0 + nchb].unsqueeze(2).to_broadcast([128, nchb, 64]), op=ALU.subtract),
                             reads=["td", "MLOC"], writes=["td"])
                        P.op("act", lambda e, n=n: e.activation(out=A_[:, :n], in_=A_[:, :n], func=AF.Exp), reads=["td"], writes=["X"])
                        diag(A_, n, cols["wcol"], b0 // 128)
                        P.op("dve", lambda e, n=n: e.tensor_copy(out=A_[:, :n], in_=TMP[:, :n]), reads=["te", "cols", "th"], writes=["X"])
                        diag(A_, n, cols["a2col"], b0 // 128)
                        cmxb = CMX[:, b0:b0 + n]
                        if dr == 0:
                            P.op("dve", lambda e, n=n, cmxb=cmxb: e.tensor_tensor_scan(out=cmxb, data0=negf[:, :n], data1=TMP[:, :n], initial=-1e30, op0=ALU.add, op1=ALU.max), reads=["te", "negf"], writes=["CMX"])
                        else:
                            P.op("dve", lambda e, n=n, b0=b0: e.tensor_tensor_scan(out=CMX[:, b0 + n - 1:b0 - 1 if b0 > 0 else None:-1], data0=negb[:, n - 1::-1], data1=TMP[:, n - 1::-1], initial=-1e30, op0=ALU.add, op1=ALU.max),
                                 reads=["te", "negb"], writes=["CMX"])
                    BL, MLOC, MAF, M0, SP, SL, tch = (ch[k] for k in ("BL", "MLOC", "MAF", "M0", "SP", "SL", "tch"))
                    if dr == 0:
                        P.op("dve", lambda e: e.tensor_tensor_scan(out=MAF[:, :], data0=BL[:, :], data1=MLOC[:, :], initial=0.0, op0=ALU.add, op1=ALU.max), reads=["BL", "MLOC"], writes=["MAF"])
                        P.op("dve", lambda e: e.memset(M0[:, 0:1], 0.0), writes=["M0"])
                        P.op("dve", lambda e: e.tensor_copy(out=M0[:, 1:NCH], in_=MAF[:, 0:NCH - 1]), reads=["MAF"], writes=["M0"])
                    else:
                        P.op("dve", lambda e: e.tensor_tensor_scan(out=MAF[:, ncc - 1::-1], data0=BL[:, ncc - 1::-1], data1=MLOC[:, ncc - 1::-1], initial=0.0, op0=ALU.add, op1=ALU.max), reads=["BL", "MLOC"], writes=["MAF"])
                        P.op("dve", lambda e: e.tensor_tensor_scan(out=MAF[:, NCH - 1:ncc - 1:-1], data0=BL[:, NCH - 1:ncc - 1:-1], data1=MLOC[:, NCH - 1:ncc - 1:-1], initial=MAF[:, 0:1], op0=ALU.add, op1=ALU.max),
                             reads=["BL", "MLOC", "MAF"], writes=["MAF"])
                        P.op("dve", lambda e: e.memset(M0[:, ncc - 1:ncc], 0.0), writes=["M0"])
                        if ncc > 1:
                            P.op("dve", lambda e: e.tensor_copy(out=M0[:, 0:ncc - 1], in_=MAF[:, 1:ncc]), reads=["MAF"], writes=["M0"])
                        P.op("dve", lambda e: e.tensor_copy(out=M0[:, ncc:NCH - 1], in_=MAF[:, ncc + 1:NCH]), reads=["MAF"], writes=["M0"])
                        P.op("dve", lambda e: e.tensor_copy(out=M0[:, NCH - 1:NCH], in_=MAF[:, 0:1]), reads=["MAF"], writes=["M0"])
                    P.op("dve", lambda e: e.tensor_tensor(out=tch[:, :], in0=BL[:, :], in1=M0[:, :], op=ALU.add), reads=["BL", "M0"], writes=["tch"])
                    P.op("dve", lambda e: e.tensor_tensor(out=tch[:, :], in0=tch[:, :], in1=MAF[:, :], op=ALU.subtract), reads=["tch", "MAF"], writes=["tch"])
                    P.op("act", lambda e: e.activation(out=SP[:, :], in_=tch[:, :], func=AF.Exp), reads=["tch"], writes=["SP"])
                    P.op("dve", lambda e: e.tensor_tensor(out=tch[:, :], in0=MLOC[:, :], in1=MAF[:, :], op=ALU.subtract), reads=["MLOC", "MAF", "SP"], writes=["tch"])
                    P.op("act", lambda e: e.activation(out=SL[:, :], in_=tch[:, :], func=AF.Exp), reads=["tch"], writes=["SL"])
                    for (b0, n) in blocks:
                        nchb = n // 64
                        c0 = b0 // 64
                        Z, X1, X2 = T["ta"], T["tb"], T["td"]
                        v3 = lambda X, n=n: X[:, :n].rearrange("p (a b) -> p a b", b=64)
                        m0bc = lambda c0=c0, nchb=nchb: M0[:, c0:c0 + nchb].unsqueeze(2).to_broadcast([128, nchb, 64])
                        cmxb = CMX[:, b0:b0 + n]
                        cumb = CUM[:, b0:b0 + n]
                        P.op("dve", lambda e, n=n, cmxb=cmxb, m0bc=m0bc, v3=v3: e.tensor_tensor(out=v3(Z), in0=cmxb.rearrange("p (a b) -> p a b", b=64), in1=m0bc(), op=ALU.max), reads=["CMX", "M0"], writes=["ta"])
                        P.op("dve", lambda e, n=n, m0bc=m0bc, v3=v3: e.tensor_tensor(out=v3(X1), in0=m0bc(), in1=v3(Z), op=ALU.subtract), reads=["ta", "M0"], writes=["tb"])
                        P.op("act", lambda e, n=n: e.activation(out=X2[:, :n], in_=X1[:, :n], func=AF.Exp, bias=lnsc[:, 0:1]), reads=["tb", "lnsc"], writes=["X"])
                        diag(X2, n, cols["sicol"], b0 // 128)
                        P.op("dve", lambda e, n=n, cumb=cumb: e.tensor_tensor(out=X1[:, :n], in0=cumb, in1=Z[:, :n], op=ALU.add), reads=["ta", "CUM", "cols", "th"], writes=["tb"])
                        P.op("act", lambda e, n=n: e.activation(out=X2[:, :n], in_=X1[:, :n], func=AF.Exp, scale=-1.0), reads=["tb"], writes=["X"])
                        diag(X2, n, cols["emcol"], b0 // 128)
                        P.op("dve", lambda e, n=n, cmxb=cmxb: e.tensor_scalar(out=cmxb, in0=Z[:, :n], scalar1=-1.0, scalar2=None, op0=ALU.mult), reads=["ta", "cols"], writes=["ROWP"])
                    if dr == 0:
                        order = list(range(ntile))
                    else:
                        order = list(range(nct - 1, -1, -1)) + list(range(ntile - 1, nct - 1, -1))
                    P.op("dve", lambda e: e.memset(Cst[:], 0.0), writes=["Cst"])
                    for oi, ti in enumerate(order):
                        isctx = ti < nct
                        chunks = (2 * ti, 2 * ti + 1) if dr == 0 else (2 * ti + 1, 2 * ti)
                        kwt = kw[oi % 2]
                        P.op("dve", lambda e, kwt=kwt, ti=ti: e.tensor_scalar(out=kwt[:], in0=ktok[:, ti, :], scalar1=cols["wcol"][:, ti:ti + 1], scalar2=None, op0=ALU.mult),
                             reads=["ktok", "cols"], writes=[("kw", oi % 2)])
                        c0s = []
                        for cc in chunks:
                            hb = (cc % 2) * 64
                            c0b = C0c[(2 * oi + (cc % 2)) % 4]
                            c0s.append((hb, c0b, (2 * oi + (cc % 2)) % 4))
                            P.op("dve", lambda e, c0b=c0b: e.tensor_copy(out=c0b[:, 0:129], in_=Cst[:, 0:129]), reads=["Cst"], writes=[("C0c", (2 * oi + (cc % 2)) % 4)])
                            P.op("pe", lambda e, kwt=kwt, hb=hb, ti=ti: e.matmul(ps[6][:, 0:129], lhsT=kwt[hb:hb + 64, :], rhs=vaug[hb:hb + 64, ti, 0:129], start=True, stop=True),
                                 reads=[("kw", oi % 2), "vaug"], writes=[("ps", 6)])
                            ct = ctmp[cc % 2]
                            P.op("act", lambda e, ct=ct, cc=cc: e.activation(out=ct[:, 0:129], in_=ps[6][:, 0:129], func=AF.Copy, scale=SL[:, cc:cc + 1]), reads=[("ps", 6), "SL"], writes=[("ctmp", cc % 2)])
                            P.op("dve", lambda e, ct=ct, cc=cc: e.scalar_tensor_tensor(out=Cst[:, 0:129], in0=Cst[:, 0:129], scalar=SP[:, cc:cc + 1], in1=ct[:, 0:129], op0=ALU.mult, op1=ALU.add),
                                 reads=[("ctmp", cc % 2), "SP", "Cst"], writes=["Cst"])
                        if isctx and last:
                            continue
                        t0 = ti * 128
                        pS = ps[oi % 2]
                        pN = ps[2 + oi % 2]
                        pI = ps[4 + oi % 2]
                        e_t = et[oi % 2]
                        sw = swT[oi % 2]
                        P.op("pe", lambda e, pS=pS, t0=t0: e.matmul(pS[:, :128], lhsT=kcT[:, t0:t0 + 128], rhs=qcT[:, t0:t0 + 128], start=True, stop=True), reads=[("qk", 0), ("qk", 1)], writes=[("ps", oi % 2)])
                        P.op("dve", lambda e, e_t=e_t, t0=t0, ti=ti: e.scalar_tensor_tensor(out=e_t[:], in0=CMX[:, t0:t0 + 128], scalar=cols["a2col"][:, ti:ti + 1], in1=MC[:, dr, :], op0=ALU.add, op1=ALU.add),
                             reads=["ROWP", "cols", "MC"], writes=[("et", oi % 2)])
                        P.op("act", lambda e, e_t=e_t: e.activation(out=e_t[:], in_=e_t[:], func=AF.Exp), reads=[("et", oi % 2)], writes=[("et", oi % 2)])
                        P.op("dve", lambda e, e_t=e_t, sw=sw, pS=pS: e.scalar_tensor_tensor(out=sw[:], in0=pS[:, :128], scalar=isq, in1=e_t[:], op0=ALU.mult, op1=ALU.mult),
                             reads=[("ps", oi % 2), ("et", oi % 2)], writes=[("sw", oi % 2)])
                        P.op("pe", lambda e, pN=pN, sw=sw, ti=ti: e.matmul(pN[:, 0:129], lhsT=sw[:], rhs=vaug[:, ti, 0:129], start=True, stop=True), reads=[("sw", oi % 2), "vaug"], writes=[("ps", 2 + oi % 2)])
                        for (hb, c0b, ck) in c0s:
                            P.op("pe", lambda e, pI=pI, hb=hb, c0b=c0b, t0=t0: e.matmul(pI[hb:hb + 64, 0:129], lhsT=qcT[:, t0 + hb:t0 + hb + 64], rhs=c0b[:, 0:129], start=True, stop=True),
                                 reads=[("qk", 0), ("C0c", ck)], writes=[("ps", 4 + oi % 2, hb)])
                        H = Hs[oi % 2]
                        P.op("act", lambda e, H=H, pI=pI, ti=ti: e.activation(out=H[:, 0:129], in_=pI[:, 0:129], func=AF.Copy, scale=cols["sicol"][:, ti:ti + 1]),
                             reads=[("ps", 4 + oi % 2, 0), ("ps", 4 + oi % 2, 64), "cols"], writes=[("H", oi % 2)])
                        P.op("dve", lambda e, H=H, pN=pN: e.tensor_tensor(out=H[:, 0:129], in0=H[:, 0:129], in1=pN[:, 0:129], op=ALU.add), reads=[("H", oi % 2), ("ps", 2 + oi % 2)], writes=[("H", oi % 2)])
                        s1 = sc1[oi % 2]
                        P.op("dve", lambda e, H=H, s1=s1: e.tensor_scalar(out=s1[:, 0:1], in0=H[:, 128:129], scalar1=-1.0, scalar2=None, op0=ALU.mult), reads=[("H", oi % 2)], writes=[("sc1", oi % 2)])
                        P.op("dve", lambda e, H=H, s1=s1: e.tensor_tensor(out=s1[:, 0:1], in0=s1[:, 0:1], in1=H[:, 128:129], op=ALU.max), reads=[("H", oi % 2), ("sc1", oi % 2)], writes=[("sc1", oi % 2)])
                        P.op("dve", lambda e, s1=s1, ti=ti: e.tensor_scalar(out=s1[:, 0:1], in0=s1[:, 0:1], scalar1=cols["emcol"][:, ti:ti + 1], scalar2=None, op0=ALU.max),
                             reads=[("sc1", oi % 2), "cols"], writes=[("sc1", oi % 2)])
                        P.op("dve", lambda e, s1=s1: e.reciprocal(out=s1[:, 0:1], in_=s1[:, 0:1]), reads=[("sc1", oi % 2)], writes=[("sc1", oi % 2)])
                        hft = hf[oi % 2]
                        if dr == 0:
                            P.op("dve", lambda e, H=H, s1=s1, hft=hft: e.tensor_scalar(out=hft[:], in0=H[:, 0:128], scalar1=s1[:, 0:1], scalar2=None, op0=ALU.mult), reads=[("H", oi % 2), ("sc1", oi % 2)], writes=[("hf", oi % 2)])
                            P.op("sp", lambda e, hft=hft, ti=ti: e.dma_start(out=hscr[ti * 128:(ti + 1) * 128, :], in_=hft[:]), reads=[("hf", oi % 2)], writes=[("hscr", ti)])
                        else:
                            P.op("sp", lambda e, hft=hft, ti=ti: e.dma_start(out=hft[:], in_=hscr[ti * 128:(ti + 1) * 128, :]), reads=[("hscr", ti)], writes=[("hf", oi % 2)])
                            P.op("dve", lambda e, H=H, s1=s1, hft=hft: e.scalar_tensor_tensor(out=hft[:], in0=H[:, 0:128], scalar=s1[:, 0:1], in1=hft[:], op0=ALU.mult, op1=ALU.add),
                                 reads=[("H", oi % 2), ("sc1", oi % 2), ("hf", oi % 2)], writes=[("hf", oi % 2)])
                            hnt = hn[oi % 2]
                            P.op("act", lambda e, hft=hft, hnt=hnt, s1=s1: e.activation(out=hnt[:], in_=hft[:], func=AF.Square, accum_out=s1[:, 1:2]), reads=[("hf", oi % 2)], writes=[("hn", oi % 2), ("sc1", oi % 2)])
                            P.op("act", lambda e, s1=s1: e.activation(out=s1[:, 1:2], in_=s1[:, 1:2], func=AF.Sqrt, scale=1.0 / 128, bias=epsb[:, 0:1]), reads=[("sc1", oi % 2), "epsb"], writes=[("sc1", oi % 2)])
                            P.op("dve", lambda e, s1=s1: e.reciprocal(out=s1[:, 1:2], in_=s1[:, 1:2]), reads=[("sc1", oi % 2)], writes=[("sc1", oi % 2)])
                            P.op("dve", lambda e, hft=hft, hnt=hnt, s1=s1: e.tensor_scalar(out=hnt[:], in0=hft[:], scalar1=s1[:, 1:2], scalar2=None, op0=ALU.mult), reads=[("hf", oi % 2), ("sc1", oi % 2), ("hn", oi % 2)], writes=[("hn", oi % 2)])
                            P.op("pe", lambda e, hnt=hnt: e.transpose(out=ps[7][:, :128], in_=hnt[:], identity=ident[:]), reads=[("hn", oi % 2)], writes=[("ps", 7)])
                            so = sgo[oi % 2]
                            P.op("sp", lambda e, so=so, t0=t0: e.dma_start(out=so[:], in_=self.pTc[3072 + h * 128:3072 + (h + 1) * 128, t0:t0 + 128]), writes=[("sgo", oi % 2)])
                            P.op("act", lambda e, so=so: e.activation(out=so[:], in_=so[:], func=AF.Sigmoid), reads=[("sgo", oi % 2)], writes=[("sgo", oi % 2)])
                            P.op("dve", lambda e, so=so, t0=t0: e.scalar_tensor_tensor(out=obC[:, t0:t0 + 128], in0=ps[7][:, :128], scalar=mng[:, h:h + 1], in1=so[:], op0=ALU.mult, op1=ALU.mult),
                                 reads=[("ps", 7), ("sgo", oi % 2), "mng"], writes=[("obC", ti)])
                for dr_ in range(2):
                    do_dir(dr_)
                t0o = 0 if not last else c.CTX
                P.op("sp", lambda e, h=h, t0o=t0o: e.dma_start(out=self.oT[2048 + h * 128:2048 + (h + 1) * 128, t0o:], in_=obC[:, t0o:]),
                     reads=[("obC", ti) for ti in range(t0o // 128, ntile)], writes=[("oT", "C", h)])
            for h_ in range(8):
                do_head(h_)
            P.emit()


def host_inputs(cfg, inp, b):
    c = cfg
    f = np.float32

    def fm(v):
        return np.ascontiguousarray(v.reshape(c.KC, 128).T)
    cvec = np.stack([fm(inp["c"][b]), fm(inp["c_ctx"])], axis=-1)
    m = {
        "x": np.ascontiguousarray(inp["x"][b]),
        "ctx": np.ascontiguousarray(inp["ctx"][b]),
        "cvec": np.ascontiguousarray(cvec, dtype=f),
        "w_mod": inp["w_mod"],
        "b_modT": np.ascontiguousarray(inp["b_mod"].reshape(c.L, 9 * c.KC, 128).transpose(0, 2, 1)),
        "norm_gT": np.ascontiguousarray(inp["norm_g"].reshape(c.L, 3, c.KC, 128).transpose(0, 3, 1, 2)),
        "ffn_w1": inp["ffn_w1"], "ffn_w3": inp["ffn_w3"], "ffn_w2": inp["ffn_w2"],
        "ident": np.eye(128, dtype=f),
        "w_in": inp["w_in"], "w_gate": inp["w_gate"], "w_branch": inp["w_branch"], "w_out": inp["w_out"],
        "b_gateT": np.ascontiguousarray(inp["b_gate"].reshape(c.L, 3, c.KC, 128).transpose(0, 3, 1, 2)),
    }
    mc = mixer_consts(c)
    m["cosT"], m["sinT"], m["RmD"] = mc["cosT"], mc["sinT"], mc["Rm"]
    m["MAD"] = np.ascontiguousarray(mc["MA"].transpose(1, 0, 2))
    m["MCD"] = np.ascontiguousarray(mc["MC"].transpose(1, 0, 2))
    m["mnegBD"] = np.ascontiguousarray(mc["mnegB"].transpose(1, 0, 2))
    rp = inp["na_relpos"]
    rb = np.stack([rp[:, :, dr, dc] for (_, dr, dc) in mc["case_list"]], axis=2)
    m["rbD"] = np.ascontiguousarray(rb.transpose(0, 1, 3, 2, 4), dtype=f)
    m["qk_gT"] = np.ascontiguousarray(inp["qk_g"].transpose(0, 2, 1))
    m["mresD"], m["negfD"], m["negbD"] = mc["mres"][:, :512].copy(), mc["negf"][:, :512].copy(), mc["negb"][:, :512].copy()
    m["conv_wT"] = np.ascontiguousarray(inp["mlstm_conv_w"].reshape(c.L, 5, 16, 128).transpose(0, 3, 2, 1))
    m["conv_bT"] = np.ascontiguousarray(inp["mlstm_conv_b"].reshape(c.L, 16, 128).transpose(0, 2, 1))
    m["gate_bB"] = np.ascontiguousarray(np.broadcast_to(inp["mlstm_gate_b"].reshape(c.L, 1, 32), (c.L, 128, 32)))
    m["mngT"] = np.ascontiguousarray(inp["mlstm_norm_g"].reshape(c.L, 8, 128).transpose(0, 2, 1))
    m["sinkB"] = np.ascontiguousarray(np.broadcast_to(inp["attn_sink"][:, None, :], (c.L, 128, 8)))
    return m


def mixer_consts(cfg):
    c = cfg
    f = np.float32
    NT = c.NTOK
    d = np.arange(128)
    axis = d // 64
    half = (d % 64) // 32
    p = d % 32
    inv = (10000.0 ** (-np.arange(32, dtype=np.float32) / 32)).astype(np.float32)
    cosT = np.ones((128, NT), f)
    sinT = np.zeros((128, NT), f)
    t = np.arange(c.SEQ)
    row = (t // GRID_W).astype(np.float32)
    col = (t % GRID_W).astype(np.float32)
    pos = np.where(axis[:, None] == 0, row[None, :], col[None, :]).astype(np.float32)
    ang = pos * inv[p][:, None]
    cosT[:, c.CTX:] = np.cos(ang)
    sinT[:, c.CTX:] = np.sin(ang)
    Rm = np.zeros((128, 128), f)
    for m in range(128):
        if half[m] == 0:
            Rm[m + 32, m] = -1.0
        else:
            Rm[m - 32, m] = 1.0
    NEG = -30000.0
    j = np.arange(128)[:, None]
    i = np.arange(128)[None, :]
    MA = np.stack([np.where(j >= i, 0.0, NEG), np.where(j <= i, 0.0, NEG)], 0).astype(f)
    same = (j // 64) == (i // 64)
    MC = np.stack([np.where(same & (j <= i), 0.0, NEG), np.where(same & (j >= i), 0.0, NEG)], 0).astype(f)
    rows = c.ROWS
    cases = {}
    case_list = []
    sched = []
    for b in range(rows // 2):
        lst = []
        rq = 2 * b + (np.arange(128) // 64)
        cq = np.arange(128) % 64
        r0 = np.clip(rq - 4, 0, rows - 8)
        cs = np.clip(cq - 8, 0, GRID_W - 16)
        for a in range(rows // 2):
            rk = 2 * a + (np.arange(128) // 64)
            ck = np.arange(128) % 64
            ok = (rk[:, None] >= r0[None, :]) & (rk[:, None] < r0[None, :] + 8) & (ck[:, None] >= cs[None, :]) & (ck[:, None] < cs[None, :] + 16)
            if not ok.any():
                continue
            dr = np.clip(rk[:, None] - rq[None, :] + 7, 0, 14)
            dc = np.clip(ck[:, None] - cq[None, :], -15, 15) + 15
            key = (ok.tobytes(), dr.tobytes(), dc.tobytes())
            if key not in cases:
                cases[key] = len(case_list)
                case_list.append((ok, dr, dc))
            lst.append((a, cases[key]))
        sched.append(lst)
    mnegB = np.stack([np.where(ok, 0.0, NEG) for ok, _, _ in case_list], 0).astype(f)
    tt = np.arange(NT)
    mres = np.where(tt % 64 == 0, 0.0, 1.0).astype(f)
    negf = np.where(tt % 64 == 0, -1e30, 0.0).astype(f)
    negb = np.where(tt % 64 == 63, -1e30, 0.0).astype(f)
    return dict(cosT=cosT, sinT=sinT, Rm=Rm, MA=MA, MC=MC, mnegB=mnegB, case_list=case_list, sched=sched,
                mres=np.broadcast_to(mres, (128, NT)).copy(), negf=np.broadcast_to(negf, (128, NT)).copy(), negb=np.broadcast_to(negb, (128, NT)).copy())


def kernel(**inp):
    from concourse.bass_utils import run_bass_kernel_spmd
    inp = {k: np.asarray(v) for k, v in inp.items()}
    cfg = FULL
    kb = K(cfg)
    nc = kb.build()
    maps = [host_inputs(cfg, inp, b) for b in range(2)]
    res = run_bass_kernel_spmd(nc, maps, core_ids=[0, 1])
    return np.stack([r["out"] for r in res.results]).astype(np.float32)
```

```python
import numpy as np
import concourse.bass as bass
import concourse.mybir as mybir

F32 = mybir.dt.float32
BF16 = mybir.dt.bfloat16
AF = mybir.ActivationFunctionType
ALU = mybir.AluOpType
AX = mybir.AxisListType

COMPUTE = ("pe", "act", "dve", "pool")
DMAQ = ("sp", "gq")


class Op:
    __slots__ = ("eng", "fn", "reads", "writes", "idx", "sig", "deps", "dma_n")

    def __init__(self, eng, fn, reads, writes):
        self.eng = eng
        self.fn = fn
        self.reads = reads
        self.writes = writes
        self.sig = None
        self.deps = ()
        self.dma_n = None


class SemPool:
    def __init__(self, nc, es, ring=20, ngq=16):
        self.sems = {e: es.enter_context(nc.semaphore(f"sem_{e}")) for e in COMPUTE}
        self.dsp = [es.enter_context(nc.semaphore(f"dsem_sp_{i}")) for i in range(ring)]
        self.gq = [es.enter_context(nc.semaphore(f"gsem_{i}")) for i in range(ngq)]


class Prog:
    _uid = [0]

    def __init__(self, nc, pool):
        self.nc = nc
        self.ops = []
        self.pool = pool
        self.ring = len(pool.dsp)

    def op(self, eng, fn, reads=(), writes=()):
        o = Op(eng, fn, tuple(reads), tuple(writes))
        o.idx = len(self.ops)
        self.ops.append(o)
        return o

    @staticmethod
    def stream(eng):
        return "pool" if eng == "gq" else eng

    def analyze(self):
        last_w = {}
        readers = {}
        need_sig = set()
        for o in self.ops:
            deps = set()
            for k in o.reads:
                w = last_w.get(k)
                if w is not None:
                    deps.add(w)
            for k in o.writes:
                w = last_w.get(k)
                if w is not None:
                    deps.add(w)
                r = readers.get(k)
                if r:
                    for v in r.values():
                        if isinstance(v, list):
                            deps.update(v)
                        else:
                            deps.add(v)
            deps.discard(o.idx)
            fin = []
            for d in deps:
                p = self.ops[d]
                if p.eng == "pe" and o.eng == "pe":
                    continue
                fin.append(d)
                need_sig.add(d)
            o.deps = fin
            for k in o.writes:
                last_w[k] = o.idx
                readers[k] = {}
            for k in o.reads:
                r = readers.setdefault(k, {})
                if o.eng in DMAQ:
                    r.setdefault(o.eng, []).append(o.idx)
                    if len(r[o.eng]) > 64:
                        r[o.eng] = r[o.eng][-64:]
                else:
                    r[o.eng] = o.idx
        cnt = {e: 0 for e in COMPUTE}
        dcnt = {e: 0 for e in DMAQ}
        self.gqgen = {}
        for o in self.ops:
            if o.eng in DMAQ:
                o.dma_n = dcnt[o.eng]
                dcnt[o.eng] += 1
            elif o.idx in need_sig:
                cnt[o.eng] += 1
                o.sig = cnt[o.eng]
        self.nsig = cnt
        self.ndma = dcnt

    def emit(self, final_wait_ops=()):
        nc = self.nc
        self.analyze()
        from contextlib import ExitStack
        K = self.ring
        with ExitStack() as es:
            sp_ = self.pool
            sems = sp_.sems
            dsem = {"sp": sp_.dsp}
            dsem["gq"] = sp_.gq
            KQ = {"sp": len(sp_.dsp), "gq": len(sp_.gq)}
            es2 = ExitStack()
            block = es2.enter_context(nc.Block())
            engobj = {"pe": "tensor", "act": "scalar", "dve": "vector", "pool": "gpsimd", "sp": "sync"}
            streams = {s: [] for s in engobj}
            for o in self.ops:
                streams[self.stream(o.eng)].append(o)
            ops = self.ops

            def dma_target(o):
                kq = KQ[o.eng]
                return dsem[o.eng][o.dma_n % kq], 16 * (o.dma_n // kq + 1)

            def run_stream(sname, eng):
                seen = {}

                def wait(sem, val):
                    key = id(sem)
                    if isinstance(val, tuple):
                        if seen.get(key, 0) >= val[1]:
                            return
                        eng.wait_ge(sem, 16)
                        seen[key] = val[1]
                        return
                    if seen.get(key, 0) >= val:
                        return
                    eng.wait_ge(sem, val)
                    seen[key] = val

                for o in streams[sname]:
                    cw = {}
                    for d in o.deps:
                        p = ops[d]
                        if p.eng in DMAQ:
                            s, v = dma_target(p)
                            wait(s, v)
                        else:
                            cw[p.eng] = max(cw.get(p.eng, 0), p.sig)
                    for e, v in cw.items():
                        wait(sems[e], v)
                    if o.eng in DMAQ and o.dma_n >= KQ[o.eng]:
                        kq = KQ[o.eng]
                        wait(dsem[o.eng][o.dma_n % kq], 16 * (o.dma_n // kq))
                    ins = o.fn(eng)
                    if o.eng in DMAQ:
                        s, v = dma_target(o)
                        ins.then_inc(s, 16)
                    elif o.sig is not None:
                        ins.then_inc(sems[o.eng], 1)
                if sname == "sp":
                    lst = [o for o in ops if o.eng == "sp"][-KQ["sp"]:] + [o for o in ops if o.eng == "gq"][-KQ["gq"]:]
                    for o in lst:
                        s, v = dma_target(o)
                        wait(s, v)

            for sname, attr in engobj.items():
                if not streams[sname] and sname != "sp":
                    continue
                dec = getattr(block, attr)

                def body(eng, sname=sname):
                    run_stream(sname, eng)
                dec(body)
            es2.close()
            allsems = list(sems.values()) + dsem["sp"] + dsem["gq"]
            with nc.Block() as b2:
                def clr(eng):
                    for sm in allsems:
                        eng.sem_clear(sm)
                b2.sync(clr)


import os
from contextlib import ExitStack

HD = 128
NH = 8
GRID_W = 64
EPS = 1e-6


class Cfg:
    def __init__(self, D, SEQ, CTX, DFF, L):
        self.D, self.SEQ, self.CTX, self.DFF, self.L = D, SEQ, CTX, DFF, L
        self.KC = D // 128
        self.JC = DFF // 128
        self.NTOK = CTX + SEQ
        self.ROWS = SEQ // GRID_W
        self.INW = 8 * 128 + 2 * 128 * 2 + 3 * 8 * 128 + 4 * 8 * 128 + 32
        self.tiles = []
        s = 0
        while s < CTX:
            t = min(512, CTX - s)
            self.tiles.append((s, t, True))
            s += t
        while s < self.NTOK:
            t = min(512, self.NTOK - s)
            self.tiles.append((s, t, False))
            s += t


FULL = Cfg(4096, 8192, 256, 7168, 2)


class K:
    def __init__(self, cfg, stop_after=None, dbg=False):
        self.cfg = cfg
        self.stop_after = stop_after
        nc = bass.Bass("TRN2", target_bir_lowering=False)
        self.nc = nc
        c = cfg
        L = c.L

        def din(name, shape, dt=F32):
            return nc.dram_tensor(name, list(shape), dt, kind="ExternalInput").ap()

        self.x = din("x", [c.SEQ, c.D])
        self.ctx = din("ctx", [c.CTX, c.D])
        self.cvec = din("cvec", [128, c.KC, 2])
        self.w_mod = din("w_mod", [L, c.D, 9 * c.D])
        self.b_modT = din("b_modT", [L, 128, 9 * c.KC])
        self.norm_gT = din("norm_gT", [L, 128, 3, c.KC])
        self.ffn_w1 = din("ffn_w1", [L, 2, c.D, c.DFF])
        self.ffn_w3 = din("ffn_w3", [L, 2, c.D, c.DFF])
        self.ffn_w2 = din("ffn_w2", [L, 2, c.DFF, c.D])
        self.ident = din("ident", [128, 128])
        self.w_in = din("w_in", [L, c.D, c.INW])
        self.w_gate = din("w_gate", [L, 3, c.D, c.D])
        self.w_branch = din("w_branch", [L, 3, 1024, c.D])
        self.w_out = din("w_out", [L, c.D, c.D])
        self.b_gateT = din("b_gateT", [L, 128, 3, c.KC])
        self.mc = mixer_consts(c)
        ncase = len(self.mc["case_list"])
        self.cosT = din("cosT", [128, c.NTOK])
        self.sinT = din("sinT", [128, c.NTOK])
        self.RmD = din("RmD", [128, 128])
        self.MAD = din("MAD", [128, 2, 128])
        self.MCD = din("MCD", [128, 2, 128])
        self.mnegBD = din("mnegBD", [128, ncase, 128])
        self.rbD = din("rbD", [L, 8, 128, ncase, 128])
        self.qk_gT = din("qk_gT", [L, 128, 4])
        self.sinkB = din("sinkB", [L, 128, 8])
        self.mresD = din("mresD", [128, 512])
        self.negfD = din("negfD", [128, 512])
        self.negbD = din("negbD", [128, 512])
        self.conv_wT = din("conv_wT", [L, 128, 16, 5])
        self.conv_bT = din("conv_bT", [L, 128, 16])
        self.gate_bB = din("gate_bB", [L, 128, 32])
        self.mngT = din("mngT", [L, 128, 8])
        self.hscr = nc.dram_tensor("hscr", [c.NTOK, 128], F32, kind="Internal").ap()
        skind = "ExternalOutput" if dbg else "Internal"
        self.NA = 36 * 128
        self.NC = c.INW - self.NA
        self.pTa = nc.dram_tensor("pTa", [self.NA, c.NTOK], F32, kind=skind).ap()
        self.pTc = nc.dram_tensor("pTc", [self.NC, c.NTOK], F32, kind=skind).ap()
        self.oT = nc.dram_tensor("oT", [3 * 1024, c.NTOK], BF16, kind=skind).ap()
        self.pc1 = min(256, 8192 // c.KC)
        self.pc2 = max(128, min(512, (8192 // c.JC) // 128 * 128))
        self.bf = {}

        def reg(name, key, src2d, pc):
            Kr, N = src2d.shape
            kch = Kr // 128
            npieces = (N + pc - 1) // pc
            per = max(1, (200 * 1024 * 1024) // (128 * kch * pc * 2))
            tens = []
            for t0 in range(0, npieces, per):
                n = min(per, npieces - t0)
                tens.append(nc.dram_tensor(f"{name}_bf_{'_'.join(map(str, key))}_{t0}", [n, 128, kch * pc], BF16, kind="Internal").ap())
            self.bf[(name,) + tuple(key)] = dict(tens=tens, per=per, pc=pc, kch=kch, N=N, src=src2d, npieces=npieces)

        for l in range(L):
            reg("w_mod", (l,), self.w_mod[l], 256)
            for w in range(2):
                reg("ffn_w1", (l, w), self.ffn_w1[l, w], self.pc1)
                reg("ffn_w3", (l, w), self.ffn_w3[l, w], self.pc1)
                reg("ffn_w2", (l, w), self.ffn_w2[l, w], self.pc2)
            reg("w_in", (l,), self.w_in[l], self.pc1)
            for b in range(3):
                reg("w_gate", (l, b), self.w_gate[l, b], self.pc1)
                reg("w_branch", (l, b), self.w_branch[l, b], self.pc1)
            reg("w_out", (l,), self.w_out[l], self.pc1)

        self.out = nc.dram_tensor("out", [c.SEQ, c.D], F32, kind="ExternalOutput").ap()
        self.xT = nc.dram_tensor("xT", [c.D, c.NTOK], F32, kind="Internal").ap()

    def piece(self, wkey, i):
        d = self.bf[wkey]
        t = d["tens"][i // d["per"]]
        ncols = min(d["pc"], d["N"] - i * d["pc"])
        return t[i % d["per"]], ncols, d["kch"], d["pc"]

    def wload_piece(self, P, wkey, i, buf, bkey):
        ap, ncols, kch, pc = self.piece(wkey, i)
        tot = kch * pc
        keys = []
        if ncols < pc:
            bv = buf[:, 0:tot].rearrange("p (k n) -> p k n", n=pc)
            av = ap.rearrange("p (k n) -> p k n", n=pc)
            for k8 in range(0, kch, 8):
                k9 = min(kch, k8 + 8)
                P.op("sp", lambda e, k8=k8, k9=k9: e.dma_start(out=bv[:, k8:k9, 0:ncols], in_=av[:, k8:k9, 0:ncols]), writes=[bkey + (k8,)])
                keys.append(bkey + (k8,))
            return bv, keys
        nsp = 2 if tot >= 2048 else 1
        step = tot // nsp
        for q in range(nsp):
            P.op("sp", lambda e, q=q: e.dma_start(out=buf[:, q * step:(q + 1) * step], in_=ap[:, q * step:(q + 1) * step]), writes=[bkey + (q,)])
            keys.append(bkey + (q,))
        return buf[:, 0:tot].rearrange("p (k n) -> p k n", n=pc), keys

    _uid = [0]

    def sb(self, es, name, shape, dt=F32):
        K._uid[0] += 1
        return es.enter_context(self.nc.sbuf_tensor(f"{name}_{K._uid[0]}", list(shape), dt))

    def psum_banks(self, es, n=8):
        K._uid[0] += 1
        return [es.enter_context(self.nc.psum_tensor(f"ps{i}_{K._uid[0]}", [128, 512], F32)) for i in range(n)]

    def build(self):
        c = self.cfg
        with ExitStack() as es:
            self.pool = SemPool(self.nc, es)
            self.identS = self.sb(es, "identS", [128, 128])
            self.onesF = self.sb(es, "onesF", [128, 128])
            self.modS = self.sb(es, "modS", [128, 9 * c.KC, 2])
            self.GS = self.sb(es, "GS", [128, 2, 3, c.KC])
            self.SH = self.sb(es, "SH", [128, 2, 3, c.KC])
            self.GT = self.sb(es, "GT", [128, 2, 3, c.KC])
            import os
            if not os.environ.get("SKIP_CAST"):
                self.phase_cast()
            self.phase_in()
            for l in range(c.L):
                if os.environ.get("SKIP_MOD"):
                    break
                self.phase_mod(l)
                if self.stop_after == ("mod", l):
                    break
                self.phase_ffn(l, 0)
                if self.stop_after == ("ffn1", l):
                    break
                if os.environ.get("FFN2"):
                    self.phase_ffn(l, 1)
                    continue
                last = (l == c.L - 1)
                self.phase_proj(l)
                if self.stop_after == ("proj", l):
                    break
                self.phase_mix(l, last)
                if self.stop_after == ("mix", l):
                    break
                self.phase_merge(l, last)
                if self.stop_after == ("merge", l):
                    break
                self.phase_ffn(l, 1, skip_ctx=last)
            self.phase_out()
        return self.nc

    def phase_cast(self):
        nc = self.nc
        jobs = []
        for wkey, d in self.bf.items():
            pc, kch = d["pc"], d["kch"]
            srcv = d["src"].rearrange("(kc p) n -> p kc n", p=128)
            for i in range(d["npieces"]):
                ap, ncols, _, _ = self.piece(wkey, i)
                dv = ap.rearrange("p (k n) -> p k n", n=pc)
                for k8 in range(0, kch, 8):
                    k9 = min(kch, k8 + 8)
                    jobs.append((dv[:, k8:k9, 0:ncols], srcv[:, k8:k9, i * pc:i * pc + ncols]))
        P = Prog(nc, self.pool)
        for i, (d, s_) in enumerate(jobs):
            P.op("gq", lambda e, d=d, s_=s_: e.dma_start(out=d, in_=s_), writes=[("cast", i)])
        P.emit()

    def phase_in(self):
        c, nc = self.cfg, self.nc
        with ExitStack() as es:
            P = Prog(nc, self.pool)
            xtok = [self.sb(es, f"xtok{i}", [128, c.D]) for i in range(2)]
            stg = [self.sb(es, f"stg{i}", [128, c.KC, 128]) for i in range(2)]
            ps = self.psum_banks(es)
            P.op("sp", lambda e: e.dma_start(out=self.identS[:], in_=self.ident[:, :]), writes=["ident"])
            P.op("dve", lambda e: e.memset(self.onesF[:], 1.0), writes=["ones"])
            nsub = c.NTOK // 128
            xTv = self.xT.rearrange("(kc p) t -> p kc t", p=128)
            for s in range(nsub):
                t0 = s * 128
                src = self.ctx[t0:t0 + 128, :] if t0 < c.CTX else self.x[t0 - c.CTX:t0 - c.CTX + 128, :]
                xb = xtok[s % 2]
                sg = stg[s % 2]
                P.op("sp", lambda e, xb=xb, src=src: e.dma_start(out=xb[:], in_=src), writes=[("xtok", s % 2)])
                for kc in range(c.KC):
                    bank = ps[(kc // 4) % 8]
                    sl = bank[:, (kc % 4) * 128:(kc % 4) * 128 + 128]
                    P.op("pe", lambda e, sl=sl, xb=xb, kc=kc: e.transpose(out=sl, in_=xb[:, kc * 128:(kc + 1) * 128], identity=self.identS[:]),
                         reads=[("xtok", s % 2), "ident"], writes=[("psb", (kc // 4) % 8, kc % 4)])
                    if kc % 4 == 3 or kc == c.KC - 1:
                        k0 = (kc // 4) * 4
                        n = kc - k0 + 1
                        eng = "act" if (kc // 4) % 2 == 0 else "dve"
                        if eng == "act":
                            fn = lambda e, sg=sg, k0=k0, n=n, bank=bank: e.activation(out=sg[:, k0:k0 + n, :], in_=bank[:, 0:n * 128].rearrange("p (a b) -> p a b", b=128), func=AF.Copy)
                        else:
                            fn = lambda e, sg=sg, k0=k0, n=n, bank=bank: e.tensor_copy(out=sg[:, k0:k0 + n, :], in_=bank[:, 0:n * 128].rearrange("p (a b) -> p a b", b=128))
                        P.op(eng, fn, reads=[("psb", (kc // 4) % 8, q) for q in range(n)], writes=[("stg", s % 2, k0)])
                for k8 in range(0, c.KC, 8):
                    k9 = min(c.KC, k8 + 8)
                    P.op("sp", lambda e, sg=sg, t0=t0, k8=k8, k9=k9: e.dma_start(out=xTv[:, k8:k9, t0:t0 + 128], in_=sg[:, k8:k9, :]),
                         reads=[("stg", s % 2, k0) for k0 in range(k8, k9, 4)], writes=[("xT", s, k8)])
            P.emit()

    def phase_out(self):
        c, nc = self.cfg, self.nc
        with ExitStack() as es:
            P = Prog(nc, self.pool)
            xtok = [self.sb(es, f"oxtok{i}", [128, c.D]) for i in range(2)]
            stg = [self.sb(es, f"ostg{i}", [128, c.KC, 128]) for i in range(2)]
            ps = self.psum_banks(es)
            xTv = self.xT.rearrange("(kc p) t -> p kc t", p=128)
            nsub = c.SEQ // 128
            for s in range(nsub):
                t0 = c.CTX + s * 128
                xb = xtok[s % 2]
                sg = stg[s % 2]
                for k8 in range(0, c.KC, 8):
                    k9 = min(c.KC, k8 + 8)
                    P.op("sp", lambda e, sg=sg, t0=t0, k8=k8, k9=k9: e.dma_start(out=sg[:, k8:k9, :], in_=xTv[:, k8:k9, t0:t0 + 128]), writes=[("stg", s % 2, k8)])
                for kc in range(c.KC):
                    bank = ps[(kc // 4) % 8]
                    sl = bank[:, (kc % 4) * 128:(kc % 4) * 128 + 128]
                    P.op("pe", lambda e, sl=sl, sg=sg, kc=kc: e.transpose(out=sl, in_=sg[:, kc, :], identity=self.identS[:]),
                         reads=[("stg", s % 2, (kc // 8) * 8)], writes=[("psb", (kc // 4) % 8, kc % 4)])
                    if kc % 4 == 3 or kc == c.KC - 1:
                        k0 = (kc // 4) * 4
                        n = kc - k0 + 1
                        eng = "act" if (kc // 4) % 2 == 0 else "dve"
                        if eng == "act":
                            fn = lambda e, xb=xb, k0=k0, n=n, bank=bank: e.activation(out=xb[:, k0 * 128:(k0 + n) * 128], in_=bank[:, 0:n * 128], func=AF.Copy)
                        else:
                            fn = lambda e, xb=xb, k0=k0, n=n, bank=bank: e.tensor_copy(out=xb[:, k0 * 128:(k0 + n) * 128], in_=bank[:, 0:n * 128])
                        P.op(eng, fn, reads=[("psb", (kc // 4) % 8, q) for q in range(n)], writes=[("xtok", s % 2, k0)])
                P.op("sp", lambda e, xb=xb, s=s: e.dma_start(out=self.out[s * 128:(s + 1) * 128, :], in_=xb[:]),
                     reads=[("xtok", s % 2, k0) for k0 in range(0, c.KC, 4)], writes=[("out", s)])
            P.emit()

    def phase_mod(self, l):
        c, nc = self.cfg, self.nc
        NF = 9 * c.KC
        PC = 256
        with ExitStack() as es:
            P = Prog(nc, self.pool)
            ps = self.psum_banks(es)
            cv = self.sb(es, "cv", [128, c.KC, 2])
            sg = self.sb(es, "sg", [128, c.KC, 2])
            scb = self.sb(es, "scb", [128, c.KC, 2], BF16)
            bm = self.sb(es, "bm", [128, NF])
            ng = self.sb(es, "ng", [128, 3, c.KC])
            wbf = [self.sb(es, f"wm{i}", [128, c.KC * PC], BF16) for i in range(3)]
            P.op("sp", lambda e: e.dma_start(out=cv[:], in_=self.cvec[:, :, :]), writes=["cv"])
            P.op("sp", lambda e: e.dma_start(out=bm[:], in_=self.b_modT[l, :, :]), writes=["bm"])
            P.op("sp", lambda e: e.dma_start(out=ng[:], in_=self.norm_gT[l, :, :, :]), writes=["ng"])
            P.op("act", lambda e: e.activation(out=sg[:], in_=cv[:], func=AF.Sigmoid), reads=["cv"], writes=["sg"])
            P.op("dve", lambda e: e.tensor_tensor(out=scb[:], in0=cv[:], in1=sg[:], op=ALU.mult), reads=["cv", "sg"], writes=["scb"])
            npieces = (9 * c.D) // PC
            for pi in range(npieces):
                w, wmkeys = self.wload_piece(P, ("w_mod", l), pi, wbf[pi % 3], ("wm", pi % 3))
                for q in range(PC // 128):
                    f = pi * (PC // 128) + q
                    bank = ps[(f * 2) // 512]
                    o = (f * 2) % 512
                    for kc in range(c.KC):
                        P.op("pe", lambda e, w=w, q=q, kc=kc, bank=bank, o=o: e.matmul(bank[:, o:o + 2], lhsT=w[:, kc, q * 128:(q + 1) * 128], rhs=scb[:, kc, :], start=(kc == 0), stop=(kc == c.KC - 1)),
                             reads=wmkeys + ["scb"], writes=[("macc", (f * 2) // 512)])
            nb = (NF * 2 + 511) // 512
            for b in range(nb):
                f0 = b * 256
                f1 = min(NF, f0 + 256)
                P.op("dve", lambda e, b=b, f0=f0, f1=f1: e.tensor_tensor(out=self.modS[:, f0:f1, :], in0=ps[b][:, 0:(f1 - f0) * 2].rearrange("p (f t) -> p f t", t=2),
                                                                     in1=bm[:, f0:f1].unsqueeze(2).to_broadcast([128, f1 - f0, 2]), op=ALU.add),
                     reads=[("macc", b), "bm"], writes=["modS"])
            KC = c.KC
            for t in range(2):
                for s in range(3):
                    sh = self.modS[:, (3 * s) * KC:(3 * s + 1) * KC, t]
                    sc = self.modS[:, (3 * s + 1) * KC:(3 * s + 2) * KC, t]
                    gt = self.modS[:, (3 * s + 2) * KC:(3 * s + 3) * KC, t]
                    P.op("dve", lambda e, t=t, s=s, sc=sc: e.scalar_tensor_tensor(out=self.GS[:, t, s, :], in0=sc, scalar=1.0, in1=ng[:, s, :], op0=ALU.add, op1=ALU.mult),
                         reads=["modS", "ng"], writes=["GS"])
                    P.op("dve", lambda e, t=t, s=s, sh=sh: e.tensor_copy(out=self.SH[:, t, s, :], in_=sh), reads=["modS"], writes=["SH"])
                    P.op("dve", lambda e, t=t, s=s, gt=gt: e.tensor_scalar(out=self.GT[:, t, s, :], in0=gt, scalar1=(1.0 if s == 1 else 0.5), scalar2=None, op0=ALU.mult),
                         reads=["modS"], writes=["GT"])
            P.emit()

    def norm_mod(self, P, R, tile, sub, dst):
        c = self.cfg
        t0, T, isctx = tile
        ty = 1 if isctx else 0
        xTv = self.xT.rearrange("(kc p) t -> p kc t", p=128)
        ssq = R["ps"][0]
        for kc in range(c.KC):
            xb = R["xc"][kc % 4]
            sq = R["sq"][kc % 2]
            P.op("sp", lambda e, xb=xb, kc=kc: e.dma_start(out=xb[:, :T], in_=xTv[:, kc, t0:t0 + T]), reads=[("xT", t0, kc)], writes=[("xc", kc % 4)])
            P.op("act", lambda e, xb=xb, sq=sq: e.activation(out=sq[:, :T], in_=xb[:, :T], func=AF.Square), reads=[("xc", kc % 4)], writes=[("sq", kc % 2)])
            P.op("pe", lambda e, sq=sq, kc=kc: e.matmul(ssq[:, :T], lhsT=self.onesF[:], rhs=sq[:, :T], start=(kc == 0), stop=(kc == c.KC - 1)),
                 reads=[("sq", kc % 2), "ones"], writes=[("ps", 0)])
        rs = R["rstd"]
        P.op("act", lambda e: e.activation(out=rs[:, :T], in_=ssq[:, :T], func=AF.Sqrt, scale=1.0 / c.D, bias=R["epsb"][:, 0:1]), reads=[("ps", 0), "epsb"], writes=["rstd"])
        P.op("dve", lambda e: e.reciprocal(out=rs[:, :T], in_=rs[:, :T]), reads=["rstd"], writes=["rstd"])
        for kc in range(c.KC):
            xb = R["xc"][kc % 4]
            tmp = R["tmp"][kc % 2]
            P.op("sp", lambda e, xb=xb, kc=kc: e.dma_start(out=xb[:, :T], in_=xTv[:, kc, t0:t0 + T]), reads=[("xT", t0, kc)], writes=[("xc", kc % 4)])
            P.op("dve", lambda e, xb=xb, tmp=tmp, kc=kc: e.scalar_tensor_tensor(out=tmp[:, :T], in0=xb[:, :T], scalar=self.GS[:, ty, sub, kc:kc + 1], in1=rs[:, :T], op0=ALU.mult, op1=ALU.mult),
                 reads=[("xc", kc % 4), "rstd", "GS"], writes=[("tmp", kc % 2)])
            P.op("act", lambda e, tmp=tmp, kc=kc: e.activation(out=dst[:, kc, :T], in_=tmp[:, :T], func=AF.Identity, bias=self.SH[:, ty, sub, kc:kc + 1]),
                 reads=[("tmp", kc % 2), "SH"], writes=[("xn", kc)])

    def alloc_dense(self, es):
        c = self.cfg
        R = {}
        R["ps"] = self.psum_banks(es)
        R["xc"] = [self.sb(es, f"xc{i}", [128, 512]) for i in range(4)]
        R["sq"] = [self.sb(es, f"sq{i}", [128, 512]) for i in range(2)]
        R["tmp"] = [self.sb(es, f"tmp{i}", [128, 512]) for i in range(2)]
        R["ob"] = [self.sb(es, f"ob{i}", [128, 512]) for i in range(2)]
        R["rstd"] = self.sb(es, "rstd", [128, 512])
        R["epsb"] = self.sb(es, "epsb", [128, 1])
        R["xn"] = self.sb(es, "xn", [128, c.KC, 512], BF16)
        return R

    def phase_ffn(self, l, which, skip_ctx=False):
        c, nc = self.cfg, self.nc
        sub = 0 if which == 0 else 2
        PC = 256
        with ExitStack() as es:
            P = Prog(nc, self.pool)
            R = self.alloc_dense(es)
            ps = R["ps"]
            h = self.sb(es, "h", [128, c.JC, 512], BF16)
            NW = 4
            wb = [self.sb(es, f"wb{i}", [128, 8192], BF16) for i in range(NW)]
            P.op("dve", lambda e: e.memset(R["epsb"][:], EPS), writes=["epsb"])
            xTv = self.xT.rearrange("(kc p) t -> p kc t", p=128)
            wi = [0]

            def wload(wkey, col0, pc):
                i = wi[0] % NW
                wi[0] += 1
                return self.wload_piece(P, wkey, col0 // pc, wb[i], ("wb", i))

            pcol = self.pc1
            p2col = self.pc2
            ty_of = lambda tile: 1 if tile[2] else 0
            def do_tile(tile, nxt, first):
                t0, T, isctx = tile
                ty = ty_of(tile)
                if first:
                    self.norm_mod(P, R, tile, sub, R["xn"])
                xn = R["xn"]
                xnkeys = [("xn", kc) for kc in range(c.KC)]
                nA = 0
                for j0 in range(0, c.DFF, pcol):
                    b1, k1 = wload(("ffn_w1", l, which), j0, pcol)
                    b3, k3 = wload(("ffn_w3", l, which), j0, pcol)
                    for q in range(pcol // 128):
                        j = j0 // 128 + q
                        p1 = ps[1 + nA % 2]
                        p3 = ps[3 + nA % 2]
                        tm = R["tmp"][nA % 2]
                        for kc in range(c.KC):
                            P.op("pe", lambda e, b1=b1, q=q, kc=kc, p1=p1: e.matmul(p1[:, :T], lhsT=b1[:, kc, q * 128:(q + 1) * 128], rhs=xn[:, kc, :T], start=(kc == 0), stop=(kc == c.KC - 1)),
                                 reads=k1 + xnkeys, writes=[("ps", 1 + nA % 2)])
                        for kc in range(c.KC):
                            P.op("pe", lambda e, b3=b3, q=q, kc=kc, p3=p3: e.matmul(p3[:, :T], lhsT=b3[:, kc, q * 128:(q + 1) * 128], rhs=xn[:, kc, :T], start=(kc == 0), stop=(kc == c.KC - 1)),
                                 reads=k3 + xnkeys, writes=[("ps", 3 + nA % 2)])
                        P.op("act", lambda e, tm=tm, p1=p1: e.activation(out=tm[:, :T], in_=p1[:, :T], func=AF.Silu), reads=[("ps", 1 + nA % 2)], writes=[("tmp", nA % 2)])
                        P.op("dve", lambda e, tm=tm, p3=p3, j=j: e.tensor_tensor(out=h[:, j, :T], in0=tm[:, :T], in1=p3[:, :T], op=ALU.mult),
                             reads=[("tmp", nA % 2), ("ps", 3 + nA % 2)], writes=[("h", j)])
                        nA += 1
                if nxt is not None:
                    self.norm_mod(P, R, nxt, sub, R["xn"])
                hkeys = [("h", j) for j in range(c.JC)]
                nB = 0
                d0s = list(range(0, c.D, p2col))
                pre = {}

                def fetch(ix):
                    if ix < len(d0s) and ix not in pre:
                        pre[ix] = wload(("ffn_w2", l, which), d0s[ix], p2col)
                fetch(0)
                fetch(1)
                for ix, d0 in enumerate(d0s):
                    fetch(ix + 2)
                    b2, k2 = pre.pop(ix)
                    for q in range(p2col // 128):
                        dc = d0 // 128 + q
                        pb = ps[5 + nB % 3]
                        for jc in range(c.JC):
                            P.op("pe", lambda e, b2=b2, q=q, jc=jc, pb=pb: e.matmul(pb[:, :T], lhsT=b2[:, jc, q * 128:(q + 1) * 128], rhs=h[:, jc, :T], start=(jc == 0), stop=(jc == c.JC - 1)),
                                 reads=k2 + hkeys, writes=[("ps", 5 + nB % 3)])
                        xb = R["xc"][nB % 4]
                        ob = R["ob"][nB % 2]
                        P.op("sp", lambda e, xb=xb, dc=dc: e.dma_start(out=xb[:, :T], in_=xTv[:, dc, t0:t0 + T]), reads=[("xT", t0, dc)], writes=[("xc", nB % 4)])
                        P.op("dve", lambda e, xb=xb, ob=ob, pb=pb, dc=dc: e.scalar_tensor_tensor(out=ob[:, :T], in0=pb[:, :T], scalar=self.GT[:, ty, sub, dc:dc + 1], in1=xb[:, :T], op0=ALU.mult, op1=ALU.add),
                             reads=[("ps", 5 + nB % 3), ("xc", nB % 4), "GT"], writes=[("ob", nB % 2)])
                        P.op("sp", lambda e, ob=ob, dc=dc: e.dma_start(out=xTv[:, dc, t0:t0 + T], in_=ob[:, :T]), reads=[("ob", nB % 2)], writes=[("xT", t0, dc)])
                        nB += 1

            tl = [t for t in c.tiles if not (skip_ctx and t[2])]
            for i, tile in enumerate(tl):
                do_tile(tile, tl[i + 1] if i + 1 < len(tl) else None, i == 0)
            P.emit()

    def phase_proj(self, l):
        c, nc = self.cfg, self.nc
        with ExitStack() as es:
            P = Prog(nc, self.pool)
            R = self.alloc_dense(es)
            ps = R["ps"]
            NW = 4
            wb = [self.sb(es, f"wbp{i}", [128, 8192], BF16) for i in range(NW)]
            stg = [self.sb(es, f"pst{i}", [128, 512]) for i in range(4)]
            P.op("dve", lambda e: e.memset(R["epsb"][:], EPS), writes=["epsb"])
            pcol = self.pc1
            wi = [0]

            def wload(col0, ncols):
                i = wi[0] % NW
                wi[0] += 1
                return self.wload_piece(P, ("w_in", l), col0 // pcol, wb[i], ("wb", i))

            def do_tile(tile):
                t0, T, isctx = tile
                self.norm_mod(P, R, tile, 1, R["xn"])
                xn = R["xn"]
                xnkeys = [("xn", kc) for kc in range(c.KC)]
                n = 0
                c0s = list(range(0, c.INW, pcol))
                pre = {}

                def fetch(ix):
                    if ix < len(c0s) and ix not in pre:
                        pre[ix] = wload(c0s[ix], min(pcol, c.INW - c0s[ix]))
                fetch(0)
                fetch(1)
                for ix, c0 in enumerate(c0s):
                    nco = min(pcol, c.INW - c0)
                    fetch(ix + 2)
                    bw, kw = pre.pop(ix)
                    for q0 in range(0, nco, 128):
                        m = min(128, nco - q0)
                        col = c0 + q0
                        pb = ps[1 + n % 4]
                        sg = stg[n % 4]
                        for kc in range(c.KC):
                            P.op("pe", lambda e, bw=bw, q0=q0, m=m, kc=kc, pb=pb: e.matmul(pb[:m, :T], lhsT=bw[:, kc, q0:q0 + m], rhs=xn[:, kc, :T], start=(kc == 0), stop=(kc == c.KC - 1)),
                                 reads=kw + xnkeys, writes=[("ps", 1 + n % 4)])
                        eng = "act" if n % 2 == 0 else "dve"
                        if eng == "act":
                            P.op("act", lambda e, sg=sg, pb=pb, m=m: e.activation(out=sg[:m, :T], in_=pb[:m, :T], func=AF.Copy), reads=[("ps", 1 + n % 4)], writes=[("pst", n % 4)])
                        else:
                            P.op("dve", lambda e, sg=sg, pb=pb, m=m: e.tensor_copy(out=sg[:m, :T], in_=pb[:m, :T]), reads=[("ps", 1 + n % 4)], writes=[("pst", n % 4)])
                        if col < self.NA:
                            dst = self.pTa[col:col + m, t0:t0 + T]
                        else:
                            dst = self.pTc[col - self.NA:col - self.NA + m, t0:t0 + T]
                        P.op("sp", lambda e, sg=sg, dst=dst, m=m: e.dma_start(out=dst, in_=sg[:m, :T]), reads=[("pst", n % 4)], writes=[("pT", col, t0)])
                        n += 1
            for tile in c.tiles:
                do_tile(tile)
            P.emit()

    def phase_merge(self, l, last):
        c, nc = self.cfg, self.nc
        with ExitStack() as es:
            P = Prog(nc, self.pool)
            R = self.alloc_dense(es)
            ps = R["ps"]
            NW = 4
            wb = [self.sb(es, f"wbm{i}", [128, 8192], BF16) for i in range(NW)]
            wbb = [self.sb(es, f"wbb{i}", [128, 2048], BF16) for i in range(4)]
            osb = self.sb(es, "osb", [128, 24, 512], BF16)
            ysb = self.sb(es, "ysb", [128, c.KC, 512], BF16)
            sgs = [self.sb(es, f"sgs{i}", [128, 512]) for i in range(2)]
            yac = [self.sb(es, f"yac{i}", [128, 512]) for i in range(2)]
            bg = self.sb(es, "bg", [128, 3, c.KC])
            P.op("dve", lambda e: e.memset(R["epsb"][:], EPS), writes=["epsb"])
            P.op("sp", lambda e: e.dma_start(out=bg[:], in_=self.b_gateT[l, :, :, :]), writes=["bg"])
            oTv = self.oT.rearrange("(r p) t -> p r t", p=128)
            xTv = self.xT.rearrange("(kc p) t -> p kc t", p=128)
            pcol = self.pc1
            wi = [0]
            wj = [0]

            def wload(wkey, col0, small=False):
                if small:
                    i = wj[0] % 4
                    wj[0] += 1
                    return self.wload_piece(P, wkey, col0 // pcol, wbb[i], ("wbb", i))
                i = wi[0] % NW
                wi[0] += 1
                return self.wload_piece(P, wkey, col0 // pcol, wb[i], ("wb", i))

            def do_tile(tile):
                t0, T, isctx = tile
                ty = 1 if isctx else 0
                self.norm_mod(P, R, tile, 1, R["xn"])
                xn = R["xn"]
                xnkeys = [("xn", kc) for kc in range(c.KC)]
                for r8 in range(0, 24, 8):
                    P.op("sp", lambda e, r8=r8: e.dma_start(out=osb[:, r8:r8 + 8, :T], in_=oTv[:, r8:r8 + 8, t0:t0 + T]), reads=[("oT", t0)], writes=[("osb", r8)])
                n = 0
                g0s = list(range(0, c.D, pcol))
                preg = {}

                def fetchg(ix):
                    if ix < len(g0s) and ix not in preg:
                        preg[ix] = ([wload(("w_gate", l, b), g0s[ix]) for b in range(3)], [wload(("w_branch", l, b), g0s[ix], small=True) for b in range(3)])
                for gx, d0 in enumerate(g0s):
                    fetchg(gx)
                    gw, bwl = preg.pop(gx)
                    for q in range(pcol // 128):
                        dc = d0 // 128 + q
                        ya = yac[n % 2]
                        for b in range(3):
                            pg = ps[1 + (n * 3 + b) % 3]
                            pt = ps[4 + (n * 3 + b) % 3]
                            gb, gk = gw[b]
                            bb, bk = bwl[b]
                            sg = sgs[(n * 3 + b) % 2]
                            for kc in range(c.KC):
                                P.op("pe", lambda e, gb=gb, q=q, kc=kc, pg=pg: e.matmul(pg[:, :T], lhsT=gb[:, kc, q * 128:(q + 1) * 128], rhs=xn[:, kc, :T], start=(kc == 0), stop=(kc == c.KC - 1)),
                                     reads=gk + xnkeys, writes=[("ps", 1 + (n * 3 + b) % 3)])
                            for kc in range(8):
                                P.op("pe", lambda e, bb=bb, q=q, kc=kc, pt=pt, b=b: e.matmul(pt[:, :T], lhsT=bb[:, kc, q * 128:(q + 1) * 128], rhs=osb[:, b * 8 + kc, :T], start=(kc == 0), stop=(kc == 7)),
                                     reads=bk + [("osb", b * 8)], writes=[("ps", 4 + (n * 3 + b) % 3)])
                            P.op("act", lambda e, sg=sg, pg=pg, b=b, dc=dc: e.activation(out=sg[:, :T], in_=pg[:, :T], func=AF.Sigmoid, bias=bg[:, b, dc:dc + 1]),
                                 reads=[("ps", 1 + (n * 3 + b) % 3), "bg"], writes=[("sgs", (n * 3 + b) % 2)])
                            if b == 0:
                                P.op("dve", lambda e, sg=sg, pt=pt, ya=ya: e.tensor_tensor(out=ya[:, :T], in0=sg[:, :T], in1=pt[:, :T], op=ALU.mult),
                                     reads=[("sgs", (n * 3 + b) % 2), ("ps", 4 + (n * 3 + b) % 3)], writes=[("yac", n % 2)])
                            else:
                                P.op("dve", lambda e, sg=sg, pt=pt: e.tensor_tensor(out=sg[:, :T], in0=sg[:, :T], in1=pt[:, :T], op=ALU.mult),
                                     reads=[("sgs", (n * 3 + b) % 2), ("ps", 4 + (n * 3 + b) % 3)], writes=[("sgs", (n * 3 + b) % 2)])
                                if b == 1:
                                    P.op("dve", lambda e, sg=sg, ya=ya: e.tensor_tensor(out=ya[:, :T], in0=ya[:, :T], in1=sg[:, :T], op=ALU.add),
                                         reads=[("sgs", (n * 3 + b) % 2), ("yac", n % 2)], writes=[("yac", n % 2)])
                                else:
                                    P.op("dve", lambda e, sg=sg, ya=ya, dc=dc: e.tensor_tensor(out=ysb[:, dc, :T], in0=ya[:, :T], in1=sg[:, :T], op=ALU.add),
                                         reads=[("sgs", (n * 3 + b) % 2), ("yac", n % 2)], writes=[("ysb", dc)])
                        n += 1
                ykeys = [("ysb", dc) for dc in range(c.KC)]
                nB = 0
                d0s = list(range(0, c.D, pcol))
                pre = {}

                def fetch(ix):
                    if ix < len(d0s) and ix not in pre:
                        pre[ix] = wload(("w_out", l), d0s[ix])
                fetch(0)
                for ix, d0 in enumerate(d0s):
                    fetch(ix + 1)
                    ow, ok = pre.pop(ix)
                    for q in range(pcol // 128):
                        dc = d0 // 128 + q
                        pb = ps[7]
                        for kc in range(c.KC):
                            P.op("pe", lambda e, ow=ow, q=q, kc=kc, pb=pb: e.matmul(pb[:, :T], lhsT=ow[:, kc, q * 128:(q + 1) * 128], rhs=ysb[:, kc, :T], start=(kc == 0), stop=(kc == c.KC - 1)),
                                 reads=ok + ykeys, writes=[("ps", 7)])
                        xb = R["xc"][nB % 4]
                        ob = R["ob"][nB % 2]
                        P.op("sp", lambda e, xb=xb, dc=dc: e.dma_start(out=xb[:, :T], in_=xTv[:, dc, t0:t0 + T]), reads=[("xT", t0, dc)], writes=[("xc", nB % 4)])
                        P.op("dve", lambda e, xb=xb, ob=ob, pb=pb, dc=dc: e.scalar_tensor_tensor(out=ob[:, :T], in0=pb[:, :T], scalar=self.GT[:, ty, 1, dc:dc + 1], in1=xb[:, :T], op0=ALU.mult, op1=ALU.add),
                             reads=[("ps", 7), ("xc", nB % 4), "GT"], writes=[("ob", nB % 2)])
                        P.op("sp", lambda e, ob=ob, dc=dc: e.dma_start(out=xTv[:, dc, t0:t0 + T], in_=ob[:, :T]), reads=[("ob", nB % 2)], writes=[("xT", t0, dc)])
                        nB += 1
            for tile in c.tiles:
                if last and tile[2]:
                    continue
                do_tile(tile)
            P.emit()


    def phase_mix(self, l, last):
        self.mix_attn(l, last)
        if not os.environ.get("NO_MLSTM"):
            self.mix_mlstm(l, last)

    def prep_qk(self, P, W, src, dst, gcol, rope, tagn):
        c = self.cfg
        ps = W["ps"]
        for bi, b0 in enumerate(range(0, c.NTOK, 512)):
            n = min(512, c.NTOK - b0)
            ld = W["ld"][bi % 2]
            sq = W["sq"][bi % 2]
            rs = W["rs"][bi % 2]
            xn = W["xn"][bi % 2]
            k = bi % 2
            P.op("sp", lambda e, ld=ld, b0=b0, n=n: e.dma_start(out=ld[:, :n], in_=src[:, b0:b0 + n]), writes=[("ld", k)])
            P.op("act", lambda e, ld=ld, sq=sq, n=n: e.activation(out=sq[:, :n], in_=ld[:, :n], func=AF.Square), reads=[("ld", k)], writes=[("sq", k)])
            P.op("pe", lambda e, sq=sq, n=n: e.matmul(ps[6][:, :n], lhsT=self.onesF[:], rhs=sq[:, :n], start=True, stop=True), reads=[("sq", k)], writes=[("ps", 6)])
            P.op("act", lambda e, rs=rs, n=n: e.activation(out=rs[:, :n], in_=ps[6][:, :n], func=AF.Sqrt, scale=1.0 / 128, bias=W["epsb"][:, 0:1]), reads=[("ps", 6), "epsb"], writes=[("rs", k)])
            P.op("dve", lambda e, rs=rs, n=n: e.reciprocal(out=rs[:, :n], in_=rs[:, :n]), reads=[("rs", k)], writes=[("rs", k)])
            if rope:
                cs = W["cs"][bi % 2]
                sn = W["sn"][bi % 2]
                P.op("dve", lambda e, ld=ld, rs=rs, xn=xn, n=n: e.scalar_tensor_tensor(out=xn[:, :n], in0=ld[:, :n], scalar=gcol, in1=rs[:, :n], op0=ALU.mult, op1=ALU.mult),
                     reads=[("ld", k), ("rs", k), "qkg"], writes=[("xn", k)])
                P.op("sp", lambda e, cs=cs, b0=b0, n=n: e.dma_start(out=cs[:, :n], in_=self.cosT[:, b0:b0 + n]), writes=[("cs", k)])
                P.op("sp", lambda e, sn=sn, b0=b0, n=n: e.dma_start(out=sn[:, :n], in_=self.sinT[:, b0:b0 + n]), writes=[("sn", k)])
                P.op("pe", lambda e, xn=xn, n=n: e.matmul(ps[7][:, :n], lhsT=W["Rm"][:], rhs=xn[:, :n], start=True, stop=True), reads=[("xn", k), "Rm"], writes=[("ps", 7)])
                P.op("dve", lambda e, sn=sn, n=n: e.tensor_tensor(out=sn[:, :n], in0=ps[7][:, :n], in1=sn[:, :n], op=ALU.mult), reads=[("ps", 7), ("sn", k)], writes=[("sn", k)])
                P.op("dve", lambda e, cs=cs, xn=xn, n=n: e.tensor_tensor(out=cs[:, :n], in0=xn[:, :n], in1=cs[:, :n], op=ALU.mult), reads=[("xn", k), ("cs", k)], writes=[("cs", k)])
                P.op("dve", lambda e, cs=cs, sn=sn, b0=b0, n=n: e.tensor_tensor(out=dst[:, b0:b0 + n], in0=cs[:, :n], in1=sn[:, :n], op=ALU.add), reads=[("cs", k), ("sn", k)], writes=[(tagn, bi)])
            else:
                P.op("dve", lambda e, ld=ld, rs=rs, b0=b0, n=n: e.scalar_tensor_tensor(out=dst[:, b0:b0 + n], in0=ld[:, :n], scalar=gcol, in1=rs[:, :n], op0=ALU.mult, op1=ALU.mult),
                     reads=[("ld", k), ("rs", k), "qkg"], writes=[(tagn, bi)])
        return [(tagn, bi) for bi in range((c.NTOK + 511) // 512)]

    def prep_tok(self, P, W, src, dst, tagn, scale_cols=None):
        c = self.cfg
        ps = W["ps"]
        for bi, b0 in enumerate(range(0, c.NTOK, 512)):
            n = min(512, c.NTOK - b0)
            k = bi % 2
            ld = W["ld"][k]
            P.op("sp", lambda e, ld=ld, b0=b0, n=n: e.dma_start(out=ld[:, :n], in_=src[:, b0:b0 + n]), writes=[("ld", k)])
            nt = n // 128
            for q in range(nt):
                P.op("pe", lambda e, ld=ld, q=q: e.transpose(out=ps[7][:, q * 128:(q + 1) * 128], in_=ld[:, q * 128:(q + 1) * 128], identity=self.identS[:]),
                     reads=[("ld", k)], writes=[("ps", 7)])
            P.op("act", lambda e, b0=b0, nt=nt: e.activation(out=dst[:, b0 // 128:b0 // 128 + nt, :], in_=ps[7][:, :nt * 128].rearrange("p (a b) -> p a b", b=128), func=AF.Copy),
                 reads=[("ps", 7)], writes=[(tagn, bi)])
        return [(tagn, bi) for bi in range((c.NTOK + 511) // 512)]

    def attn(self, P, W, qb, kb, vb, rkeys, sched, sinkcol, ob):
        ps = W["ps"]
        scale = 128 ** -0.5
        steps = []
        for qi, (qt, klist) in enumerate(sched):
            nk = len(klist)
            for ki, (kt, bias, bkey) in enumerate(klist):
                steps.append((qi, qt, ki, nk, kt, bias, bkey))

        def front(cnt, st):
            qi, qt, ki, nk, kt, bias, bkey = st
            pss = ps[cnt % 2]
            pt = W["pt"][cnt % 3]
            P.op("pe", lambda e: e.matmul(pss[:, :128], lhsT=kb[:, kt * 128:(kt + 1) * 128], rhs=qb[:, qt * 128:(qt + 1) * 128], start=True, stop=True),
                 reads=rkeys, writes=[("ps", cnt % 2)])
            if bias is None:
                P.op("act", lambda e: e.activation(out=pt[:], in_=pss[:, :128], func=AF.Exp, scale=scale), reads=[("ps", cnt % 2)], writes=[("pt", cnt % 3)])
            else:
                tb = W["tb"][cnt % 2]
                P.op("dve", lambda e: e.scalar_tensor_tensor(out=tb[:], in0=pss[:, :128], scalar=scale, in1=bias, op0=ALU.mult, op1=ALU.add),
                     reads=[("ps", cnt % 2), bkey], writes=[("tb", cnt % 2)])
                P.op("act", lambda e: e.activation(out=pt[:], in_=tb[:], func=AF.Exp), reads=[("tb", cnt % 2)], writes=[("pt", cnt % 3)])

        def back(cnt, st):
            qi, qt, ki, nk, kt, bias, bkey = st
            po = ps[2 + qi % 2]
            pd = ps[4 + qi % 2]
            pt = W["pt"][cnt % 3]
            P.op("pe", lambda e: e.matmul(po[:, :128], lhsT=vb[:, kt, :], rhs=pt[:], start=(ki == 0), stop=(ki == nk - 1)),
                 reads=rkeys + [("pt", cnt % 3)], writes=[("ps", 2 + qi % 2)])
            P.op("pe", lambda e: e.matmul(pd[:, :128], lhsT=W["onesB"][:], rhs=pt[:], start=(ki == 0), stop=(ki == nk - 1)),
                 reads=[("pt", cnt % 3), "onesB"], writes=[("ps", 4 + qi % 2)])
            if ki == nk - 1:
                rc = W["rc"][qi % 2]
                if sinkcol is not None:
                    P.op("dve", lambda e: e.tensor_scalar(out=rc[:], in0=pd[:, :128], scalar1=sinkcol, scalar2=None, op0=ALU.add), reads=[("ps", 4 + qi % 2), "sink"], writes=[("rc", qi % 2)])
                    P.op("dve", lambda e: e.reciprocal(out=rc[:], in_=rc[:]), reads=[("rc", qi % 2)], writes=[("rc", qi % 2)])
                else:
                    P.op("dve", lambda e: e.reciprocal(out=rc[:], in_=pd[:, :128]), reads=[("ps", 4 + qi % 2)], writes=[("rc", qi % 2)])
                P.op("dve", lambda e: e.tensor_tensor(out=ob[:, qt * 128:(qt + 1) * 128], in0=po[:, :128], in1=rc[:], op=ALU.mult),
                     reads=[("ps", 2 + qi % 2), ("rc", qi % 2)], writes=[("ob", qt)])

        n = len(steps)
        for i in range(n + 1):
            if i < n:
                front(i, steps[i])
            if i >= 1:
                back(i - 1, steps[i - 1])

    def mix_attn(self, l, last):
        c, nc = self.cfg, self.nc
        MCN = self.mc
        ncase = len(MCN["case_list"])
        NT = c.NTOK
        ntile = NT // 128
        nctx = c.CTX // 128
        nblk = c.SEQ // 128
        with ExitStack() as es:
            P = Prog(nc, self.pool)
            W = {}
            W["ps"] = self.psum_banks(es)
            for nm in ("ld", "sq", "rs", "xn", "cs", "sn"):
                W[nm] = [self.sb(es, f"{nm}{i}", [128, 512]) for i in range(2)]
            W["epsb"] = self.sb(es, "epsb", [128, 1])
            W["Rm"] = self.sb(es, "Rm", [128, 128])
            W["onesB"] = self.sb(es, "onesB", [128, 128], BF16)
            W["pt"] = [self.sb(es, f"pt{i}", [128, 128], BF16) for i in range(3)]
            W["tb"] = [self.sb(es, f"tb{i}", [128, 128]) for i in range(2)]
            W["rc"] = [self.sb(es, f"rc{i}", [128, 128]) for i in range(2)]
            MA = self.sb(es, "MA", [128, 2, 128])
            mnegB = self.sb(es, "mnegB", [128, ncase, 128])
            RBM = self.sb(es, "RBM", [128, ncase, 128])
            qkg = self.sb(es, "qkg", [128, 4])
            snk = self.sb(es, "snk", [128, 8])
            qb = self.sb(es, "qb", [128, NT], BF16)
            kb = self.sb(es, "kb", [128, NT], BF16)
            vb = self.sb(es, "vb", [128, ntile, 128], BF16)
            ob = self.sb(es, "ob", [128, NT], BF16)
            P.op("dve", lambda e: e.memset(W["epsb"][:], EPS), writes=["epsb"])
            P.op("dve", lambda e: e.memset(W["onesB"][:], 1.0), writes=["onesB"])
            P.op("sp", lambda e: e.dma_start(out=W["Rm"][:], in_=self.RmD[:, :]), writes=["Rm"])
            P.op("sp", lambda e: e.dma_start(out=MA[:], in_=self.MAD[:, :, :]), writes=["MA"])
            P.op("sp", lambda e: e.dma_start(out=mnegB[:], in_=self.mnegBD[:, :, :]), writes=["mnegB"])
            P.op("sp", lambda e: e.dma_start(out=qkg[:], in_=self.qk_gT[l, :, :]), writes=["qkg"])
            P.op("sp", lambda e: e.dma_start(out=snk[:], in_=self.sinkB[l, :, :]), writes=["snk0"])
            P.op("act", lambda e: e.activation(out=snk[:], in_=snk[:], func=AF.Exp), reads=["snk0"], writes=["sink"])
            qtiles_ctx = [] if last else list(range(nctx))
            for kv in range(2):
                kk = self.prep_qk(P, W, self.pTa[1024 + kv * 128:1024 + (kv + 1) * 128, :], kb, qkg[:, 1:2], True, "kb")
                vk = self.prep_tok(P, W, self.pTa[1280 + kv * 128:1280 + (kv + 1) * 128, :], vb, "vb")
                for g in range(4):
                    h = kv * 4 + g
                    qk = self.prep_qk(P, W, self.pTa[h * 128:(h + 1) * 128, :], qb, qkg[:, 0:1], True, "qb")
                    sched = []
                    for qt in qtiles_ctx:
                        sched.append((qt, [(kt, None, None) for kt in range(nctx)]))
                    for n in range(nblk):
                        kl = [(kt, None, None) for kt in range(nctx)]
                        if n > 0:
                            kl.append((nctx + n - 1, MA[:, 0, :], "MA"))
                        kl.append((nctx + n, None, None))
                        if n < nblk - 1:
                            kl.append((nctx + n + 1, MA[:, 1, :], "MA"))
                        sched.append((nctx + n, kl))
                    self.attn(P, W, qb, kb, vb, kk + vk + qk, sched, snk[:, h:h + 1], ob)
                    t0o = 0 if not last else c.CTX
                    P.op("sp", lambda e, h=h, t0o=t0o: e.dma_start(out=self.oT[h * 128:(h + 1) * 128, t0o:], in_=ob[:, t0o:]),
                         reads=[("ob", qt) for qt, _ in sched], writes=[("oT", "A", h)])
            for h in range(8):
                P.op("sp", lambda e, h=h: e.dma_start(out=RBM[:], in_=self.rbD[l, h, :, :, :]), writes=["RBM"])
                P.op("dve", lambda e: e.tensor_tensor(out=RBM[:], in0=RBM[:], in1=mnegB[:], op=ALU.add), reads=["RBM", "mnegB"], writes=["RBM"])
                kk = self.prep_qk(P, W, self.pTa[2560 + h * 128:2560 + (h + 1) * 128, :], kb, qkg[:, 3:4], False, "kb")
                vk = self.prep_tok(P, W, self.pTa[3584 + h * 128:3584 + (h + 1) * 128, :], vb, "vb")
                qk = self.prep_qk(P, W, self.pTa[1536 + h * 128:1536 + (h + 1) * 128, :], qb, qkg[:, 2:3], False, "qb")
                sched = []
                for qt in qtiles_ctx:
                    sched.append((qt, [(kt, None, None) for kt in range(nctx)]))
                for n in range(nblk):
                    kl = [(kt, None, None) for kt in range(nctx)]
                    for (a, cs_) in MCN["sched"][n]:
                        kl.append((nctx + a, RBM[:, cs_, :], "RBM"))
                    sched.append((nctx + n, kl))
                self.attn(P, W, qb, kb, vb, kk + vk + qk, sched, None, ob)
                t0o = 0 if not last else c.CTX
                P.op("sp", lambda e, h=h, t0o=t0o: e.dma_start(out=self.oT[1024 + h * 128:1024 + (h + 1) * 128, t0o:], in_=ob[:, t0o:]),
                     reads=[("ob", qt) for qt, _ in sched], writes=[("oT", "B", h)])
            P.emit()


    def mix_mlstm(self, l, last):
        c, nc = self.cfg, self.nc
        NT = c.NTOK
        ntile = NT // 128
        NCH = NT // 64
        ncc = c.CTX // 64
        nct = c.CTX // 128
        isq = 128 ** -0.5
        blocks = [(0, c.CTX)] if c.CTX <= 512 else [(b, min(512, c.CTX - b)) for b in range(0, c.CTX, 512)]
        blocks += [(b, min(512, NT - b)) for b in range(c.CTX, NT, 512)]
        with ExitStack() as es:
            P = Prog(nc, self.pool)
            ps = self.psum_banks(es)
            T = {}
            for nm in ("ta", "tb", "tc", "td", "te", "tf", "tg", "th"):
                T[nm] = self.sb(es, "m" + nm, [128, 516])
            epsb = self.sb(es, "epsb", [128, 1])
            lnsc = self.sb(es, "lnsc", [128, 1])
            MC = self.sb(es, "MC", [128, 2, 128])
            mres = self.sb(es, "mres", [128, 512])
            negf = self.sb(es, "negf", [128, 512])
            negb = self.sb(es, "negb", [128, 512])
            cw = self.sb(es, "cw", [128, 16, 5])
            cb = self.sb(es, "cb", [128, 16])
            gb = self.sb(es, "gb", [128, 32])
            ngb = self.sb(es, "ngb", [128, 32])
            mng = self.sb(es, "mng", [128, 8])
            CUM = self.sb(es, "CUM", [128, NT])
            CMX = self.sb(es, "CMX", [128, NT])
            qcT = self.sb(es, "qcT", [128, NT], BF16)
            kcT = self.sb(es, "kcT", [128, NT], BF16)
            ktok = self.sb(es, "ktok", [128, ntile, 128], BF16)
            vaug = self.sb(es, "vaug", [128, ntile, 130], BF16)
            obC = self.sb(es, "obC", [128, NT], BF16)
            cols = {nm: self.sb(es, nm, [128, ntile]) for nm in ("wcol", "a2col", "sicol", "emcol")}
            ch = {nm: self.sb(es, nm, [128, NCH]) for nm in ("BL", "MLOC", "MAF", "M0", "SP", "SL", "tch")}
            Cst = self.sb(es, "Cst", [128, 130])
            C0c = [self.sb(es, f"C0c{i}", [128, 130], BF16) for i in range(4)]
            ctmp = [self.sb(es, f"ctmp{i}", [128, 130]) for i in range(2)]
            kw = [self.sb(es, f"kw{i}", [128, 128], BF16) for i in range(2)]
            et = [self.sb(es, f"et{i}", [128, 128]) for i in range(2)]
            swT = [self.sb(es, f"swT{i}", [128, 128], BF16) for i in range(2)]
            Hs = [self.sb(es, f"Hs{i}", [128, 130]) for i in range(2)]
            hf = [self.sb(es, f"hf{i}", [128, 128]) for i in range(2)]
            hn = [self.sb(es, f"hn{i}", [128, 128]) for i in range(2)]
            sc1 = [self.sb(es, f"sc1{i}", [128, 2]) for i in range(2)]
            sgo = [self.sb(es, f"sgo{i}", [128, 128]) for i in range(2)]
            ident = self.identS
            hscr = self.hscr
            ldv = [self.sb(es, f"ldv{i}", [128, 512]) for i in range(2)]
            P.op("dve", lambda e: e.memset(epsb[:], EPS), writes=["epsb"])
            P.op("dve", lambda e: e.memset(lnsc[:], float(np.log(isq))), writes=["lnsc"])
            P.op("dve", lambda e: e.memset(vaug[:, :, 128:130], 1.0), writes=["vones"])
            P.op("sp", lambda e: e.dma_start(out=MC[:], in_=self.MCD[:, :, :]), writes=["MC"])
            P.op("sp", lambda e: e.dma_start(out=mres[:], in_=self.mresD[:, 0:512]), writes=["mres"])
            P.op("sp", lambda e: e.dma_start(out=negf[:], in_=self.negfD[:, 0:512]), writes=["negf"])
            P.op("sp", lambda e: e.dma_start(out=negb[:], in_=self.negbD[:, 0:512]), writes=["negb"])
            P.op("sp", lambda e: e.dma_start(out=cw[:], in_=self.conv_wT[l, :, :, :]), writes=["cw"])
            P.op("sp", lambda e: e.dma_start(out=cb[:], in_=self.conv_bT[l, :, :]), writes=["cb"])
            P.op("sp", lambda e: e.dma_start(out=gb[:], in_=self.gate_bB[l, :, :]), writes=["gb"])
            P.op("dve", lambda e: e.tensor_scalar(out=ngb[:], in0=gb[:], scalar1=-1.0, scalar2=None, op0=ALU.mult), reads=["gb"], writes=["ngb"])
            P.op("sp", lambda e: e.dma_start(out=mng[:], in_=self.mngT[l, :, :]), writes=["mng"])

            def diag(X, n, dst, t0):
                nt = n // 128
                tmp = T["th"]
                P.op("dve", lambda e: e.tensor_tensor(out=tmp[:, :n].rearrange("p (a b) -> p a b", b=128), in0=X[:, :n].rearrange("p (a b) -> p a b", b=128),
                                                      in1=ident[:].unsqueeze(1).to_broadcast([128, nt, 128]), op=ALU.mult), reads=["X", "ident"], writes=["th"])
                P.op("dve", lambda e: e.tensor_reduce(out=dst[:, t0:t0 + nt], in_=tmp[:, :n].rearrange("p (a b) -> p a b", b=128), axis=AX.X, op=ALU.add), reads=["th"], writes=["cols"])

            def do_head(h):
                def do_conv(which, dstT):
                    src = self.pTc[which * 1024 + h * 128:which * 1024 + (h + 1) * 128, :]
                    ci = which * 8 + h
                    for (b0, n) in blocks:
                        seg0, seg1 = (0, c.CTX) if b0 < c.CTX else (c.CTX, NT)
                        lo = max(seg0, b0 - 2)
                        hi = min(seg1, b0 + n + 2)
                        ld = T["ta"]
                        acc = T["tb"]
                        P.op("dve", lambda e, ld=ld: e.memset(ld[:], 0.0), writes=["ta"])
                        P.op("sp", lambda e, ld=ld, lo=lo, hi=hi, b0=b0: e.dma_start(out=ld[:, lo - (b0 - 2):hi - (b0 - 2)], in_=src[:, lo:hi]), writes=["ta"])
                        P.op("dve", lambda e, ld=ld, acc=acc, n=n, ci=ci: e.tensor_scalar(out=acc[:, :n], in0=ld[:, 0:n], scalar1=cw[:, ci, 0:1], scalar2=cb[:, ci:ci + 1], op0=ALU.mult, op1=ALU.add),
                             reads=["ta", "cw", "cb"], writes=["tb"])
                        for j in range(1, 5):
                            P.op("dve", lambda e, ld=ld, acc=acc, n=n, ci=ci, j=j: e.scalar_tensor_tensor(out=acc[:, :n], in0=ld[:, j:j + n], scalar=cw[:, ci, j:j + 1], in1=acc[:, :n], op0=ALU.mult, op1=ALU.add),
                                 reads=["ta", "tb", "cw"], writes=["tb"])
                        P.op("act", lambda e, acc=acc, b0=b0, n=n: e.activation(out=dstT[:, b0:b0 + n], in_=acc[:, :n], func=AF.Silu), reads=["tb"], writes=[("qk", which)])
                do_conv(0, qcT)
                do_conv(1, kcT)
                for ti in range(ntile):
                    kf = T["tc"]
                    P.op("dve", lambda e, ti=ti: e.tensor_copy(out=kf[:, :128], in_=kcT[:, ti * 128:(ti + 1) * 128]), reads=[("qk", 1)], writes=["tc"])
                    P.op("pe", lambda e: e.transpose(out=ps[7][:, :128], in_=kf[:, :128], identity=ident[:]), reads=["tc"], writes=[("ps", 7)])
                    P.op("act", lambda e, ti=ti: e.activation(out=ktok[:, ti, :], in_=ps[7][:, :128], func=AF.Copy), reads=[("ps", 7)], writes=["ktok"])
                W = {"ps": ps, "ld": ldv}
                self.prep_tok(P, W, self.pTc[2048 + h * 128:2048 + (h + 1) * 128, :], vaug[:, :, 0:128], "vaugw")
                P.op("dve", lambda e: e.tensor_copy(out=T["tc"][:, 0:1], in_=T["tc"][:, 0:1]), reads=[("vaugw", bi) for bi in range((NT + 511) // 512)] + ["vones"], writes=["vaug"])

                def do_dir(dr):
                    gi_row = 4096 + (2 * dr) * 8 + h
                    gf_row = 4096 + (2 * dr + 1) * 8 + h
                    gi_c = (2 * dr) * 8 + h
                    gf_c = (2 * dr + 1) * 8 + h
                    for (b0, n) in blocks:
                        nchb = n // 64
                        c0 = b0 // 64
                        LI, LF, CP, A_, TMP = T["ta"], T["tb"], T["tc"], T["td"], T["te"]
                        P.op("sp", lambda e, b0=b0, n=n: e.dma_start(out=LI[:, :n], in_=self.pTc[gi_row:gi_row + 1, b0:b0 + n].to_broadcast([128, n])), writes=["ta"])
                        P.op("sp", lambda e, b0=b0, n=n: e.dma_start(out=LF[:, :n], in_=self.pTc[gf_row:gf_row + 1, b0:b0 + n].to_broadcast([128, n])), writes=["tb"])
                        P.op("act", lambda e, n=n: e.activation(out=LI[:, :n], in_=LI[:, :n], func=AF.Identity, bias=gb[:, gi_c:gi_c + 1]), reads=["ta", "gb"], writes=["ta"])
                        P.op("act", lambda e, n=n: e.activation(out=LF[:, :n], in_=LF[:, :n], func=AF.Exp, scale=-1.0, bias=ngb[:, gf_c:gf_c + 1]), reads=["tb", "ngb"], writes=["tb"])
                        P.op("dve", lambda e, n=n: e.tensor_scalar(out=LF[:, :n], in0=LF[:, :n], scalar1=1.0, scalar2=None, op0=ALU.add), reads=["tb"], writes=["tb"])
                        P.op("act", lambda e, n=n: e.activation(out=LF[:, :n], in_=LF[:, :n], func=AF.Ln), reads=["tb"], writes=["tb"])
                        P.op("dve", lambda e, n=n: e.tensor_scalar(out=LF[:, :n], in0=LF[:, :n], scalar1=-1.0, scalar2=None, op0=ALU.mult), reads=["tb"], writes=["tb"])
                        P.op("dve", lambda e, n=n: e.tensor_tensor_scan(out=CP[:, :n], data0=mres[:, :n], data1=LF[:, :n], initial=0.0, op0=ALU.mult, op1=ALU.add), reads=["tb", "mres"], writes=["tc"])
                        P.op("dve", lambda e, n=n, c0=c0, nchb=nchb: e.tensor_copy(out=ch["BL"][:, c0:c0 + nchb], in_=CP[:, :n].rearrange("p (a b) -> p a b", b=64)[:, :, 63]), reads=["tc"], writes=["BL"])
                        blbc = lambda c0=c0, nchb=nchb: ch["BL"][:, c0:c0 + nchb].unsqueeze(2).to_broadcast([128, nchb, 64])
                        v3 = lambda X, n=n: X[:, :n].rearrange("p (a b) -> p a b", b=64)
                        cumb = CUM[:, b0:b0 + n]
                        if dr == 0:
                            P.op("dve", lambda e, n=n, cumb=cumb: e.tensor_copy(out=cumb, in_=CP[:, :n]), reads=["tc"], writes=["CUM"])
                        else:
                            P.op("dve", lambda e, n=n, blbc=blbc, v3=v3: e.tensor_tensor(out=v3(TMP), in0=blbc(), in1=v3(CP), op=ALU.subtract), reads=["tc", "BL"], writes=["te"])
                            P.op("dve", lambda e, n=n, cumb=cumb: e.tensor_tensor(out=cumb, in0=TMP[:, :n], in1=LF[:, :n], op=ALU.add), reads=["te", "tb"], writes=["CUM"])
                        P.op("dve", lambda e, n=n, cumb=cumb: e.tensor_tensor(out=TMP[:, :n], in0=LI[:, :n], in1=cumb, op=ALU.subtract), reads=["ta", "CUM"], writes=["te"])
                        P.op("dve", lambda e, n=n, blbc=blbc, v3=v3: e.tensor_tensor(out=v3(A_), in0=v3(TMP), in1=blbc(), op=ALU.add), reads=["te", "BL"], writes=["td"])
                        P.op("dve", lambda e, n=n, c0=c0, nchb=nchb, v3=v3: e.tensor_reduce(out=ch["MLOC"][:, c0:c0 + nchb], in_=v3(A_), axis=AX.X, op=ALU.max), reads=["td"], writes=["MLOC"])
                        P.op("dve", lambda e, n=n, c0=c0, nchb=nchb, v3=v3: e.tensor_tensor(out=v3(A_), in0=v3(A_), in1=ch["MLOC"][:, c0:c0 + nchb].unsqueeze(2).to_broadcast([128, nchb, 64]), op=ALU.subtract),
                             reads=["td", "MLOC"], writes=["td"])
                        P.op("act", lambda e, n=n: e.activation(out=A_[:, :n], in_=A_[:, :n], func=AF.Exp), reads=["td"], writes=["X"])
                        diag(A_, n, cols["wcol"], b0 // 128)
                        P.op("dve", lambda e, n=n: e.tensor_copy(out=A_[:, :n], in_=TMP[:, :n]), reads=["te", "cols", "th"], writes=["X"])
                        diag(A_, n, cols["a2col"], b0 // 128)
                        cmxb = CMX[:, b0:b0 + n]
                        if dr == 0:
                            P.op("dve", lambda e, n=n, cmxb=cmxb: e.tensor_tensor_scan(out=cmxb, data0=negf[:, :n], data1=TMP[:, :n], initial=-1e30, op0=ALU.add, op1=ALU.max), reads=["te", "negf"], writes=["CMX"])
                        else:
                            P.op("dve", lambda e, n=n, b0=b0: e.tensor_tensor_scan(out=CMX[:, b0 + n - 1:b0 - 1 if b0 > 0 else None:-1], data0=negb[:, n - 1::-1], data1=TMP[:, n - 1::-1], initial=-1e30, op0=ALU.add, op1=ALU.max),
                                 reads=["te", "negb"], writes=["CMX"])
                    BL, MLOC, MAF, M0, SP, SL, tch = (ch[k] for k in ("BL", "MLOC", "MAF", "M0", "SP", "SL", "tch"))
                    if dr == 0:
                        P.op("dve", lambda e: e.tensor_tensor_scan(out=MAF[:, :], data0=BL[:, :], data1=MLOC[:, :], initial=0.0, op0=ALU.add, op1=ALU.max), reads=["BL", "MLOC"], writes=["MAF"])
                        P.op("dve", lambda e: e.memset(M0[:, 0:1], 0.0), writes=["M0"])
                        P.op("dve", lambda e: e.tensor_copy(out=M0[:, 1:NCH], in_=MAF[:, 0:NCH - 1]), reads=["MAF"], writes=["M0"])
                    else:
                        P.op("dve", lambda e: e.tensor_tensor_scan(out=MAF[:, ncc - 1::-1], data0=BL[:, ncc - 1::-1], data1=MLOC[:, ncc - 1::-1], initial=0.0, op0=ALU.add, op1=ALU.max), reads=["BL", "MLOC"], writes=["MAF"])
                        P.op("dve", lambda e: e.tensor_tensor_scan(out=MAF[:, NCH - 1:ncc - 1:-1], data0=BL[:, NCH - 1:ncc - 1:-1], data1=MLOC[:, NCH - 1:ncc - 1:-1], initial=MAF[:, 0:1], op0=ALU.add, op1=ALU.max),
                             reads=["BL", "MLOC", "MAF"], writes=["MAF"])
                        P.op("dve", lambda e: e.memset(M0[:, ncc - 1:ncc], 0.0), writes=["M0"])
                        if ncc > 1:
                            P.op("dve", lambda e: e.tensor_copy(out=M0[:, 0:ncc - 1], in_=MAF[:, 1:ncc]), reads=["MAF"], writes=["M0"])
                        P.op("dve", lambda e: e.tensor_copy(out=M0[:, ncc:NCH - 1], in_=MAF[:, ncc + 1:NCH]), reads=["MAF"], writes=["M0"])
                        P.op("dve", lambda e: e.tensor_copy(out=M0[:, NCH - 1:NCH], in_=MAF[:, 0:1]), reads=["MAF"], writes=["M0"])
                    P.op("dve", lambda e: e.tensor_tensor(out=tch[:, :], in0=BL[:, :], in1=M0[:, :], op=ALU.add), reads=["BL", "M0"], writes=["tch"])
                    P.op("dve", lambda e: e.tensor_tensor(out=tch[:, :], in0=tch[:, :], in1=MAF[:, :], op=ALU.subtract), reads=["tch", "MAF"], writes=["tch"])
                    P.op("act", lambda e: e.activation(out=SP[:, :], in_=tch[:, :], func=AF.Exp), reads=["tch"], writes=["SP"])
                    P.op("dve", lambda e: e.tensor_tensor(out=tch[:, :], in0=MLOC[:, :], in1=MAF[:, :], op=ALU.subtract), reads=["MLOC", "MAF", "SP"], writes=["tch"])
                    P.op("act", lambda e: e.activation(out=SL[:, :], in_=tch[:, :], func=AF.Exp), reads=["tch"], writes=["SL"])
                    for (b0, n) in blocks:
                        nchb = n // 64
                        c0 = b0 // 64
                        Z, X1, X2 = T["ta"], T["tb"], T["td"]
                        v3 = lambda X, n=n: X[:, :n].rearrange("p (a b) -> p a b", b=64)
                        m0bc = lambda c0=c0, nchb=nchb: M0[:, c0:c0 + nchb].unsqueeze(2).to_broadcast([128, nchb, 64])
                        cmxb = CMX[:, b0:b0 + n]
                        cumb = CUM[:, b0:b0 + n]
                        P.op("dve", lambda e, n=n, cmxb=cmxb, m0bc=m0bc, v3=v3: e.tensor_tensor(out=v3(Z), in0=cmxb.rearrange("p (a b) -> p a b", b=64), in1=m0bc(), op=ALU.max), reads=["CMX", "M0"], writes=["ta"])
                        P.op("dve", lambda e, n=n, m0bc=m0bc, v3=v3: e.tensor_tensor(out=v3(X1), in0=m0bc(), in1=v3(Z), op=ALU.subtract), reads=["ta", "M0"], writes=["tb"])
                        P.op("act", lambda e, n=n: e.activation(out=X2[:, :n], in_=X1[:, :n], func=AF.Exp, bias=lnsc[:, 0:1]), reads=["tb", "lnsc"], writes=["X"])
                        diag(X2, n, cols["sicol"], b0 // 128)
                        P.op("dve", lambda e, n=n, cumb=cumb: e.tensor_tensor(out=X1[:, :n], in0=cumb, in1=Z[:, :n], op=ALU.add), reads=["ta", "CUM", "cols", "th"], writes=["tb"])
                        P.op("act", lambda e, n=n: e.activation(out=X2[:, :n], in_=X1[:, :n], func=AF.Exp, scale=-1.0), reads=["tb"], writes=["X"])
                        diag(X2, n, cols["emcol"], b0 // 128)
                        P.op("dve", lambda e, n=n, cmxb=cmxb: e.tensor_scalar(out=cmxb, in0=Z[:, :n], scalar1=-1.0, scalar2=None, op0=ALU.mult), reads=["ta", "cols"], writes=["ROWP"])
                    if dr == 0:
                        order = list(range(ntile))
                    else:
                        order = list(range(nct - 1, -1, -1)) + list(range(ntile - 1, nct - 1, -1))
                    P.op("dve", lambda e: e.memset(Cst[:], 0.0), writes=["Cst"])
                    for oi, ti in enumerate(order):
                        isctx = ti < nct
                        chunks = (2 * ti, 2 * ti + 1) if dr == 0 else (2 * ti + 1, 2 * ti)
                        kwt = kw[oi % 2]
                        P.op("dve", lambda e, kwt=kwt, ti=ti: e.tensor_scalar(out=kwt[:], in0=ktok[:, ti, :], scalar1=cols["wcol"][:, ti:ti + 1], scalar2=None, op0=ALU.mult),
                             reads=["ktok", "cols"], writes=[("kw", oi % 2)])
                        c0s = []
                        for cc in chunks:
                            hb = (cc % 2) * 64
                            c0b = C0c[(2 * oi + (cc % 2)) % 4]
                            c0s.append((hb, c0b, (2 * oi + (cc % 2)) % 4))
                            P.op("dve", lambda e, c0b=c0b: e.tensor_copy(out=c0b[:, 0:129], in_=Cst[:, 0:129]), reads=["Cst"], writes=[("C0c", (2 * oi + (cc % 2)) % 4)])
                            P.op("pe", lambda e, kwt=kwt, hb=hb, ti=ti: e.matmul(ps[6][:, 0:129], lhsT=kwt[hb:hb + 64, :], rhs=vaug[hb:hb + 64, ti, 0:129], start=True, stop=True),
                                 reads=[("kw", oi % 2), "vaug"], writes=[("ps", 6)])
                            ct = ctmp[cc % 2]
                            P.op("act", lambda e, ct=ct, cc=cc: e.activation(out=ct[:, 0:129], in_=ps[6][:, 0:129], func=AF.Copy, scale=SL[:, cc:cc + 1]), reads=[("ps", 6), "SL"], writes=[("ctmp", cc % 2)])
                            P.op("dve", lambda e, ct=ct, cc=cc: e.scalar_tensor_tensor(out=Cst[:, 0:129], in0=Cst[:, 0:129], scalar=SP[:, cc:cc + 1], in1=ct[:, 0:129], op0=ALU.mult, op1=ALU.add),
                                 reads=[("ctmp", cc % 2), "SP", "Cst"], writes=["Cst"])
                        if isctx and last:
                            continue
                        t0 = ti * 128
                        pS = ps[oi % 2]
                        pN = ps[2 + oi % 2]
                        pI = ps[4 + oi % 2]
                        e_t = et[oi % 2]
                        sw = swT[oi % 2]
                        P.op("pe", lambda e, pS=pS, t0=t0: e.matmul(pS[:, :128], lhsT=kcT[:, t0:t0 + 128], rhs=qcT[:, t0:t0 + 128], start=True, stop=True), reads=[("qk", 0), ("qk", 1)], writes=[("ps", oi % 2)])
                        P.op("dve", lambda e, e_t=e_t, t0=t0, ti=ti: e.scalar_tensor_tensor(out=e_t[:], in0=CMX[:, t0:t0 + 128], scalar=cols["a2col"][:, ti:ti + 1], in1=MC[:, dr, :], op0=ALU.add, op1=ALU.add),
                             reads=["ROWP", "cols", "MC"], writes=[("et", oi % 2)])
                        P.op("act", lambda e, e_t=e_t: e.activation(out=e_t[:], in_=e_t[:], func=AF.Exp), reads=[("et", oi % 2)], writes=[("et", oi % 2)])
                        P.op("dve", lambda e, e_t=e_t, sw=sw, pS=pS: e.scalar_tensor_tensor(out=sw[:], in0=pS[:, :128], scalar=isq, in1=e_t[:], op0=ALU.mult, op1=ALU.mult),
                             reads=[("ps", oi % 2), ("et", oi % 2)], writes=[("sw", oi % 2)])
                        P.op("pe", lambda e, pN=pN, sw=sw, ti=ti: e.matmul(pN[:, 0:129], lhsT=sw[:], rhs=vaug[:, ti, 0:129], start=True, stop=True), reads=[("sw", oi % 2), "vaug"], writes=[("ps", 2 + oi % 2)])
                        for (hb, c0b, ck) in c0s:
                            P.op("pe", lambda e, pI=pI, hb=hb, c0b=c0b, t0=t0: e.matmul(pI[hb:hb + 64, 0:129], lhsT=qcT[:, t0 + hb:t0 + hb + 64], rhs=c0b[:, 0:129], start=True, stop=True),
                                 reads=[("qk", 0), ("C0c", ck)], writes=[("ps", 4 + oi % 2, hb)])
                        H = Hs[oi % 2]
                        P.op("act", lambda e, H=H, pI=pI, ti=ti: e.activation(out=H[:, 0:129], in_=pI[:, 0:129], func=AF.Copy, scale=cols["sicol"][:, ti:ti + 1]),
                             reads=[("ps", 4 + oi % 2, 0), ("ps", 4 + oi % 2, 64), "cols"], writes=[("H", oi % 2)])
                        P.op("dve", lambda e, H=H, pN=pN: e.tensor_tensor(out=H[:, 0:129], in0=H[:, 0:129], in1=pN[:, 0:129], op=ALU.add), reads=[("H", oi % 2), ("ps", 2 + oi % 2)], writes=[("H", oi % 2)])
                        s1 = sc1[oi % 2]
                        P.op("dve", lambda e, H=H, s1=s1: e.tensor_scalar(out=s1[:, 0:1], in0=H[:, 128:129], scalar1=-1.0, scalar2=None, op0=ALU.mult), reads=[("H", oi % 2)], writes=[("sc1", oi % 2)])
                        P.op("dve", lambda e, H=H, s1=s1: e.tensor_tensor(out=s1[:, 0:1], in0=s1[:, 0:1], in1=H[:, 128:129], op=ALU.max), reads=[("H", oi % 2), ("sc1", oi % 2)], writes=[("sc1", oi % 2)])
                        P.op("dve", lambda e, s1=s1, ti=ti: e.tensor_scalar(out=s1[:, 0:1], in0=s1[:, 0:1], scalar1=cols["emcol"][:, ti:ti + 1], scalar2=None, op0=ALU.max),
                             reads=[("sc1", oi % 2), "cols"], writes=[("sc1", oi % 2)])
                        P.op("dve", lambda e, s1=s1: e.reciprocal(out=s1[:, 0:1], in_=s1[:, 0:1]), reads=[("sc1", oi % 2)], writes=[("sc1", oi % 2)])
                        hft = hf[oi % 2]
                        if dr == 0:
                            P.op("dve", lambda e, H=H, s1=s1, hft=hft: e.tensor_scalar(out=hft[:], in0=H[:, 0:128], scalar1=s1[:, 0:1], scalar2=None, op0=ALU.mult), reads=[("H", oi % 2), ("sc1", oi % 2)], writes=[("hf", oi % 2)])
                            P.op("sp", lambda e, hft=hft, ti=ti: e.dma_start(out=hscr[ti * 128:(ti + 1) * 128, :], in_=hft[:]), reads=[("hf", oi % 2)], writes=[("hscr", ti)])
                        else:
                            P.op("sp", lambda e, hft=hft, ti=ti: e.dma_start(out=hft[:], in_=hscr[ti * 128:(ti + 1) * 128, :]), reads=[("hscr", ti)], writes=[("hf", oi % 2)])
                            P.op("dve", lambda e, H=H, s1=s1, hft=hft: e.scalar_tensor_tensor(out=hft[:], in0=H[:, 0:128], scalar=s1[:, 0:1], in1=hft[:], op0=ALU.mult, op1=ALU.add),
                                 reads=[("H", oi % 2), ("sc1", oi % 2), ("hf", oi % 2)], writes=[("hf", oi % 2)])
                            hnt = hn[oi % 2]
                            P.op("act", lambda e, hft=hft, hnt=hnt, s1=s1: e.activation(out=hnt[:], in_=hft[:], func=AF.Square, accum_out=s1[:, 1:2]), reads=[("hf", oi % 2)], writes=[("hn", oi % 2), ("sc1", oi % 2)])
                            P.op("act", lambda e, s1=s1: e.activation(out=s1[:, 1:2], in_=s1[:, 1:2], func=AF.Sqrt, scale=1.0 / 128, bias=epsb[:, 0:1]), reads=[("sc1", oi % 2), "epsb"], writes=[("sc1", oi % 2)])
                            P.op("dve", lambda e, s1=s1: e.reciprocal(out=s1[:, 1:2], in_=s1[:, 1:2]), reads=[("sc1", oi % 2)], writes=[("sc1", oi % 2)])
                            P.op("dve", lambda e, hft=hft, hnt=hnt, s1=s1: e.tensor_scalar(out=hnt[:], in0=hft[:], scalar1=s1[:, 1:2], scalar2=None, op0=ALU.mult), reads=[("hf", oi % 2), ("sc1", oi % 2), ("hn", oi % 2)], writes=[("hn", oi % 2)])
                            P.op("pe", lambda e, hnt=hnt: e.transpose(out=ps[7][:, :128], in_=hnt[:], identity=ident[:]), reads=[("hn", oi % 2)], writes=[("ps", 7)])
                            so = sgo[oi % 2]
                            P.op("sp", lambda e, so=so, t0=t0: e.dma_start(out=so[:], in_=self.pTc[3072 + h * 128:3072 + (h + 1) * 128, t0:t0 + 128]), writes=[("sgo", oi % 2)])
                            P.op("act", lambda e, so=so: e.activation(out=so[:], in_=so[:], func=AF.Sigmoid), reads=[("sgo", oi % 2)], writes=[("sgo", oi % 2)])
                            P.op("dve", lambda e, so=so, t0=t0: e.scalar_tensor_tensor(out=obC[:, t0:t0 + 128], in0=ps[7][:, :128], scalar=mng[:, h:h + 1], in1=so[:], op0=ALU.mult, op1=ALU.mult),
                                 reads=[("ps", 7), ("sgo", oi % 2), "mng"], writes=[("obC", ti)])
                for dr_ in range(2):
                    do_dir(dr_)
                t0o = 0 if not last else c.CTX
                P.op("sp", lambda e, h=h, t0o=t0o: e.dma_start(out=self.oT[2048 + h * 128:2048 + (h + 1) * 128, t0o:], in_=obC[:, t0o:]),
                     reads=[("obC", ti) for ti in range(t0o // 128, ntile)], writes=[("oT", "C", h)])
            for h_ in range(8):
                do_head(h_)
            P.emit()


def host_inputs(cfg, inp, b):
    c = cfg
    f = np.float32

    def fm(v):
        return np.ascontiguousarray(v.reshape(c.KC, 128).T)
    cvec = np.stack([fm(inp["c"][b]), fm(inp["c_ctx"])], axis=-1)
    m = {
        "x": np.ascontiguousarray(inp["x"][b]),
        "ctx": np.ascontiguousarray(inp["ctx"][b]),
        "cvec": np.ascontiguousarray(cvec, dtype=f),
        "w_mod": inp["w_mod"],
        "b_modT": np.ascontiguousarray(inp["b_mod"].reshape(c.L, 9 * c.KC, 128).transpose(0, 2, 1)),
        "norm_gT": np.ascontiguousarray(inp["norm_g"].reshape(c.L, 3, c.KC, 128).transpose(0, 3, 1, 2)),
        "ffn_w1": inp["ffn_w1"], "ffn_w3": inp["ffn_w3"], "ffn_w2": inp["ffn_w2"],
        "ident": np.eye(128, dtype=f),
        "w_in": inp["w_in"], "w_gate": inp["w_gate"], "w_branch": inp["w_branch"], "w_out": inp["w_out"],
        "b_gateT": np.ascontiguousarray(inp["b_gate"].reshape(c.L, 3, c.KC, 128).transpose(0, 3, 1, 2)),
    }
    mc = mixer_consts(c)
    m["cosT"], m["sinT"], m["RmD"] = mc["cosT"], mc["sinT"], mc["Rm"]
    m["MAD"] = np.ascontiguousarray(mc["MA"].transpose(1, 0, 2))
    m["MCD"] = np.ascontiguousarray(mc["MC"].transpose(1, 0, 2))
    m["mnegBD"] = np.ascontiguousarray(mc["mnegB"].transpose(1, 0, 2))
    rp = inp["na_relpos"]
    rb = np.stack([rp[:, :, dr, dc] for (_, dr, dc) in mc["case_list"]], axis=2)
    m["rbD"] = np.ascontiguousarray(rb.transpose(0, 1, 3, 2, 4), dtype=f)
    m["qk_gT"] = np.ascontiguousarray(inp["qk_g"].transpose(0, 2, 1))
    m["mresD"], m["negfD"], m["negbD"] = mc["mres"][:, :512].copy(), mc["negf"][:, :512].copy(), mc["negb"][:, :512].copy()
    m["conv_wT"] = np.ascontiguousarray(inp["mlstm_conv_w"].reshape(c.L, 5, 16, 128).transpose(0, 3, 2, 1))
    m["conv_bT"] = np.ascontiguousarray(inp["mlstm_conv_b"].reshape(c.L, 16, 128).transpose(0, 2, 1))
    m["gate_bB"] = np.ascontiguousarray(np.broadcast_to(inp["mlstm_gate_b"].reshape(c.L, 1, 32), (c.L, 128, 32)))
    m["mngT"] = np.ascontiguousarray(inp["mlstm_norm_g"].reshape(c.L, 8, 128).transpose(0, 2, 1))
    m["sinkB"] = np.ascontiguousarray(np.broadcast_to(inp["attn_sink"][:, None, :], (c.L, 128, 8)))
    return m


def mixer_consts(cfg):
    c = cfg
    f = np.float32
    NT = c.NTOK
    d = np.arange(128)
    axis = d // 64
    half = (d % 64) // 32
    p = d % 32
    inv = (10000.0 ** (-np.arange(32, dtype=np.float32) / 32)).astype(np.float32)
    cosT = np.ones((128, NT), f)
    sinT = np.zeros((128, NT), f)
    t = np.arange(c.SEQ)
    row = (t // GRID_W).astype(np.float32)
    col = (t % GRID_W).astype(np.float32)
    pos = np.where(axis[:, None] == 0, row[None, :], col[None, :]).astype(np.float32)
    ang = pos * inv[p][:, None]
    cosT[:, c.CTX:] = np.cos(ang)
    sinT[:, c.CTX:] = np.sin(ang)
    Rm = np.zeros((128, 128), f)
    for m in range(128):
        if half[m] == 0:
            Rm[m + 32, m] = -1.0
        else:
            Rm[m - 32, m] = 1.0
    NEG = -30000.0
    j = np.arange(128)[:, None]
    i = np.arange(128)[None, :]
    MA = np.stack([np.where(j >= i, 0.0, NEG), np.where(j <= i, 0.0, NEG)], 0).astype(f)
    same = (j // 64) == (i // 64)
    MC = np.stack([np.where(same & (j <= i), 0.0, NEG), np.where(same & (j >= i), 0.0, NEG)], 0).astype(f)
    rows = c.ROWS
    cases = {}
    case_list = []
    sched = []
    for b in range(rows // 2):
        lst = []
        rq = 2 * b + (np.arange(128) // 64)
        cq = np.arange(128) % 64
        r0 = np.clip(rq - 4, 0, rows - 8)
        cs = np.clip(cq - 8, 0, GRID_W - 16)
        for a in range(rows // 2):
            rk = 2 * a + (np.arange(128) // 64)
            ck = np.arange(128) % 64
            ok = (rk[:, None] >= r0[None, :]) & (rk[:, None] < r0[None, :] + 8) & (ck[:, None] >= cs[None, :]) & (ck[:, None] < cs[None, :] + 16)
            if not ok.any():
                continue
            dr = np.clip(rk[:, None] - rq[None, :] + 7, 0, 14)
            dc = np.clip(ck[:, None] - cq[None, :], -15, 15) + 15
            key = (ok.tobytes(), dr.tobytes(), dc.tobytes())
            if key not in cases:
                cases[key] = len(case_list)
                case_list.append((ok, dr, dc))
            lst.append((a, cases[key]))
        sched.append(lst)
    mnegB = np.stack([np.where(ok, 0.0, NEG) for ok, _, _ in case_list], 0).astype(f)
    tt = np.arange(NT)
    mres = np.where(tt % 64 == 0, 0.0, 1.0).astype(f)
    negf = np.where(tt % 64 == 0, -1e30, 0.0).astype(f)
    negb = np.where(tt % 64 == 63, -1e30, 0.0).astype(f)
    return dict(cosT=cosT, sinT=sinT, Rm=Rm, MA=MA, MC=MC, mnegB=mnegB, case_list=case_list, sched=sched,
                mres=np.broadcast_to(mres, (128, NT)).copy(), negf=np.broadcast_to(negf, (128, NT)).copy(), negb=np.broadcast_to(negb, (128, NT)).copy())


def kernel(**inp):
    from concourse.bass_utils import run_bass_kernel_spmd
    inp = {k: np.asarray(v) for k, v in inp.items()}
    cfg = FULL
    kb = K(cfg)
    nc = kb.build()
    maps = [host_inputs(cfg, inp, b) for b in range(2)]
    res = run_bass_kernel_spmd(nc, maps, core_ids=[0, 1])
    return np.stack([r["out"] for r in res.results]).astype(np.float32)
```
